# Optimizing a Trainium2 kernel written in Bass

```python
import math
import jax, jax.numpy as jnp
from jax import lax
import numpy as np

D_MODEL = 1024
BATCH = 8
SEQ = 2048
DEPTH = 1

GDN_HEADS = 4
GDN_DK = 128
GDN_DV = 128
GDN_CONV = 4
GDN_CHUNK = 64
NSA_HEADS = 8
NSA_KV_HEADS = 2
NSA_DH = 64
CMP_BLOCK = 32
CMP_STRIDE = 16
CMP_HIDDEN = 256
SEL_BLOCK = 64
SEL_TOPK = 16
SEL_Q_BLOCK = 64
WINDOW = 512
WIN_Q_BLOCK = 128
FORCE_SCORE = 1e9
D_FF = 2816
FFN_CONV = 3
DN_ALPHA = (2 * DEPTH) ** 0.25
DN_BETA = (8 * DEPTH) ** -0.25
LN_EPS = 1e-5
RMS_EPS = 1e-6

GDN_QK = GDN_HEADS * GDN_DK
GDN_VW = GDN_HEADS * GDN_DV
NSA_QW = NSA_HEADS * NSA_DH
NSA_KVW = NSA_KV_HEADS * NSA_DH
IN_SIZES = (GDN_QK, GDN_QK, GDN_VW, GDN_VW, GDN_HEADS, GDN_HEADS,
            NSA_QW, NSA_KVW, NSA_KVW, NSA_KVW, NSA_KVW, NSA_KVW, NSA_KVW, 3 * NSA_HEADS,
            2 * D_MODEL)
D_IN = sum(IN_SIZES)
IN_SPLIT = [int(v) for v in np.cumsum(IN_SIZES)[:-1]]

kernel_name = "hybrid_gdn_nsa_convffn_deepnorm"


def layer_norm(x, g, b):
    xf = x.astype(jnp.float32)
    mu = jnp.mean(xf, -1, keepdims=True)
    var = jnp.mean(jnp.square(xf - mu), -1, keepdims=True)
    return ((xf - mu) * lax.rsqrt(var + LN_EPS)).astype(x.dtype) * g + b


def causal_dwconv(x, w):
    k = w.shape[0]
    return lax.conv_general_dilated(x, w[:, None, :], window_strides=(1,), padding=[(k - 1, 0)],
                                    dimension_numbers=('NWC', 'WIO', 'NWC'),
                                    feature_group_count=x.shape[-1])


def masked_softmax(s, mask):
    s = jnp.where(mask, s.astype(jnp.float32), -jnp.inf)
    m = jnp.max(s, axis=-1, keepdims=True)
    p = jnp.exp(s - jnp.where(jnp.isfinite(m), m, 0.0))
    d = jnp.sum(p, axis=-1, keepdims=True)
    return p / jnp.where(d > 0, d, 1.0)


def l2norm(t):
    return t * lax.rsqrt(jnp.sum(t * t, -1, keepdims=True) + RMS_EPS)


def chunk_gated_delta(q, k, v, g, beta):
    b, s, h, dk = q.shape
    dv = v.shape[-1]
    c = GDN_CHUNK
    n = s // c

    def to_chunks(t):
        return t.reshape(b, n, c, h, -1).transpose(0, 3, 1, 2, 4)

    q, k, v = to_chunks(q), to_chunks(k), to_chunks(v)
    g = g.reshape(b, n, c, h).transpose(0, 3, 1, 2)
    beta = beta.reshape(b, n, c, h).transpose(0, 3, 1, 2)
    gc = jnp.cumsum(g, axis=-1)
    tril = jnp.tril(jnp.ones((c, c), bool))
    strict = jnp.tril(jnp.ones((c, c), bool), -1)
    diff = gc[..., :, None] - gc[..., None, :]
    decay = jnp.where(tril, jnp.exp(jnp.where(tril, diff, 0.0)), 0.0)
    kk = jnp.einsum('bhnid,bhnjd->bhnij', k, k)
    a_mat = jnp.where(strict, kk * decay * beta[..., :, None], 0.0)
    lhs = jnp.eye(c, dtype=jnp.float32) + a_mat
    rhs = jnp.concatenate([v * beta[..., None], k * (beta * jnp.exp(gc))[..., None]], -1)
    sol = lax.linalg.triangular_solve(lhs, rhs, left_side=True, lower=True)
    u0, kcd = sol[..., :dv], sol[..., dv:]
    qk = jnp.einsum('bhnid,bhnjd->bhnij', q, k) * decay
    q_dec = q * jnp.exp(gc)[..., None]
    k_dec = k * jnp.exp(gc[..., -1:] - gc)[..., None]
    g_last = jnp.exp(gc[..., -1])

    def lead(t):
        return jnp.moveaxis(t, 2, 0)

    xs = (lead(u0), lead(kcd), lead(qk), lead(q_dec), lead(k_dec), jnp.moveaxis(g_last, 2, 0))

    def step(state, xc):
        u0_c, kcd_c, qk_c, qd_c, kd_c, gl_c = xc
        u = u0_c - jnp.einsum('bhck,bhkv->bhcv', kcd_c, state)
        o = jnp.einsum('bhck,bhkv->bhcv', qd_c, state) + jnp.einsum('bhij,bhjv->bhiv', qk_c, u)
        state = state * gl_c[..., None, None] + jnp.einsum('bhck,bhcv->bhkv', kd_c, u)
        return state, o

    s0 = jnp.zeros((b, h, dk, dv), jnp.float32)
    _, o = lax.scan(step, s0, xs)
    return o.transpose(1, 0, 3, 2, 4).reshape(b, s, h, dv)


def gated_deltanet(q, k, v, z, bb, aa, conv_w, a_log, dt_bias, norm_w):
    bsz, s, _ = q.shape
    dt = q.dtype
    qkv = jax.nn.silu(causal_dwconv(jnp.concatenate([q, k, v], -1), conv_w))
    q, k, v = jnp.split(qkv.astype(jnp.float32), [GDN_QK, 2 * GDN_QK], -1)
    q = l2norm(q.reshape(bsz, s, GDN_HEADS, GDN_DK)) * (GDN_DK ** -0.5)
    k = l2norm(k.reshape(bsz, s, GDN_HEADS, GDN_DK))
    v = v.reshape(bsz, s, GDN_HEADS, GDN_DV)
    beta = jax.nn.sigmoid(bb.astype(jnp.float32))
    g = -jnp.exp(a_log.astype(jnp.float32)) * jax.nn.softplus(aa.astype(jnp.float32) + dt_bias.astype(jnp.float32))
    o = chunk_gated_delta(q, k, v, g, beta)
    o = o * lax.rsqrt(jnp.mean(o * o, -1, keepdims=True) + RMS_EPS) * norm_w.astype(jnp.float32)
    o = o * jax.nn.silu(z.astype(jnp.float32).reshape(bsz, s, GDN_HEADS, GDN_DV))
    return o.reshape(bsz, s, GDN_VW).astype(dt)


def native_sparse_attention(q, k_c, v_c, k_s, v_s, k_w, v_w, gate,
                            pos_k, w1_k, w2_k, pos_v, w1_v, w2_v):
    bsz, s, _ = q.shape
    gkv, dh = NSA_KV_HEADS, NSA_DH
    rep = NSA_HEADS // NSA_KV_HEADS
    dt = q.dtype
    q = q.reshape(bsz, s, gkv, rep, dh) * (dh ** -0.5)
    k_c, v_c, k_s, v_s, k_w, v_w = [t.reshape(bsz, s, gkv, dh) for t in (k_c, v_c, k_s, v_s, k_w, v_w)]
    pos = jnp.arange(s)

    n_cmp = (s - CMP_BLOCK) // CMP_STRIDE + 1
    cidx = np.arange(n_cmp)[:, None] * CMP_STRIDE + np.arange(CMP_BLOCK)[None]

    def compress(t, pe, w1, w2):
        blk = t[:, cidx] + pe[:, None, :]
        blk = blk.transpose(0, 1, 3, 2, 4).reshape(bsz, n_cmp, gkv, CMP_BLOCK * dh)
        return jax.nn.gelu(blk @ w1) @ w2

    kc = compress(k_c, pos_k, w1_k, w2_k)
    vc = compress(v_c, pos_v, w1_v, w2_v)
    cmp_end = jnp.asarray(cidx[:, -1])
    s_cmp = jnp.einsum('bsgrd,bngd->bgrsn', q, kc)
    p_cmp = masked_softmax(s_cmp, cmp_end[None, :] <= pos[:, None])
    o_cmp = jnp.einsum('bgrsn,bngd->bsgrd', p_cmp.astype(dt), vc)

    n_sel_blk = s // SEL_BLOCK
    starts = np.arange(n_cmp) * CMP_STRIDE
    jb = np.arange(n_sel_blk) * SEL_BLOCK
    overlap = ((starts[:, None] < jb[None] + SEL_BLOCK) & (starts[:, None] + CMP_BLOCK > jb[None])).astype(np.float32)
    imp = jnp.einsum('bgrsn,nj->bgsj', p_cmp, jnp.asarray(overlap))
    jj = jnp.arange(n_sel_blk)
    cur = pos // SEL_BLOCK
    forced = (jj[None] == 0) | (jj[None] == cur[:, None]) | (jj[None] == cur[:, None] - 1)
    causal_blk = (jj[None] * SEL_BLOCK) <= pos[:, None]
    imp = jnp.where(forced, FORCE_SCORE, jnp.where(causal_blk, imp, -jnp.inf))
    n_top = min(SEL_TOPK, n_sel_blk)
    _, sel = lax.top_k(imp, n_top)

    ks_blk = k_s.reshape(bsz, n_sel_blk, SEL_BLOCK, gkv, dh).transpose(0, 3, 1, 2, 4)
    vs_blk = v_s.reshape(bsz, n_sel_blk, SEL_BLOCK, gkv, dh).transpose(0, 3, 1, 2, 4)
    nq = s // SEL_Q_BLOCK
    q_ch = q.reshape(bsz, nq, SEL_Q_BLOCK, gkv, rep, dh).transpose(1, 0, 2, 3, 4, 5)
    sel_ch = sel.reshape(bsz, gkv, nq, SEL_Q_BLOCK, n_top).transpose(2, 0, 1, 3, 4)
    pos_ch = pos.reshape(nq, SEL_Q_BLOCK)
    bi = jnp.arange(bsz)[:, None, None, None]
    gi = jnp.arange(gkv)[None, :, None, None]
    offs = jnp.arange(SEL_BLOCK)

    def sel_block(args):
        qc, sc, pc = args
        kg = ks_blk[bi, gi, sc]
        vg = vs_blk[bi, gi, sc]
        sc_ = jnp.einsum('bqgrd,bgqnld->bgrqnl', qc, kg)
        kpos = sc[..., None] * SEL_BLOCK + offs
        mask = (kpos <= pc[None, None, :, None, None]).reshape(bsz, gkv, 1, SEL_Q_BLOCK, n_top * SEL_BLOCK)
        p = masked_softmax(sc_.reshape(bsz, gkv, rep, SEL_Q_BLOCK, n_top * SEL_BLOCK), mask)
        p = p.reshape(bsz, gkv, rep, SEL_Q_BLOCK, n_top, SEL_BLOCK).astype(dt)
        return jnp.einsum('bgrqnl,bgqnld->bqgrd', p, vg)

    o_slc = lax.map(sel_block, (q_ch, sel_ch, pos_ch))
    o_slc = o_slc.transpose(1, 0, 2, 3, 4, 5).reshape(bsz, s, gkv, rep, dh)

    nb = s // WIN_Q_BLOCK
    span = WINDOW + WIN_Q_BLOCK
    widx = np.arange(nb)[:, None] * WIN_Q_BLOCK + np.arange(span)[None]
    kwb = jnp.pad(k_w, ((0, 0), (WINDOW, 0), (0, 0), (0, 0)))[:, widx]
    vwb = jnp.pad(v_w, ((0, 0), (WINDOW, 0), (0, 0), (0, 0)))[:, widx]
    qb = q.reshape(bsz, nb, WIN_Q_BLOCK, gkv, rep, dh)
    s_win = jnp.einsum('bcqgrd,bckgd->bcgrqk', qb, kwb)
    qpos = pos.reshape(nb, WIN_Q_BLOCK)[:, :, None]
    kpos = jnp.asarray(widx - WINDOW)[:, None, :]
    wmask = (kpos <= qpos) & (kpos > qpos - WINDOW) & (kpos >= 0)
    p_win = masked_softmax(s_win, wmask[None, :, None, None]).astype(dt)
    o_win = jnp.einsum('bcgrqk,bckgd->bcqgrd', p_win, vwb).reshape(bsz, s, gkv, rep, dh)

    gt = jax.nn.sigmoid(gate).reshape(bsz, s, gkv, rep, 3)
    o = gt[..., 0:1] * o_cmp + gt[..., 1:2] * o_slc + gt[..., 2:3] * o_win
    return o.reshape(bsz, s, NSA_QW)


def setup_inputs(seed: int = 0) -> dict:
    key = jax.random.key(seed)
    ks = jax.random.split(key, 24)
    f32 = jnp.float32
    L = DEPTH

    def nrm(k, shape, scale):
        return jax.random.normal(k, shape, f32) * scale

    x = jax.random.normal(ks[0], (BATCH, SEQ, D_MODEL), f32)
    w_in = nrm(ks[1], (L, D_MODEL, D_IN), D_MODEL ** -0.5)
    gdn_conv_w = nrm(ks[2], (L, GDN_CONV, 2 * GDN_QK + GDN_VW), GDN_CONV ** -0.5)
    gdn_a_log = jnp.log(jax.random.uniform(ks[3], (L, GDN_HEADS), f32, 1.0, 16.0))
    dtv = jnp.exp(jax.random.uniform(ks[4], (L, GDN_HEADS), f32, math.log(1e-3), math.log(1e-1)))
    gdn_dt_bias = dtv + jnp.log(-jnp.expm1(-dtv))
    gdn_norm_w = 1.0 + nrm(ks[5], (L, GDN_DV), 0.02)
    cmp_pos_k = nrm(ks[6], (L, CMP_BLOCK, NSA_DH), 0.02)
    cmp_w1_k = nrm(ks[7], (L, CMP_BLOCK * NSA_DH, CMP_HIDDEN), (CMP_BLOCK * NSA_DH) ** -0.5)
    cmp_w2_k = nrm(ks[8], (L, CMP_HIDDEN, NSA_DH), CMP_HIDDEN ** -0.5)
    cmp_pos_v = nrm(ks[9], (L, CMP_BLOCK, NSA_DH), 0.02)
    cmp_w1_v = nrm(ks[10], (L, CMP_BLOCK * NSA_DH, CMP_HIDDEN), (CMP_BLOCK * NSA_DH) ** -0.5)
    cmp_w2_v = nrm(ks[11], (L, CMP_HIDDEN, NSA_DH), CMP_HIDDEN ** -0.5)
    w_branch_gdn = nrm(ks[12], (L, GDN_VW, D_MODEL), GDN_VW ** -0.5)
    w_branch_nsa = nrm(ks[13], (L, NSA_QW, D_MODEL), NSA_QW ** -0.5)
    w_out = nrm(ks[14], (L, D_MODEL, D_MODEL), D_MODEL ** -0.5 * DN_BETA)
    ln1_g = 1.0 + nrm(ks[15], (L, D_MODEL), 0.02)
    ln1_b = nrm(ks[16], (L, D_MODEL), 0.02)
    w_up = nrm(ks[17], (L, D_MODEL, 2 * D_FF), D_MODEL ** -0.5)
    ffn_conv_w = nrm(ks[18], (L, FFN_CONV, 2 * D_FF), FFN_CONV ** -0.5)
    w_down = nrm(ks[19], (L, D_FF, D_MODEL), D_FF ** -0.5 * DN_BETA)
    ln2_g = 1.0 + nrm(ks[20], (L, D_MODEL), 0.02)
    ln2_b = nrm(ks[21], (L, D_MODEL), 0.02)
    return {"x": x, "w_in": w_in, "gdn_conv_w": gdn_conv_w, "gdn_a_log": gdn_a_log,
            "gdn_dt_bias": gdn_dt_bias, "gdn_norm_w": gdn_norm_w,
            "cmp_pos_k": cmp_pos_k, "cmp_w1_k": cmp_w1_k, "cmp_w2_k": cmp_w2_k,
            "cmp_pos_v": cmp_pos_v, "cmp_w1_v": cmp_w1_v, "cmp_w2_v": cmp_w2_v,
            "w_branch_gdn": w_branch_gdn, "w_branch_nsa": w_branch_nsa, "w_out": w_out,
            "ln1_g": ln1_g, "ln1_b": ln1_b, "w_up": w_up, "ffn_conv_w": ffn_conv_w,
            "w_down": w_down, "ln2_g": ln2_g, "ln2_b": ln2_b}


def reference(x, w_in, gdn_conv_w, gdn_a_log, gdn_dt_bias, gdn_norm_w,
              cmp_pos_k, cmp_w1_k, cmp_w2_k, cmp_pos_v, cmp_w1_v, cmp_w2_v,
              w_branch_gdn, w_branch_nsa, w_out, ln1_g, ln1_b,
              w_up, ffn_conv_w, w_down, ln2_g, ln2_b):
    for i in range(DEPTH):
        proj = x @ w_in[i]
        (g_q, g_k, g_v, g_z, g_b, g_a, n_q, n_kc, n_vc, n_ks, n_vs, n_kw, n_vw, n_gate,
         mix_gate) = jnp.split(proj, IN_SPLIT, axis=-1)
        o_a = gated_deltanet(g_q, g_k, g_v, g_z, g_b, g_a, gdn_conv_w[i], gdn_a_log[i],
                             gdn_dt_bias[i], gdn_norm_w[i])
        o_b = native_sparse_attention(n_q, n_kc, n_vc, n_ks, n_vs, n_kw, n_vw, n_gate,
                                      cmp_pos_k[i], cmp_w1_k[i], cmp_w2_k[i],
                                      cmp_pos_v[i], cmp_w1_v[i], cmp_w2_v[i])
        gate_a, gate_b = jnp.split(mix_gate, 2, axis=-1)
        mixed = (jax.nn.sigmoid(gate_a) * (o_a @ w_branch_gdn[i])
                 + jax.nn.sigmoid(gate_b) * (o_b @ w_branch_nsa[i]))
        h = layer_norm(DN_ALPHA * x + mixed @ w_out[i], ln1_g[i], ln1_b[i])
        u = causal_dwconv(h @ w_up[i], ffn_conv_w[i])
        u_g, u_v = jnp.split(u, 2, axis=-1)
        f = (jax.nn.silu(u_g) * u_v) @ w_down[i]
        x = layer_norm(DN_ALPHA * h + f, ln2_g[i], ln2_b[i])
    return x
```

```python
import math
from contextlib import ExitStack
import numpy as np
import concourse.bass as bass
import concourse.mybir as mybir
from concourse.bass_utils import run_bass_kernel_spmd

F32 = mybir.dt.float32
BF16 = mybir.dt.bfloat16
AF = mybir.ActivationFunctionType
ALU = mybir.AluOpType
AX = mybir.AxisListType

S = 2048
D = 1024
NT = 16
D_IN = 5408
D_FF = 2816
DN_ALPHA = 2.0 ** 0.25
LN_EPS = 1e-5
RMS_EPS = 1e-6
NEG = -30000.0
import os as _os
FFN_RATIO = int(_os.environ.get("FFN_RATIO", "5"))
PE_FIX = float(_os.environ.get("PE_FIX", "45"))
PE_COL = float(_os.environ.get("PE_COL", "0.45"))
FFN_TAP_ACT = int(_os.environ.get("FFN_TAP_ACT", "1"))

C_GQ, C_GK, C_GV, C_GZ, C_GB, C_GA = 0, 512, 1024, 1536, 2048, 2052
C_NQ, C_NKC, C_NVC, C_NKS, C_NVS, C_NKW, C_NVW, C_NG, C_MG = 2056, 2568, 2696, 2824, 2952, 3080, 3208, 3336, 3360

EPOCH = 6000


class Timeline:
    def __init__(self, prog, name, step):
        self.prog = prog
        self.name = name
        self.step = step
        self.count = 0
        self.sems = []

    def sem_for(self, idx):
        ep = (idx - 1) // EPOCH
        while len(self.sems) <= ep:
            self.sems.append(self.prog.new_sem(f"{self.name}_{len(self.sems)}"))
        return self.sems[ep], ((idx - 1) % EPOCH + 1) * self.step

    def next(self):
        self.count += 1
        return self.count


class Buf:
    __slots__ = ("name", "last_write", "reads", "excl", "also", "persist")

    def __init__(self, name, excl=False, persist=False):
        self.name = name
        self.persist = persist
        self.last_write = None
        self.reads = []
        self.excl = excl
        self.also = None


class Op:
    __slots__ = ("idx", "eng", "fn", "tl", "deps", "epoch", "dur", "busy", "pos", "fin", "sched", "final", "bar", "prio")

    def __init__(self, idx, eng, fn, tl, deps, epoch, dur, busy, bar=True):
        self.idx, self.eng, self.fn, self.tl, self.deps = idx, eng, fn, tl, deps
        self.epoch, self.dur, self.busy = epoch, dur, busy
        self.bar = bar
        self.prio = 0
        self.pos = None
        self.fin = None
        self.sched = False
        self.final = False


def _free(ap):
    n = 1
    for d in list(ap.shape)[1:]:
        n *= int(d)
    return n


class Prog:
    ENGS = ("pe", "act", "dve", "pool", "sp")
    WINDOW = int(_os.environ.get("SCHED_WINDOW", "128"))
    SEM_LAT = float(_os.environ.get("SCHED_SEMLAT", "500"))

    def __init__(self, nc):
        self.nc = nc
        self.stack = ExitStack()
        self.tl = {e: Timeline(self, "c_" + e, 1) for e in self.ENGS}
        self.ops = []
        self.epoch = 0
        self.dma_pool = {}
        self.dma_rr = {}
        self.dma_last = {}
        self.all_dma_tl = []
        self.same_engine_sync = bool(int(_os.environ.get("SAME_ENG_SYNC", "1")))
        self.finals = []
        self.cur_prio = 0

    def new_sem(self, name):
        return self.stack.enter_context(self.nc.semaphore(name))

    def new_dma_tl(self, name):
        t = Timeline(self, "d_" + name, 16)
        self.all_dma_tl.append(t)
        return t

    def _deps(self, reads, writes):
        deps = set()
        for b in reads:
            if b.last_write is not None:
                deps.add(b.last_write)
            if b.excl:
                deps.update(b.reads)
            if b.also:
                deps.update(b.also)
        for b in writes:
            if b.last_write is not None:
                deps.add(b.last_write)
            deps.update(b.reads)
        return deps

    def _mark(self, op, reads, writes):
        for b in reads:
            if b.excl:
                b.last_write = op
                b.reads = []
            else:
                b.reads.append(op)
        for b in writes:
            b.last_write = op
            b.reads = []

    def op(self, eng, fn, reads=(), writes=(), cost=150.0):
        deps = self._deps(reads, writes)
        bar = not all(b.persist for b in list(reads) + list(writes))
        o = Op(len(self.ops), eng, fn, self.tl[eng], deps, self.epoch, cost, cost, bar)
        o.prio = self.cur_prio
        self.ops.append(o)
        self._mark(o, reads, writes)
        return o

    def dma(self, eng, out, in_, reads=(), writes=(), tl=None, **kw):
        if tl is None:
            if eng not in self.dma_pool:
                self.dma_pool[eng] = [self.new_dma_tl(f"{eng}{i}") for i in range(6)]
            pool = self.dma_pool[eng]
            i = self.dma_rr.get(eng, 0)
            self.dma_rr[eng] = (i + 1) % len(pool)
            tl = pool[i]
        deps = self._deps(reads, writes)
        if tl in self.dma_last:
            deps.add(self.dma_last[tl])
        nbytes = _free(out) * int(out.shape[0]) * 4
        dur = 2200.0 + nbytes / 150.0

        def fn(e, out=out, in_=in_, kw=kw):
            return e.dma_start(out=out, in_=in_, **kw)

        bar = not all(b.persist for b in list(reads) + list(writes))
        o = Op(len(self.ops), eng, fn, tl, deps, self.epoch, dur, 150.0 if eng == "sp" else 400.0, bar)
        self.ops.append(o)
        self.dma_last[tl] = o
        self._mark(o, reads, writes)
        return o

    def barrier(self):
        self.epoch += 1

    def final_wait(self, eng, bufs):
        deps = self._deps(bufs, bufs)
        self.finals.append((eng, deps))

    def schedule(self):
        pend = {e: [] for e in self.ENGS}
        for o in self.ops:
            pend[o.eng].append(o)
        head = {e: 0 for e in self.ENGS}
        tfree = {e: 0.0 for e in self.ENGS}
        order = {e: [] for e in self.ENGS}
        n_left = len(self.ops)
        ep_left = {}
        for o in self.ops:
            ep_left[o.epoch] = ep_left.get(o.epoch, 0) + 1
        ep_end = {-1: 0.0}
        cur_ep = 0
        ep_fin = 0.0
        while n_left:
            while ep_left.get(cur_ep, 0) == 0:
                ep_end[cur_ep] = ep_fin
                cur_ep += 1
            best = None
            for e in self.ENGS:
                lst = pend[e]
                h = head[e]
                while h < len(lst) and lst[h].sched:
                    h += 1
                head[e] = h
                cnt = 0
                i = h
                while i < len(lst) and cnt < self.WINDOW:
                    o = lst[i]
                    i += 1
                    if o.sched:
                        continue
                    cnt += 1
                    if o.epoch != cur_ep:
                        if o.bar or o.epoch < cur_ep:
                            continue
                        rdy = 0.0
                    else:
                        rdy = ep_end[cur_ep - 1] if o.bar else 0.0
                    ok = True
                    for d in o.deps:
                        if not d.sched:
                            ok = False
                            break
                        f = d.fin + (0.0 if d.eng == e and d.tl is self.tl[e] else self.SEM_LAT)
                        if f > rdy:
                            rdy = f
                    if not ok:
                        continue
                    st = rdy if rdy > tfree[e] else tfree[e]
                    key = (st, -o.prio, o.idx)
                    if best is None or key < best[0]:
                        best = (key, e, o, st)
            if best is None:
                raise RuntimeError("scheduler: no candidate")
            _, e, o, st = best
            o.sched = True
            o.fin = st + o.dur
            tfree[e] = st + o.busy
            order[e].append(o)
            n_left -= 1
            ep_left[o.epoch] -= 1
            if o.fin > ep_fin:
                ep_fin = o.fin
        self.est_ns = ep_fin
        return order

    def emit(self):
        nc = self.nc
        order = self.schedule()
        for e in self.ENGS:
            for o in order[e]:
                o.pos = o.tl.next()
        streams = {}
        for e in self.ENGS:
            known = {}
            out = []
            mytl = self.tl[e]
            last_ep = 0
            done_pos = {}
            for o in order[e]:
                waits = []
                if o.bar and o.epoch > last_ep:
                    for tl, p in self._epoch_max(o.epoch).items():
                        if tl is mytl:
                            continue
                        if known.get(tl, 0) < p:
                            known[tl] = p
                            waits.append(tl.sem_for(p))
                    last_ep = o.epoch
                for d in o.deps:
                    if d.tl is mytl and (e in ("pe", "sp") or not self.same_engine_sync):
                        continue
                    if known.get(d.tl, 0) >= d.pos:
                        continue
                    known[d.tl] = d.pos
                    waits.append(d.tl.sem_for(d.pos))
                sem, _ = o.tl.sem_for(o.pos)
                out.append((waits, o.fn, sem, o.tl.step))
            streams[e] = out
        for (e, deps) in self.finals:
            waits = []
            best = {}
            for d in deps:
                if best.get(d.tl, 0) < d.pos:
                    best[d.tl] = d.pos
            for tl, p in best.items():
                waits.append(tl.sem_for(p))
            streams[e].append((waits, None, None, 0))
        self.streams = streams

        def replay(name):
            def body(e):
                for (waits, fn, sem, inc) in streams[name]:
                    for (s_, v) in waits:
                        e.wait_ge(s_, v)
                    if fn is not None:
                        fn(e).then_inc(sem, inc)
            return body

        with nc.Block() as block:
            block.tensor(replay("pe"))
            block.scalar(replay("act"))
            block.vector(replay("dve"))
            block.gpsimd(replay("pool"))
            block.sync(replay("sp"))

    def _epoch_max(self, epoch):
        if not hasattr(self, "_epmax"):
            self._epmax = {}
        if epoch not in self._epmax:
            m = {}
            for o in self.ops:
                if o.epoch < epoch and m.get(o.tl, 0) < o.pos:
                    m[o.tl] = o.pos
            self._epmax[epoch] = m
        return self._epmax[epoch]


def host_consts():
    c = {}
    i = np.arange(128)
    same = (i[:, None] // 64) == (i[None, :] // 64)
    c["c_ident"] = np.eye(128, dtype=np.float32)
    c["c_tri"] = ((i[:, None] <= i[None, :]) & same).astype(np.float32)
    c["c_blk"] = same.astype(np.float32)
    c["c_nega"] = np.where((i[:, None] > i[None, :]) & same, 0.0, NEG).astype(np.float32)
    c["c_negq"] = np.where((i[None, :] >= i[:, None]) & same, 0.0, NEG).astype(np.float32)
    sel = np.zeros((4, 4, 128), np.float32)
    for h in range(4):
        sel[h, h, :] = 1.0
    c["c_sel4"] = sel
    sel12 = np.zeros((12, 4, 128), np.float32)
    for h in range(4):
        sel12[3 * h:3 * h + 3, h, :] = 1.0
    c["c_sel12"] = sel12
    sr = np.zeros((128, 2, 128), np.float32)
    sr[0, 0, :] = 1.0
    sr[64, 1, :] = 1.0
    c["c_selrow"] = sr
    n = np.arange(127)
    t = np.arange(S)
    c["c_cmpmask"] = np.concatenate([((n[:, None] * 16 + 31) <= t[None, :]).astype(np.float32),
                                     np.zeros((1, S), np.float32)], 0)
    starts = n * 16
    jb = np.arange(32) * 64
    ov = ((starts[:, None] < jb[None] + 64) & (starts[:, None] + 32 > jb[None])).astype(np.float32)
    c["c_overlap"] = np.concatenate([ov, np.zeros((1, 32), np.float32)], 0)
    c["c_causal"] = (i[:, None] <= i[None, :]).astype(np.float32)
    c["c_anti"] = (i[:, None] > i[None, :]).astype(np.float32)
    E = np.zeros((32, 16, 128), np.float32)
    for kt in range(16):
        for k in range(128):
            E[2 * kt + k // 64, kt, k] = 1.0
    c["c_expand"] = E
    fb = np.zeros((128, 8, 32), np.float32)
    for qi in range(8):
        qt = 8 + qi
        pos = qt * 128 + i
        cur = pos // 64
        jj = np.arange(32)
        causal = (jj[None] * 64) <= pos[:, None]
        b_ = np.where(causal, 0.0, -1e30)
        b_ = np.where(jj[None] == 0, 1e9, b_)
        b_ = np.where(jj[None] == cur[:, None], 2e9, b_)
        b_ = np.where(jj[None] == cur[:, None] - 1, 3e9, b_)
        fb[:, qi, :] = b_
    c["c_forced"] = fb
    return c


CONST_SHAPES = {k: v.shape for k, v in host_consts().items()}


def host_params(inp):
    p = {}
    f = np.float32
    p["p_gconv"] = np.ascontiguousarray(inp["gdn_conv_w"][0].reshape(4, 12, 128).transpose(2, 1, 0)).astype(f)
    p["p_alog"] = np.ascontiguousarray(np.broadcast_to(inp["gdn_a_log"][0][None, None, :], (128, NT, 4))).astype(f)
    p["p_dtb"] = np.ascontiguousarray(np.broadcast_to(inp["gdn_dt_bias"][0][None, None, :], (128, NT, 4))).astype(f)
    p["p_gnw"] = np.ascontiguousarray(np.broadcast_to(inp["gdn_norm_w"][0][None, None, :], (128, 4, 128))).astype(f)
    p["p_poskT"] = np.ascontiguousarray(np.concatenate([inp["cmp_pos_k"][0].T] * 2, 0)).astype(f)
    p["p_posvT"] = np.ascontiguousarray(np.concatenate([inp["cmp_pos_v"][0].T] * 2, 0)).astype(f)
    p["p_ln1g"] = np.ascontiguousarray(np.broadcast_to(inp["ln1_g"][0][None, :], (128, D))).astype(f)
    p["p_ln1b"] = np.ascontiguousarray(np.broadcast_to(inp["ln1_b"][0][None, :], (128, D))).astype(f)
    p["p_ln2g"] = np.ascontiguousarray(np.broadcast_to(inp["ln2_g"][0][None, :], (128, D))).astype(f)
    p["p_ln2b"] = np.ascontiguousarray(np.broadcast_to(inp["ln2_b"][0][None, :], (128, D))).astype(f)
    p["p_fconv"] = np.ascontiguousarray(inp["ffn_conv_w"][0].reshape(3, 44, 128).transpose(2, 1, 0)).astype(f)
    return p


def _pk(w, cols):
    sub = w[:, cols]
    return np.ascontiguousarray(sub.reshape(8, 128, sub.shape[1]).transpose(1, 0, 2))


def host_packed(inp):
    f = np.float32
    w_in = np.asarray(inp["w_in"][0], dtype=f)
    w_up = np.asarray(inp["w_up"][0], dtype=f)
    p = {}
    ar = np.arange
    p["pk_up"] = np.stack([_pk(w_up, np.concatenate([ar(i * 128, (i + 1) * 128), ar(D_FF + i * 128, D_FF + (i + 1) * 128)]))
                           for i in range(22)], 0)
    p["pk_mg"] = np.stack([_pk(w_in, np.concatenate([ar(C_MG + j * 128, C_MG + (j + 1) * 128),
                                                     ar(C_MG + 1024 + j * 128, C_MG + 1024 + (j + 1) * 128)]))
                           for j in range(8)], 0)
    p["pk_g"] = np.stack([_pk(w_in, ar(ck * 128, (ck + 1) * 128)) for ck in range(12)], 0)
    jobs = [ar(C_NQ + i * 128, C_NQ + (i + 1) * 128) for i in range(4)]
    jobs += [ar(C_NKS, C_NKS + 128), ar(C_NKW, C_NKW + 128), ar(C_NKC, C_NKC + 128), ar(C_NVC, C_NVC + 128)]
    p["pk_n128"] = np.stack([_pk(w_in, c) for c in jobs], 0)
    return p


PACKED_SHAPES = {"pk_up": (22, 128, 8, 256), "pk_mg": (8, 128, 8, 256), "pk_g": (12, 128, 8, 128),
                 "pk_n128": (8, 128, 8, 128)}

PARAM_SHAPES = {"p_gconv": (128, 12, 4), "p_alog": (128, NT, 4), "p_dtb": (128, NT, 4), "p_gnw": (128, 4, 128),
                "p_poskT": (128, 32), "p_posvT": (128, 32), "p_ln1g": (128, D), "p_ln1b": (128, D),
                "p_ln2g": (128, D), "p_ln2b": (128, D), "p_fconv": (128, 44, 3)}

WEIGHT_SHAPES = {"w_in": (D, D_IN), "cmp_w1_k": (2048, 256), "cmp_w2_k": (256, 64), "cmp_w1_v": (2048, 256),
                 "cmp_w2_v": (256, 64), "w_branch_gdn": (512, D), "w_branch_nsa": (512, D), "w_out": (D, D),
                 "w_up": (D, 2 * D_FF), "w_down": (D_FF, D)}


class Ctx:
    pass


def build(taps=(), phases=("gdn", "nsa", "mix", "ffn")):
    nc = bass.Bass("TRN2", target_bir_lowering=False)
    dr = {}
    dr["x"] = nc.dram_tensor("x", [S, D], F32, kind="ExternalInput").ap()
    for k, shp in list(WEIGHT_SHAPES.items()) + list(CONST_SHAPES.items()) + list(PARAM_SHAPES.items()) + list(PACKED_SHAPES.items()):
        dr[k] = nc.dram_tensor(k, list(shp), F32, kind="ExternalInput").ap()
    out_d = nc.dram_tensor("out", [S, D], F32, kind="ExternalOutput").ap()
    hscr = nc.dram_tensor("hscr", [S, D], F32).ap()
    P = Prog(nc)
    C = Ctx()
    C.nc, C.P, C.dr, C.out_d, C.hscr, C.taps = nc, P, dr, out_d, hscr, set(taps)
    C.tap_out = {}
    with P.stack:
        C.ps = [nc.alloc_psum_tensor(f"ps{i}", [128, 512], F32) for i in range(8)]
        C.PB = [Buf(f"ps{i}", excl=True, persist=True) for i in range(8)]
        C.ps_rr = 0
        C.ps_reserved = set()
        C.xT = nc.alloc_sbuf_tensor("xT", [128, 8, S], BF16)
        C.XT = [Buf(f"xT{t}", persist=True) for t in range(NT)]
        C.OAT = [Buf(f"oAT{t}", persist=True) for t in range(NT)]
        C.OBT = [Buf(f"oBT{t}", persist=True) for t in range(NT)]
        C.wst = [nc.alloc_sbuf_tensor(f"wst{i}", [128, 8, 256], BF16) for i in range(3)]
        C.WST = [Buf(f"wst{i}", persist=True) for i in range(3)]
        C.wst_tl = [P.new_dma_tl(f"wst{i}") for i in range(3)]
        C.wk = 0
        C.CONST = Buf("const")
        C.tl_const = {"sp": P.new_dma_tl("const_sp"), "pool": P.new_dma_tl("const_pool")}
        load_consts(C)
        with ExitStack() as ab:
            C.oAT = ab.enter_context(nc.sbuf_tensor("oAT", [128, 4, S], BF16))
            with ExitStack() as ph:
                phase_x(C, ph)
                if "gdn" in phases:
                    phase_gdn(C, ph)
            P.barrier()
            C.oBT = ab.enter_context(nc.sbuf_tensor("oBT", [128, 4, S], BF16))
            if "nsa" in phases:
                with ExitStack() as ph:
                    phase_nsa(C, ph)
                P.barrier()
            if "mix" in phases:
                with ExitStack() as ph:
                    phase_mix(C, ph)
                P.barrier()
        if "ffn" in phases:
            with ExitStack() as ph:
                phase_ffn(C, ph)
        P.final_wait("sp", C.final_bufs)
        P.emit()
    return nc, C


def getps(C):
    while True:
        i = C.ps_rr
        C.ps_rr = (i + 1) % 8
        if i not in C.ps_reserved:
            return C.ps[i], C.PB[i]


def reserve_ps(C):
    while True:
        i = C.ps_rr
        C.ps_rr = (i + 1) % 8
        if i not in C.ps_reserved:
            C.ps_reserved.add(i)
            return i, C.ps[i], C.PB[i]


def const_dma(C, eng, out, in_):
    P = C.P
    tl = C.tl_const[eng]
    nbytes = _free(out) * int(out.shape[0]) * 4

    def fn(e, out=out, in_=in_):
        return e.dma_start(out=out, in_=in_)

    o = Op(len(P.ops), eng, fn, tl, set(), P.epoch, 2200.0 + nbytes / 150.0, 150.0 if eng == "sp" else 400.0)
    P.ops.append(o)
    if C.CONST.also is None:
        C.CONST.also = []
    C.CONST.also.append(o)


def cload(C, nc_alloc, name, key, shape, dtype=F32):
    t = nc_alloc(name, list(shape), dtype)
    const_dma(C, "pool" if dtype != F32 else "sp", t[:], C.dr[key])
    return t


def load_consts(C):
    nc = C.nc
    K = Ctx()
    C.K = K

    def al(name, shape, dtype):
        return nc.alloc_sbuf_tensor(name, shape, dtype)

    K.ident_f = cload(C, al, "k_identf", "c_ident", [128, 128])
    K.ident_b = cload(C, al, "k_identb", "c_ident", [128, 128], BF16)
    K.ones_b = nc.alloc_sbuf_tensor("k_onesb", [128, 128], BF16)
    C.ONES = Buf("ones")
    C.P.op("dve", lambda e: e.memset(K.ones_b[:], 1.0), writes=[C.ONES])


def tap(C, name, ap, shape, rd):
    if name not in C.taps:
        return
    t = C.nc.dram_tensor("tap_" + name, list(shape), ap.dtype, kind="ExternalOutput").ap()
    b = Buf("tap_" + name)
    C.P.dma("sp", t, ap, reads=rd, writes=[b])
    C.final_bufs.append(b)
    C.tap_out[name] = "tap_" + name


def phase_x(C, phx):
    nc, P, dr, K = C.nc, C.P, C.dr, C.K
    C.final_bufs = []
    xb = [phx.enter_context(nc.sbuf_tensor(f"xb{i}", [128, D], BF16)) for i in range(2)]
    XB = [Buf(f"xb{i}") for i in range(2)]
    tls = [P.new_dma_tl(f"xb{i}") for i in range(2)]
    for tt in range(NT):
        s = tt % 2
        P.dma("pool", xb[s][:, :], dr["x"][tt * 128:(tt + 1) * 128, :], writes=[XB[s]], tl=tls[s])
        pb, PBb = getps(C)
        pbv = pb.bitcast(BF16)
        for c in range(8):
            P.op("pe", lambda e, o=pbv[:, c * 128:(c + 1) * 128], i=xb[s][:, c * 128:(c + 1) * 128]:
                 e.transpose(o, i, K.ident_b[:, :]), reads=[XB[s], C.CONST], writes=[PBb])
        o = C.xT[:, :, tt * 128:(tt + 1) * 128]
        i = pbv[:, :].rearrange("p (c t) -> p c t", c=8)
        if tt % 2 == 0:
            P.op("act", lambda e, o=o, i=i: e.copy(o, i), reads=[PBb], writes=[C.XT[tt]])
        else:
            P.op("dve", lambda e, o=o, i=i: e.tensor_copy(o, i), reads=[PBb], writes=[C.XT[tt]])


def mm(C, out, lhsT, rhs, start, stop, rd, wr):
    C.P.op("pe", lambda e: e.matmul(out, lhsT, rhs, start=start, stop=stop), reads=rd, writes=wr,
           cost=PE_FIX + PE_COL * _free(rhs))


def actf(C, out, in_, func, rd, wr, bias=None, scale=None):
    kw = {}
    if bias is not None:
        kw["bias"] = bias
    if scale is not None:
        kw["scale"] = scale
    C.P.op("act", lambda e: e.activation(out, in_, func, **kw), reads=rd, writes=wr, cost=220.0 + 0.75 * _free(out))


def cpy(C, eng, out, in_, rd, wr):
    if eng == "act":
        C.P.op("act", lambda e: e.copy(out, in_), reads=rd, writes=wr, cost=220.0 + 0.75 * _free(out))
    else:
        C.P.op(eng, lambda e: e.tensor_copy(out, in_), reads=rd, writes=wr, cost=(120.0 + 1.05 * _free(out)) * (6.0 if eng == "pool" else 1.0))


def tt(C, eng, out, in0, in1, op, rd, wr):
    C.P.op(eng, lambda e: e.tensor_tensor(out, in0, in1, op), reads=rd, writes=wr, cost=(120.0 + 1.1 * _free(out)) * (6.0 if eng == "pool" else 1.0))


def ts(C, eng, out, in0, s1, op0, rd, wr, s2=None, op1=None):
    if op1 is None:
        C.P.op(eng, lambda e: e.tensor_scalar(out, in0, s1, None, op0), reads=rd, writes=wr, cost=(120.0 + 1.0 * _free(out)) * (6.0 if eng == "pool" else 1.0))
    else:
        C.P.op(eng, lambda e: e.tensor_scalar(out, in0, s1, s2, op0, op1), reads=rd, writes=wr, cost=(120.0 + 1.0 * _free(out)) * (6.0 if eng == "pool" else 1.0))


def stt(C, eng, out, in0, scalar, in1, op0, op1, rd, wr):
    C.P.op(eng, lambda e: e.scalar_tensor_tensor(out, in0, scalar, in1, op0, op1), reads=rd, writes=wr, cost=120.0 + 1.4 * _free(out))


def load_w(C, dst, key, col0, ncols, wr, tl=None, row0=0, nrow_chunks=8):
    src = C.dr[key][row0:row0 + 128 * nrow_chunks, col0:col0 + ncols].rearrange("(c p) n -> p c n", p=128)
    C.P.dma("pool", dst, src, writes=wr, tl=tl)


def bc_last(ap, n):
    shp = list(ap.shape)
    return ap.unsqueeze(len(shp)).to_broadcast(shp + [n])


def bc_mid(ap, n):
    shp = list(ap.shape)
    return ap.unsqueeze(1).to_broadcast([shp[0], n] + shp[1:])


def phase_gdn(C, ph):
    nc, P, dr, K = C.nc, C.P, C.dr, C.K
    CONST = C.CONST

    def sb(name, shape, dtype):
        return ph.enter_context(nc.sbuf_tensor("g_" + name, list(shape), dtype))

    K.tri = cload(C, sb, "k_tri", "c_tri", [128, 128], BF16)
    K.blk = cload(C, sb, "k_blk", "c_blk", [128, 128], BF16)
    K.nega = cload(C, sb, "k_nega", "c_nega", [128, 128], BF16)
    K.negq = cload(C, sb, "k_negq", "c_negq", [128, 128], BF16)
    K.sel12 = cload(C, sb, "k_sel12", "c_sel12", [12, 4, 128], BF16)
    K.selrow = cload(C, sb, "k_selrow", "c_selrow", [128, 2, 128], BF16)
    K.gconv = cload(C, sb, "k_gconv", "p_gconv", [128, 12, 4])
    K.alog = cload(C, sb, "k_alog", "p_alog", [128, NT, 4])
    K.dtb = cload(C, sb, "k_dtb", "p_dtb", [128, NT, 4])
    K.gnw = cload(C, sb, "k_gnw", "p_gnw", [128, 4, 128])

    qT = sb("qT", [128, 4, S], BF16)
    kT = sb("kT", [128, 4, S], BF16)
    vT = sb("vT", [128, 4, S], BF16)
    QKV = {"q": [Buf(f"qT{h}") for h in range(4)], "k": [Buf(f"kT{h}") for h in range(4)],
           "v": [Buf(f"vT{h}") for h in range(4)]}
    qkvT = {"q": qT, "k": kT, "v": vT}

    wba = sb("wba", [128, 8, 8], BF16)
    WBA = Buf("wba")
    load_w(C, wba[:], "w_in", C_GB, 8, [WBA])
    ba = sb("ba", [128, NT, 8], F32)
    BA = Buf("ba")
    pb, PBb = getps(C)
    for t in range(NT):
        for c in range(8):
            mm(C, pb[:, t * 8:(t + 1) * 8], C.xT[:, c, t * 128:(t + 1) * 128], wba[:, c, :], c == 0, c == 7,
               [C.XT[t], WBA], [PBb])
    cpy(C, "dve", ba[:].rearrange("p t c -> p (t c)"), pb[:, 0:NT * 8], [PBb], [BA])

    import os
    stop = os.environ.get("GDN_STOP", "")
    if stop == "a0":
        return
    ss = sb("ss", [128, NT, 8], F32)
    SSb = Buf("ss")
    ph1 = ExitStack()
    sb_outer = sb

    def sb(name, shape, dtype):
        return ph1.enter_context(nc.sbuf_tensor("g_" + name, list(shape), dtype))

    wq = [t[:, :, 0:128] for t in C.wst]
    WQ = C.WST
    tlw = C.wst_tl
    raw = [sb(f"raw{i}", [128, S + 3], F32) for i in range(2)]
    RAW = [Buf(f"raw{i}") for i in range(2)]
    acc = [sb(f"acc{i}", [128, S], F32) for i in range(2)]
    ACC = [Buf(f"acc{i}") for i in range(2)]
    sq = [sb(f"sq{i}", [128, S], BF16) for i in range(2)]
    SQ = [Buf(f"sq{i}") for i in range(2)]
    for i in range(2):
        P.op("dve", lambda e, o=raw[i][:, 0:3]: e.memset(o, 0.0), writes=[RAW[i]])
    ss_i, ss_pb, SS_PB = reserve_ps(C)
    n_ss = 0
    for ck in range(12):
        which = "qkv"[ck // 4]
        h = ck % 4
        ws = C.wk % 3
        C.wk += 1
        P.dma("pool", wq[ws], dr["pk_g"][ck], writes=[WQ[ws]], tl=tlw[ws])
        rs = ck % 2
        for tb in range(4):
            pb, PBb = getps(C)
            for c in range(8):
                mm(C, pb[:, :], wq[ws][:, c, :], C.xT[:, c, tb * 512:(tb + 1) * 512], c == 0, c == 7,
                   [WQ[ws]] + C.XT[tb * 4:(tb + 1) * 4], [PBb])
            cpy(C, "act", raw[rs][:, 3 + tb * 512:3 + (tb + 1) * 512], pb[:, :], [PBb], [RAW[rs]])
        ceng = "dve"
        ts(C, ceng, acc[rs][:, :], raw[rs][:, 3:S + 3], K.gconv[:, ck, 3:4], ALU.mult, [RAW[rs], CONST], [ACC[rs]])
        for j in (2, 1, 0):
            stt(C, ceng, acc[rs][:, :], raw[rs][:, j:S + j], K.gconv[:, ck, j:j + 1], acc[rs][:, :], ALU.mult, ALU.add,
                [RAW[rs], ACC[rs], CONST], [ACC[rs]])
        dst = qkvT[which][:, h, :]
        actf(C, dst, acc[rs][:, :], AF.Silu, [ACC[rs]], [QKV[which][h]])
        if which in "qk":
            col = (0 if which == "q" else 4) + h
            actf(C, sq[rs][:, :], dst, AF.Square, [QKV[which][h]], [SQ[rs]])
            for t in range(NT):
                mm(C, ss_pb[:, t * 8 + col:t * 8 + col + 1], sq[rs][:, t * 128:(t + 1) * 128], K.ones_b[:, 0:1],
                   True, True, [SQ[rs], C.ONES], [SS_PB])
    cpy(C, "dve", ss[:].rearrange("p t c -> p (t c)"), ss_pb[:, 0:NT * 8], [SS_PB], [SSb])
    C.ps_reserved.discard(ss_i)
    P.barrier()
    ph1.close()
    sb = sb_outer
    tap(C, "qT", qT[:, 0, :], [128, S], QKV["q"])
    tap(C, "vT", vT[:, 1, :], [128, S], QKV["v"])

    if stop == "a1":
        return
    names = ["t1", "t2", "t3", "spa", "spb", "g", "lb", "gc", "gl", "lrk", "lrq", "biasA", "biasQ", "rowQ",
             "s_kbg", "s_kdec", "beta", "s_o", "ea", "ya", "yb"]
    st = {n: sb("st_" + n, [128, NT, 4], F32) for n in names}
    SB_ = {n: Buf("st_" + n) for n in names}

    def softplus(dst, y):
        actf(C, st["t1"][:], st[y][:], AF.Abs, [SB_[y]], [SB_["t1"]])
        actf(C, st["t2"][:], st["t1"][:], AF.Exp, [SB_["t1"]], [SB_["t2"]], scale=-1.0)
        actf(C, st["t3"][:], st["t2"][:], AF.Ln, [SB_["t2"]], [SB_["t3"]], bias=1.0)
        stt(C, "dve", st[dst][:], st[y][:], 0.0, st["t3"][:], ALU.max, ALU.add, [SB_[y], SB_["t3"]], [SB_[dst]])

    tt(C, "dve", st["ya"][:], ba[:, :, 4:8], K.dtb[:], ALU.add, [BA, CONST], [SB_["ya"]])
    softplus("spa", "ya")
    actf(C, st["ea"][:], K.alog[:], AF.Exp, [CONST], [SB_["ea"]])
    stt(C, "dve", st["g"][:], st["spa"][:], -1.0, st["ea"][:], ALU.mult, ALU.mult, [SB_["spa"], SB_["ea"]], [SB_["g"]])
    ts(C, "dve", st["yb"][:], ba[:, :, 0:4], -1.0, ALU.mult, [BA], [SB_["yb"]])
    softplus("spb", "yb")
    ts(C, "dve", st["lb"][:], st["spb"][:], -1.0, ALU.mult, [SB_["spb"]], [SB_["lb"]])
    lrt = sb("lrt", [128, NT, 8], F32)
    LRT = Buf("lrt")
    actf(C, lrt[:], ss[:], AF.Ln, [SSb], [LRT], bias=RMS_EPS)
    ts(C, "dve", st["lrq"][:], lrt[:, :, 0:4], -0.5, ALU.mult, [LRT], [SB_["lrq"]], s2=-0.5 * math.log(128.0), op1=ALU.add)
    ts(C, "dve", st["lrk"][:], lrt[:, :, 4:8], -0.5, ALU.mult, [LRT], [SB_["lrk"]])
    spl_r = sb("spl_r", [128, NT, 4], F32)
    SPLR = Buf("spl_r")

    def split3(name, src):
        x3 = sb("x3_" + name, [128, 3, NT, 4], BF16)
        X3 = Buf("x3_" + name)
        cur, CUR = st[src], SB_[src]
        for k in range(3):
            cpy(C, "dve", x3[:, k, :, :], cur[:], [CUR], [X3])
            if k < 2:
                tt(C, "dve", spl_r[:], cur[:], x3[:, k, :, :], ALU.subtract, [CUR, X3], [SPLR])
                cur, CUR = spl_r, SPLR
        return x3, X3

    g3, G3 = split3("g", "g")
    pb, PBb = getps(C)
    for t in range(NT):
        for k in range(3):
            mm(C, pb[:, t * 4:(t + 1) * 4], K.tri[:, :], g3[:, k, t, :], k == 0, k == 2, [CONST, G3], [PBb])
        for k in range(3):
            mm(C, pb[:, 64 + t * 4:64 + (t + 1) * 4], K.blk[:, :], g3[:, k, t, :], k == 0, k == 2, [CONST, G3], [PBb])
    cpy(C, "dve", st["gc"][:].rearrange("p t c -> p (t c)"), pb[:, 0:64], [PBb], [SB_["gc"]])
    cpy(C, "dve", st["gl"][:].rearrange("p t c -> p (t c)"), pb[:, 64:128], [PBb], [SB_["gl"]])
    tt(C, "dve", st["biasQ"][:], st["lrk"][:], st["gc"][:], ALU.subtract, [SB_["lrk"], SB_["gc"]], [SB_["biasQ"]])
    tt(C, "dve", st["biasA"][:], st["gc"][:], st["lb"][:], ALU.add, [SB_["gc"], SB_["lb"]], [SB_["biasA"]])
    tt(C, "dve", st["biasA"][:], st["biasA"][:], st["lrk"][:], ALU.add, [SB_["biasA"], SB_["lrk"]], [SB_["biasA"]])
    tt(C, "dve", st["rowQ"][:], st["gc"][:], st["lrq"][:], ALU.add, [SB_["gc"], SB_["lrq"]], [SB_["rowQ"]])
    actf(C, st["s_kbg"][:], st["biasA"][:], AF.Exp, [SB_["biasA"]], [SB_["s_kbg"]])
    tt(C, "dve", st["t1"][:], st["biasQ"][:], st["gl"][:], ALU.add, [SB_["biasQ"], SB_["gl"]], [SB_["t1"]])
    actf(C, st["s_kdec"][:], st["t1"][:], AF.Exp, [SB_["t1"]], [SB_["s_kdec"]])
    actf(C, st["beta"][:], st["lb"][:], AF.Exp, [SB_["lb"]], [SB_["beta"]])
    actf(C, st["s_o"][:], st["rowQ"][:], AF.Exp, [SB_["rowQ"]], [SB_["s_o"]])
    tap(C, "g", st["g"][:], [128, NT, 4], [SB_["g"]])
    tap(C, "gc", st["gc"][:], [128, NT, 4], [SB_["gc"]])
    tap(C, "beta", st["beta"][:], [128, NT, 4], [SB_["beta"]])
    tap(C, "lrk", st["lrk"][:], [128, NT, 4], [SB_["lrk"]])

    if stop == "stats":
        return
    eglb = sb("eglb", [128, 2, NT * 4], F32)
    EGLB = Buf("eglb")
    gl3, GL3 = split3("gl", "gl")
    pb, PBb = getps(C)
    for c in range(2):
        for k in range(3):
            mm(C, pb[:, c * 64:(c + 1) * 64], K.selrow[:, c, :], gl3[:, k, :, :].rearrange("p t h -> p (t h)"),
               k == 0, k == 2, [CONST, GL3], [PBb])
    actf(C, eglb[:].rearrange("p c n -> p (c n)"), pb[:, 0:128], AF.Exp, [PBb], [EGLB])
    bq3, BQ3 = split3("bq", "biasQ")
    rq3, RQ3 = split3("rq", "rowQ")

    if stop == "eglb":
        return
    def dbl(name, shape, dtype, n=2):
        return [sb(f"{name}{i}", shape, dtype) for i in range(n)], [Buf(f"{name}{i}") for i in range(n)]

    H4 = [128, 4, 128]
    kbg, KBG = dbl("kbg", H4, BF16)
    kdec, KDEC = dbl("kdec", H4, BF16, 3)
    vb, VB = dbl("vb", H4, BF16)
    expA, EXPA = dbl("expA", H4, F32)
    expQ, EXPQ = dbl("expQ", H4, F32)
    Xs, XS = dbl("X", H4, BF16, 6)
    Ys, YS = dbl("Y", H4, BF16, 6)
    Ws, WS = dbl("W", H4, BF16, 6)
    qkT, QKT = dbl("qkT", H4, BF16, 3)
    u0, U0 = dbl("u0", H4, F32, 3)
    kcdT, KCDT = dbl("kcdT", H4, BF16, 3)
    ut, UT = dbl("u", H4, BF16)
    ot, OT = dbl("o", H4, F32)
    tmpo, TMPO = dbl("tmpo", H4, F32)
    Sst = sb("S", H4, F32)
    Sb = sb("Sb", H4, BF16)
    SST, SBB = Buf("S"), Buf("Sb")
    P.op("dve", lambda e: e.memset(Sst[:], 0.0), writes=[SST])
    P.op("pool", lambda e: e.memset(Sb[:], 0.0), writes=[SBB])
    wz = sb("wz", [128, 8, 512], BF16)
    WZ = Buf("wz")
    load_w(C, wz[:], "w_in", C_GZ, 512, [WZ])
    sz, SZ = dbl("sz", [128, 512], F32)
    osq, OSQ = dbl("osq", H4, F32)
    rr_, RR = dbl("rr", [128, 4], F32)
    oab, OAB = dbl("oab", H4, BF16)
    ident4 = sb("ident4", H4, F32)
    ID4 = Buf("ident4")
    for h in range(4):
        cpy(C, "dve", ident4[:, h, :], K.ident_b[:, :], [CONST], [ID4])
    rowA, ROWA = dbl("rowA", [12, 128], BF16)
    rowQ, ROWQ = dbl("rowQ", [12, 128], BF16)
    r12, R12 = dbl("r12", [128, 2, 4, 3], BF16)

    def prep(p):
        s = p % 2
        s3 = p % 3
        tsl = slice(p * 128, (p + 1) * 128)
        pbk, PBK = getps(C)
        pbkv = pbk.bitcast(BF16)
        for h in range(4):
            P.op("pe", lambda e, o=pbkv[:, h * 128:(h + 1) * 128], i=kT[:, h, tsl]: e.transpose(o, i, K.ident_b[:, :]),
                 reads=[QKV["k"][h], CONST], writes=[PBK])
        kin = pbkv[:, 0:512].rearrange("p (h d) -> p h d", h=4)
        tt(C, "dve", kbg[s][:], kin, bc_last(st["s_kbg"][:, p, :], 128), ALU.mult, [PBK, SB_["s_kbg"]], [KBG[s]])
        tt(C, "dve", kdec[s3][:], kin, bc_last(st["s_kdec"][:, p, :], 128), ALU.mult, [PBK, SB_["s_kdec"]], [KDEC[s3]])
        pbv_, PBV = getps(C)
        pbvv = pbv_.bitcast(BF16)
        for h in range(4):
            P.op("pe", lambda e, o=pbvv[:, h * 128:(h + 1) * 128], i=vT[:, h, tsl]: e.transpose(o, i, K.ident_b[:, :]),
                 reads=[QKV["v"][h], CONST], writes=[PBV])
        vin = pbvv[:, 0:512].rearrange("p (h d) -> p h d", h=4)
        tt(C, "dve", vb[s][:], vin, bc_last(st["beta"][:, p, :], 128), ALU.mult, [PBV, SB_["beta"]], [VB[s]])
        yield
        cpy(C, "dve", r12[s][:, 0, :, :], bq3[:, :, p, :].rearrange("p k h -> p h k"), [BQ3], [R12[s]])
        cpy(C, "dve", r12[s][:, 1, :, :], rq3[:, :, p, :].rearrange("p k h -> p h k"), [RQ3], [R12[s]])
        prw, PRW = getps(C)
        prwv = prw.bitcast(BF16)
        P.op("pe", lambda e: e.transpose(prwv[0:12, 0:128], r12[s][:, 0, :, :].rearrange("p h k -> p (h k)"), K.ident_b[:, :]),
             reads=[R12[s], CONST], writes=[PRW])
        P.op("pe", lambda e: e.transpose(prwv[0:12, 128:256], r12[s][:, 1, :, :].rearrange("p h k -> p (h k)"), K.ident_b[:, :]),
             reads=[R12[s], CONST], writes=[PRW])
        cpy(C, "dve", rowA[s][0:12, :], prwv[0:12, 0:128], [PRW], [ROWA[s]])
        cpy(C, "dve", rowQ[s][0:12, :], prwv[0:12, 128:256], [PRW], [ROWQ[s]])
        pea, PEA = getps(C)
        peq, PEQ = getps(C)
        for h in range(4):
            hs = slice(h * 128, (h + 1) * 128)
            mm(C, pea[:, hs], K.sel12[0:12, h, :], rowA[s][0:12, :], True, False, [CONST, ROWA[s]], [PEA])
            mm(C, pea[:, hs], K.ident_b[:, :], K.nega[:, :], False, True, [CONST], [PEA])
            mm(C, peq[:, hs], K.sel12[0:12, h, :], rowQ[s][0:12, :], True, False, [CONST, ROWQ[s]], [PEQ])
            mm(C, peq[:, hs], K.ident_b[:, :], K.negq[:, :], False, True, [CONST], [PEQ])
        pkk, PKK = getps(C)
        pkq, PKQ = getps(C)
        for h in range(4):
            hs = slice(h * 128, (h + 1) * 128)
            mm(C, pkk[:, hs], kT[:, h, tsl], kT[:, h, tsl], True, True, [QKV["k"][h]], [PKK])
            mm(C, pkq[:, hs], kT[:, h, tsl], qT[:, h, tsl], True, True, [QKV["k"][h], QKV["q"][h]], [PKQ])
        for h in range(4):
            hs = slice(h * 128, (h + 1) * 128)
            actf(C, expA[s][:, h, :], pea[:, hs], AF.Exp, [PEA, SB_["biasA"]], [EXPA[s]], bias=st["biasA"][:, p, h:h + 1])
            actf(C, expQ[s][:, h, :], peq[:, hs], AF.Exp, [PEQ, SB_["biasQ"]], [EXPQ[s]], bias=st["biasQ"][:, p, h:h + 1])
        x0 = s * 3
        tt(C, "dve", Xs[x0][:].rearrange("p h d -> p (h d)"), pkk[:, :], expA[s][:].rearrange("p h d -> p (h d)"),
           ALU.mult, [PKK, EXPA[s]], [XS[x0]])
        tt(C, "dve", qkT[s3][:].rearrange("p h d -> p (h d)"), pkq[:, :], expQ[s][:].rearrange("p h d -> p (h d)"),
           ALU.mult, [PKQ, EXPQ[s]], [QKT[s3]])
        yield
        pbt, PBT = getps(C)
        pbtv = pbt.bitcast(BF16)
        for h in range(4):
            P.op("pe", lambda e, o=pbtv[:, h * 128:(h + 1) * 128], i=Xs[x0][:, h, :]: e.transpose(o, i, K.ident_b[:, :]),
                 reads=[XS[x0], CONST], writes=[PBT])
        bin_ = pbtv[:, 0:512].rearrange("p (h d) -> p h d", h=4)
        cpy(C, "act", Ys[x0][:], bin_, [PBT], [YS[x0]])
        tt(C, "dve", Ws[x0][:], ident4[:], bin_, ALU.subtract, [PBT, ID4], [WS[x0]])
        yield
        for lvl in range(1, 6):
            xi, xo = s * 3 + (lvl - 1) % 3, s * 3 + lvl % 3
            pa, PA = getps(C)
            for h in range(4):
                hs = slice(h * 128, (h + 1) * 128)
                mm(C, pa[:, hs], Ys[xi][:, h, :], Xs[xi][:, h, :], True, True, [YS[xi], XS[xi]], [PA])
            cpy(C, "act", Xs[xo][:].rearrange("p h d -> p (h d)"), pa[:, :], [PA], [XS[xo]])
            if lvl < 5:
                pbb, PBB_ = getps(C)
                for h in range(4):
                    hs = slice(h * 128, (h + 1) * 128)
                    mm(C, pbb[:, hs], Xs[xi][:, h, :], Ys[xi][:, h, :], True, True, [YS[xi], XS[xi]], [PBB_])
                cpy(C, "act", Ys[xo][:].rearrange("p h d -> p (h d)"), pbb[:, :], [PBB_], [YS[xo]])
            yield
            pw, PW = getps(C)
            for h in range(4):
                hs = slice(h * 128, (h + 1) * 128)
                mm(C, pw[:, hs], Xs[xo][:, h, :], Ws[xi][:, h, :], True, True, [XS[xo], WS[xi]], [PW])
            tt(C, "dve", Ws[xo][:].rearrange("p h d -> p (h d)"), pw[:, :], Ws[xi][:].rearrange("p h d -> p (h d)"),
               ALU.add, [PW, WS[xi]], [WS[xo]])
            yield
        wf = s * 3 + 5 % 3
        pu, PU = getps(C)
        pk, PK = getps(C)
        for h in range(4):
            hs = slice(h * 128, (h + 1) * 128)
            mm(C, pu[:, hs], Ws[wf][:, h, :], vb[s][:, h, :], True, True, [WS[wf], VB[s]], [PU])
            mm(C, pk[:, hs], kbg[s][:, h, :], Ws[wf][:, h, :], True, True, [WS[wf], KBG[s]], [PK])
        cpy(C, "act", u0[s3][:].rearrange("p h d -> p (h d)"), pu[:, :], [PU], [U0[s3]])
        cpy(C, "dve", kcdT[s3][:].rearrange("p h d -> p (h d)"), pk[:, :], [PK], [KCDT[s3]])
        yield

    def scan(p):
        s = p % 2
        s3 = p % 3
        tsl = slice(p * 128, (p + 1) * 128)
        for c in range(2):
            r = slice(64 * c, 64 * c + 64)
            n = 2 * p + c
            pm1, PM1 = getps(C)
            for h in range(4):
                hs = slice(h * 128, (h + 1) * 128)
                mm(C, pm1[:, hs], kcdT[s3][:, h, :], Sb[:, h, :], True, True, [KCDT[s3], SBB], [PM1])
            tt(C, "dve", ut[s][r, :, :].rearrange("p h d -> p (h d)"), u0[s3][r, :, :].rearrange("p h d -> p (h d)"),
               pm1[r, :], ALU.subtract, [U0[s3], PM1], [UT[s]])
            pm2i, pm2, PM2 = reserve_ps(C)
            for h in range(4):
                hs = slice(h * 128, (h + 1) * 128)
                mm(C, pm2[:, hs], qT[:, h, tsl], Sb[:, h, :], True, True, [QKV["q"][h], SBB], [PM2])
            yield
            pm3, PM3 = getps(C)
            pm4, PM4 = getps(C)
            for h in range(4):
                hs = slice(h * 128, (h + 1) * 128)
                mm(C, pm3[:, hs], qkT[s3][r, h, :], ut[s][r, h, :], True, True, [QKT[s3], UT[s]], [PM3])
                mm(C, pm4[:, hs], kdec[s3][r, h, :], ut[s][r, h, :], True, True, [KDEC[s3], UT[s]], [PM4])
            for h in range(4):
                hs = slice(h * 128, (h + 1) * 128)
                actf(C, tmpo[s][r, h, :], pm2[r, hs], AF.Identity, [PM2, SB_["s_o"]], [TMPO[s]], scale=st["s_o"][r, p, h:h + 1])
            C.ps_reserved.discard(pm2i)
            tt(C, "dve", ot[s][r, :, :].rearrange("p h d -> p (h d)"), tmpo[s][r, :, :].rearrange("p h d -> p (h d)"),
               pm3[r, :], ALU.add, [TMPO[s], PM3], [OT[s]])
            for h in range(4):
                hs = slice(h * 128, (h + 1) * 128)
                stt(C, "dve", Sst[:, h, :], Sst[:, h, :], eglb[:, c, p * 4 + h:p * 4 + h + 1], pm4[:, hs], ALU.mult, ALU.add,
                    [SST, EGLB, PM4], [SST])
            cpy(C, "act", Sb[:].rearrange("p h d -> p (h d)"), Sst[:].rearrange("p h d -> p (h d)"), [SST], [SBB])
            yield

    def outp(p):
        s = p % 2
        tsl = slice(p * 128, (p + 1) * 128)
        pz, PZ = getps(C)
        for c in range(8):
            mm(C, pz[:, :], C.xT[:, c, tsl], wz[:, c, :], c == 0, c == 7, [C.XT[p], WZ], [PZ])
        actf(C, sz[s][:, :], pz[:, :], AF.Silu, [PZ], [SZ[s]])
        o2 = ot[s][:].rearrange("p h d -> p (h d)")
        tt(C, "dve", osq[s][:].rearrange("p h d -> p (h d)"), o2, o2, ALU.mult, [OT[s]], [OSQ[s]])
        P.op("dve", lambda e: e.tensor_reduce(rr_[s][:, :], osq[s][:], AX.X, ALU.add), reads=[OSQ[s]], writes=[RR[s]])
        actf(C, rr_[s][:, :], rr_[s][:, :], AF.Ln, [RR[s]], [RR[s]], scale=1.0 / 128.0, bias=RMS_EPS)
        actf(C, rr_[s][:, :], rr_[s][:, :], AF.Exp, [RR[s]], [RR[s]], scale=-0.5)
        tt(C, "dve", osq[s][:], ot[s][:], bc_last(rr_[s][:, :], 128), ALU.mult, [OT[s], RR[s]], [OSQ[s]])
        tt(C, "dve", osq[s][:], osq[s][:], K.gnw[:], ALU.mult, [OSQ[s], CONST], [OSQ[s]])
        tt(C, "dve", oab[s][:].rearrange("p h d -> p (h d)"), osq[s][:].rearrange("p h d -> p (h d)"), sz[s][:, :],
           ALU.mult, [OSQ[s], SZ[s]], [OAB[s]])
        if p == 0:
            tap(C, "o_raw0", ot[s][:], [128, 4, 128], [OT[s]])
            tap(C, "oab0", oab[s][:], [128, 4, 128], [OAB[s]])
        if p == 9:
            tap(C, "o_raw9", ot[s][:], [128, 4, 128], [OT[s]])
        pt, PT = getps(C)
        ptv = pt.bitcast(BF16)
        for h in range(4):
            P.op("pe", lambda e, o=ptv[:, h * 128:(h + 1) * 128], i=oab[s][:, h, :]: e.transpose(o, i, K.ident_b[:, :]),
                 reads=[OAB[s], CONST], writes=[PT])
        cpy(C, "act", C.oAT[:, :, tsl], ptv[:, 0:512].rearrange("p (h d) -> p h d", h=4), [PT], [C.OAT[p]])
        yield

    SCAN_PRIO = int(os.environ.get("GDN_SCAN_PRIO", "1"))

    def scan_out(p):
        g_ = scan(p)
        while True:
            P.cur_prio = SCAN_PRIO
            try:
                next(g_)
            except StopIteration:
                P.cur_prio = 0
                break
            P.cur_prio = 0
            yield
        yield from outp(p)

    if stop != "":
        for p in range(1):
            nst = int(stop[4:]) if stop.startswith("prep") and len(stop) > 4 else 999
            for i_, _ in enumerate(prep(p)):
                if i_ + 1 >= nst:
                    break
            if stop.startswith("prep"):
                continue
            for _ in scan(p):
                pass
            if stop == "scan":
                continue
            for _ in outp(p):
                pass
    else:
        npar = int(os.environ.get("GDN_NPAR", "2"))
        active = []
        next_prep = 0
        next_scan = 0
        prep_done = set()
        scan_gen = None
        while next_scan < NT:
            while len(active) < npar and next_prep < NT and next_prep <= next_scan + 2:
                active.append([next_prep, prep(next_prep)])
                next_prep += 1
            if scan_gen is None and next_scan in prep_done:
                scan_gen = scan_out(next_scan)
            for ent in list(active):
                try:
                    next(ent[1])
                except StopIteration:
                    prep_done.add(ent[0])
                    active.remove(ent)
            if scan_gen is not None:
                try:
                    next(scan_gen)
                except StopIteration:
                    scan_gen = None
                    next_scan += 1
    tap(C, "oAT", C.oAT[:, 0, :], [128, S], C.OAT)


def mm2(C, out, lhsT, rhs, start, stop, rd, wr):
    C.P.op("pe", lambda e: e.matmul(out, lhsT, rhs, start=start, stop=stop, skip_group_check=True), reads=rd, writes=wr,
           cost=PE_FIX + PE_COL * _free(rhs))


def phase_nsa(C, ph):
    import os
    nc, P, dr, K = C.nc, C.P, C.dr, C.K
    CONST = C.CONST
    stop = os.environ.get("NSA_STOP", "")

    def sb(name, shape, dtype):
        return ph.enter_context(nc.sbuf_tensor("n_" + name, list(shape), dtype))

    phs1 = ExitStack()

    def sb1(name, shape, dtype):
        return phs1.enter_context(nc.sbuf_tensor("n_" + name, list(shape), dtype))

    K.cmpmask = cload(C, sb, "k_cmpmask", "c_cmpmask", [128, S], BF16)
    K.overlap = cload(C, sb, "k_overlap", "c_overlap", [128, 32], BF16)
    K.causal = cload(C, sb, "k_causal", "c_causal", [128, 128], BF16)
    K.anti = cload(C, sb, "k_anti", "c_anti", [128, 128], BF16)
    K.expand = cload(C, sb, "k_expand", "c_expand", [32, 16, 128], BF16)
    K.forced = cload(C, sb, "k_forced", "c_forced", [128, 8, 32])
    K.poskT = cload(C, sb, "k_poskT", "p_poskT", [128, 32], BF16)
    K.posvT = cload(C, sb, "k_posvT", "p_posvT", [128, 32], BF16)

    QT = sb("QT", [64, 8, S], BF16)
    QTB = [Buf(f"QT{i}") for i in range(8)]
    KsT = sb("KsT", [64, 2, S], BF16)
    KwT = sb("KwT", [64, 2, S], BF16)
    KST = [Buf(f"KsT{g}") for g in range(2)]
    KWT = [Buf(f"KwT{g}") for g in range(2)]
    KcTc = sb("KcTc", [64, 2, 128], BF16)
    Vca = sb("Vca", [128, 2, 97], BF16)
    Vs = sb("Vs", [128, NT, 2, 65], BF16)
    Vw = sb("Vw", [128, NT, 2, 65], BF16)
    VS, VW = Buf("Vs"), Buf("Vw")
    gts = sb("gates", [128, NT, 24], F32)
    KcT = sb1("KcT", [128, S], BF16)
    VcT = sb1("VcT", [128, S], BF16)
    KCT, VCT = Buf("KcT"), Buf("VcT")
    GTS = Buf("gates")
    P.op("pool", lambda e: e.memset(Vs[:], 1.0), writes=[VS])
    P.op("pool", lambda e: e.memset(Vw[:], 1.0), writes=[VW])

    wt = [t[:, :, 0:128] for t in C.wst]
    WT = C.WST
    tlw = C.wst_tl
    hi = [sb1(f"hi{i}", [128, 512], BF16) for i in range(2)]
    HI = [Buf(f"hi{i}") for i in range(2)]
    nhi = 0
    jobs = [("q", 0), ("q", 1), ("q", 2), ("q", 3), ("ks", 0), ("kw", 0), ("kc", 0), ("vc", 0)]
    for ji, (kind, idx) in enumerate(jobs):
        ws = C.wk % 3
        C.wk += 1
        P.dma("pool", C.wst[ws][:, :, 0:128], dr["pk_n128"][ji], writes=[WT[ws]], tl=tlw[ws])
        for tb in range(4):
            pb, PBb = getps(C)
            for c in range(8):
                mm(C, pb[:, :], wt[ws][:, c, :], C.xT[:, c, tb * 512:(tb + 1) * 512], c == 0, c == 7,
                   [WT[ws]] + C.XT[tb * 4:(tb + 1) * 4], [PBb])
            tsl = slice(tb * 512, (tb + 1) * 512)
            if kind == "kc":
                cpy(C, "act", KcT[:, tsl], pb[:, :], [PBb], [KCT])
            elif kind == "vc":
                cpy(C, "dve", VcT[:, tsl], pb[:, :], [PBb], [VCT])
            else:
                hs_ = nhi % 2
                nhi += 1
                if kind == "q":
                    actf(C, QT[:, 2 * idx, tsl], pb[0:64, :], AF.Copy, [PBb], [QTB[2 * idx]], scale=0.125)
                    actf(C, hi[hs_][64:128, :], pb[64:128, :], AF.Copy, [PBb], [HI[hs_]], scale=0.125)
                    P.dma("sp", QT[:, 2 * idx + 1, tsl], hi[hs_][64:128, :], reads=[HI[hs_]], writes=[QTB[2 * idx + 1]])
                else:
                    dst, DST = (KsT, KST) if kind == "ks" else (KwT, KWT)
                    cpy(C, "dve", dst[:, 0, tsl], pb[0:64, :], [PBb], [DST[0]])
                    cpy(C, "dve", hi[hs_][64:128, :], pb[64:128, :], [PBb], [HI[hs_]])
                    P.dma("sp", dst[:, 1, tsl], hi[hs_][64:128, :], reads=[HI[hs_]], writes=[DST[1]])
    wv = sb1("wv", [128, 8, 280], BF16)
    WV = Buf("wv")
    tlv = P.new_dma_tl("nwv")
    for (c0, ncol, dcol) in ((C_NVS, 128, 0), (C_NVW, 128, 128), (C_NG, 24, 256)):
        src = dr["w_in"][:, c0:c0 + ncol].rearrange("(c p) n -> p c n", p=128)
        P.dma("pool", wv[:, :, dcol:dcol + ncol], src, writes=[WV], tl=tlv)
    gtmp = sb1("gtmp", [128, NT, 24], F32)
    GTMP = Buf("gtmp")
    for t in range(NT):
        pb, PBb = getps(C)
        for c in range(8):
            mm(C, pb[:, 0:280], C.xT[:, c, t * 128:(t + 1) * 128], wv[:, c, :], c == 0, c == 7, [C.XT[t], WV], [PBb])
        cpy(C, "act", Vs[:, t, :, 0:64], pb[:, 0:128].rearrange("p (g d) -> p g d", g=2), [PBb], [VS])
        cpy(C, "dve", Vw[:, t, :, 0:64], pb[:, 128:256].rearrange("p (g d) -> p g d", g=2), [PBb], [VW])
        actf(C, gtmp[:, t, :], pb[:, 256:280], AF.Tanh, [PBb], [GTMP], scale=0.5)
    ts(C, "dve", gts[:], gtmp[:], 0.5, ALU.mult, [GTMP], [GTS], s2=0.5, op1=ALU.add)
    if stop == "proj":
        tap(C, "x_QT", QT[:, 3, :], [64, S], QTB)
        tap(C, "x_KsT", KsT[:, 1, :], [64, S], KST)
        tap(C, "x_Vw", Vw[:], [128, NT, 2, 65], [VW])
        tap(C, "x_gates", gts[:], [128, NT, 24], [GTS])
        return

    KCTC, VCA = Buf("KcTc"), Buf("Vca")
    P.op("pool", lambda e: e.memset(KcTc[:], 0.0), writes=[KCTC])
    P.op("pool", lambda e: e.memset(Vca[:], 0.0), writes=[VCA])
    w1 = sb1("w1", [128, 32, 256], BF16)
    W1B = Buf("w1")
    tl1 = P.new_dma_tl("nw1")
    w2k = sb1("w2k", [128, 2, 64], BF16)
    w2v = sb1("w2v", [128, 2, 64], BF16)
    W2K, W2V = Buf("w2k"), Buf("w2v")
    P.dma("pool", w2k[:, :, :], dr["cmp_w2_k"].rearrange("(j c) d -> c j d", c=128), writes=[W2K])
    P.dma("pool", w2v[:, :, :], dr["cmp_w2_v"].rearrange("(j c) d -> c j d", c=128), writes=[W2V])
    hx = sb1("hx", [128, 128], F32)
    hx2 = sb1("hx2", [128, 128], F32)
    hth = sb1("hth", [128, 128], F32)
    h1 = sb1("h1", [128, 2, 128], BF16)
    b1 = sb1("b1", [128, 2], F32)
    HX, HX2, HTH, H1, B1 = Buf("hx"), Buf("hx2"), Buf("hth"), Buf("h1"), Buf("b1")
    for kv in ("k", "v"):
        key = "cmp_w1_" + kv
        src = dr[key].rearrange("(l d) c -> d l c", d=64)
        P.dma("pool", w1[0:64, :, :], src, writes=[W1B], tl=tl1)
        P.dma("pool", w1[64:128, :, :], src, writes=[W1B], tl=tl1)
        XcT, XCT = (KcT, KCT) if kv == "k" else (VcT, VCT)
        posT = K.poskT if kv == "k" else K.posvT
        for g in range(2):
            hr = slice(64 * g, 64 * g + 64)
            pbb, PBB_ = getps(C)
            for j in range(2):
                for l in range(32):
                    mm(C, pbb[:, j:j + 1], w1[hr, l, j * 128:(j + 1) * 128], posT[hr, l:l + 1], l == 0, l == 31,
                       [W1B, CONST], [PBB_])
            cpy(C, "dve", b1[:, :], pbb[:, 0:2], [PBB_], [B1])
            P.op("dve", lambda e: e.memset(h1[:], 0.0), writes=[H1])
            for j in range(2):
                ph1, PH1 = getps(C)
                xv = XcT[hr, :].rearrange("p (n r) -> p n r", r=16)
                for l in range(32):
                    rhs = xv[:, (l // 16):(l // 16) + 127, l % 16]
                    mm(C, ph1[:, 0:127], w1[hr, l, j * 128:(j + 1) * 128], rhs, l == 0, l == 31, [W1B, XCT], [PH1])
                ts(C, "dve", hx[:, 0:127], ph1[:, 0:127], b1[:, j:j + 1], ALU.add, [PH1, B1], [HX])
                tt(C, "dve", hx2[:, 0:127], hx[:, 0:127], hx[:, 0:127], ALU.mult, [HX], [HX2])
                ts(C, "dve", hx2[:, 0:127], hx2[:, 0:127], 0.044715, ALU.mult, [HX2], [HX2], s2=1.0, op1=ALU.add)
                tt(C, "dve", hx2[:, 0:127], hx2[:, 0:127], hx[:, 0:127], ALU.mult, [HX2, HX], [HX2])
                actf(C, hth[:, 0:127], hx2[:, 0:127], AF.Tanh, [HX2], [HTH], scale=0.7978845608028654)
                stt(C, "dve", hth[:, 0:127], hth[:, 0:127], 1.0, hx[:, 0:127], ALU.add, ALU.mult, [HTH, HX], [HTH])
                ts(C, "dve", h1[:, j, 0:127], hth[:, 0:127], 0.5, ALU.mult, [HTH], [H1])
            po, PO = getps(C)
            if kv == "k":
                for j in range(2):
                    mm(C, po[0:64, 0:128], w2k[:, j, :], h1[:, j, :], j == 0, j == 1, [W2K, H1], [PO])
                cpy(C, "dve", KcTc[:, g, 0:127], po[0:64, 0:127], [PO], [KCTC])
            else:
                for j in range(2):
                    mm(C, po[:, 0:64], h1[:, j, :], w2v[:, j, :], j == 0, j == 1, [W2V, H1], [PO])
                cpy(C, "dve", Vca[0:127, g, 0:64], po[0:127, 0:64], [PO], [VCA])
    for g in range(2):
        P.op("dve", lambda e, g=g: e.memset(Vca[0:127, g, 64:65], 1.0), reads=[], writes=[VCA])
        cpy(C, "dve", Vca[:, g, 65:97], K.overlap[:, :], [CONST], [VCA])
    if stop == "cmp":
        tap(C, "x_KcTc", KcTc[:], [64, 2, 128], [KCTC])
        tap(C, "x_Vca", Vca[:], [128, 2, 97], [VCA])
        return

    P.barrier()
    phs1.close()
    NE = 4
    et = [sb(f"e{i}", [128, 512], BF16) for i in range(NE)]
    ET = [Buf(f"e{i}") for i in range(NE)]
    pt = [sb(f"p{i}", [128, 512], BF16) for i in range(NE)]
    PT_ = [Buf(f"p{i}") for i in range(NE)]
    selm4 = [sb(f"selm{i}", [128, 16, 128], BF16) for i in range(4)]
    SELM4 = [Buf(f"selm{i}") for i in range(4)]
    oB = [sb(f"oB{i}", [128, 512], F32) for i in range(2)]
    OB = [Buf(f"oB{i}") for i in range(2)]
    oBb = [sb(f"oBb{i}", [128, 512], BF16) for i in range(2)]
    OBB = [Buf(f"oBb{i}") for i in range(2)]
    rden = [sb(f"rden{i}", [128, 4], F32) for i in range(2)]
    RDEN = [Buf(f"rden{i}") for i in range(2)]
    fac = [sb(f"fac{i}", [128, 4], F32) for i in range(2)]
    FAC = [Buf(f"fac{i}") for i in range(2)]
    obr = [sb(f"obr{i}", [128, 4, 64], F32) for i in range(2)]
    OBR = [Buf(f"obr{i}") for i in range(2)]
    impt = sb("impt", [128, 4, 32], F32)
    imp = sb("imp", [128, 32], F32)
    imp2 = sb("imp2", [128, 32], F32)
    mx8 = sb("mx8", [128, 8], F32)
    thr = sb("thr", [128, 1], F32)
    bmf = sb("bmf", [128, 32], BF16)
    bmT = sb("bmT", [32, 128], BF16)
    IMPT, IMP, IMP2, MX8, THR, BMF, BMT = (Buf(n) for n in ("impt", "imp", "imp2", "mx8", "thr", "bmf", "bmT"))
    cnt = {"e": 0, "ev": 0, "m": 0}

    def gate_view(t, g, br):
        v = gts[:, t, g * 12:(g + 1) * 12].rearrange("p (b br) -> p b br", b=4)
        return v[:, :, br]

    def qk_exp(kT_, KB, g, kt, qt):
        ps_, PS_ = getps(C)
        ksl = slice(kt * 128, (kt + 1) * 128)
        qsl = slice(qt * 128, (qt + 1) * 128)
        mm(C, ps_[:, :], kT_(ksl), QT[:, 4 * g:4 * g + 4, qsl], True, True, KB + QTB[4 * g:4 * g + 4], [PS_])
        i = cnt["e"] % NE
        cnt["e"] += 1
        actf(C, et[i][:, :], ps_[:, :], AF.Exp, [PS_], [ET[i]])
        return et[i], ET[i]

    def masked(e_, E_, mask_ap, MB):
        i = cnt["m"] % NE
        cnt["m"] += 1
        eng = "dve"
        tt(C, eng, pt[i][:].rearrange("p (b q) -> p b q", b=4), e_[:].rearrange("p (b q) -> p b q", b=4),
           bc_mid(mask_ap, 4), ALU.mult, [E_] + MB, [PT_[i]])
        return pt[i], PT_[i]

    def evac(po, PO, width, t, g, br, first):
        s = t % 2
        i = cnt["ev"] % 2
        cnt["ev"] += 1
        pov = po[:, 0:4 * width].rearrange("p (b w) -> p b w", b=4)
        ts(C, "dve", rden[i][:, :], pov[:, :, 64], 1e-30, ALU.add, [PO], [RDEN[i]])
        P.op("dve", lambda e: e.reciprocal(rden[i][:, :], rden[i][:, :]), reads=[RDEN[i]], writes=[RDEN[i]])
        tt(C, "dve", fac[i][:, :], rden[i][:, :], gate_view(t, g, br), ALU.mult, [RDEN[i], GTS], [FAC[i]])
        ov = oB[s][:, g * 256:(g + 1) * 256].rearrange("p (b d) -> p b d", b=4)
        if first:
            tt(C, "dve", ov, pov[:, :, 0:64], bc_last(fac[i][:, :], 64), ALU.mult, [PO, FAC[i]], [OB[s]])
        else:
            tt(C, "dve", obr[i][:], pov[:, :, 0:64], bc_last(fac[i][:, :], 64), ALU.mult, [PO, FAC[i]], [OBR[i]])
            tt(C, "dve", ov, ov, obr[i][:], ALU.add, [OB[s], OBR[i]], [OB[s]])
        return i

    nqt = NT if stop == "" else int(os.environ.get("NSA_NQT", "16"))
    DEPTH = int(os.environ.get("NSA_DEPTH", "4"))
    blocks = []

    def add_cmp(qt, g):
        st_ = {}
        qsl = slice(qt * 128, (qt + 1) * 128)
        selm = selm4[(qt % 2) * 2:(qt % 2) * 2 + 2]
        SELM = SELM4[(qt % 2) * 2:(qt % 2) * 2 + 2]

        def front():
            e_, E_ = qk_exp(lambda ksl: KcTc[:, g, :], [KCTC], g, 0, qt)
            st_["p"] = masked(e_, E_, K.cmpmask[:, qsl], [CONST])

        def back():
            p_, P_ = st_["p"]
            po, PO = getps(C)
            for b in range(4):
                mm(C, po[:, b * 97:(b + 1) * 97], p_[:, b * 128:(b + 1) * 128], Vca[:, g, :], True, True, [P_, VCA], [PO])
            ri = evac(po, PO, 97, qt, g, 0, True)
            if qt < 8:
                return
            pov = po[:, 0:388].rearrange("p (b w) -> p b w", b=4)
            tt(C, "dve", impt[:], pov[:, :, 65:97], bc_last(rden[ri][:, :], 32), ALU.mult, [PO, RDEN[ri]], [IMPT])
            P.op("dve", lambda e: e.tensor_reduce(imp[:, :], impt[:].rearrange("p b j -> p j b"), AX.X, ALU.add),
                 reads=[IMPT], writes=[IMP])
            tt(C, "dve", imp[:, :], imp[:, :], K.forced[:, qt - 8, :], ALU.add, [IMP, CONST], [IMP])
            P.op("dve", lambda e: e.max(mx8[:, :], imp[:, :]), reads=[IMP], writes=[MX8])
            P.op("dve", lambda e: e.match_replace(imp2[:, :], mx8[:, :], imp[:, :], -3.0e38), reads=[IMP, MX8], writes=[IMP2])
            P.op("dve", lambda e: e.max(mx8[:, :], imp2[:, :]), reads=[IMP2], writes=[MX8])
            P.op("dve", lambda e: e.tensor_reduce(thr[:, :], mx8[:, :], AX.X, ALU.min), reads=[MX8], writes=[THR])
            ts(C, "dve", bmf[:, :], imp[:, :], thr[:, 0:1], ALU.is_ge, [IMP, THR], [BMF])
            pbt, PBT = getps(C)
            pbtv = pbt.bitcast(BF16)
            P.op("pe", lambda e, o=pbtv[0:32, 0:128]: e.transpose(o, bmf[:, :], K.ident_b[:, :]), reads=[BMF, CONST], writes=[PBT])
            cpy(C, "dve", bmT[:, :], pbtv[0:32, 0:128], [PBT], [BMT])
            for k4 in range(0, qt + 1, 4):
                pe_, PE_ = getps(C)
                nk = min(4, qt + 1 - k4)
                for j in range(nk):
                    mm(C, pe_[:, j * 128:(j + 1) * 128], K.expand[0:32, k4 + j, :], bmT[0:32, :], True, True, [CONST, BMT], [PE_])
                if k4 + nk - 1 == qt:
                    if nk > 1:
                        cpy(C, "act", selm[g][:, k4:k4 + nk - 1, :], pe_[:, 0:(nk - 1) * 128].rearrange("p (k q) -> p k q", q=128),
                            [PE_], [SELM[g]])
                    tt(C, "dve", selm[g][:, qt, :], pe_[:, (nk - 1) * 128:nk * 128], K.causal[:, :], ALU.mult, [PE_, CONST], [SELM[g]])
                else:
                    cpy(C, "act", selm[g][:, k4:k4 + nk, :], pe_[:, 0:nk * 128].rearrange("p (k q) -> p k q", q=128), [PE_], [SELM[g]])

        blocks.append((front, back))

    def add_branch(qt, g, br):
        acc_ = {}
        selm = selm4[(qt % 2) * 2:(qt % 2) * 2 + 2]
        SELM = SELM4[(qt % 2) * 2:(qt % 2) * 2 + 2]
        if br == 1:
            kts = list(range(qt + 1))
        else:
            kts = list(range(max(0, qt - 4), qt + 1))
        for kt in kts:
            st_ = {}

            def front(kt=kt, st_=st_):
                if br == 1:
                    e_, E_ = qk_exp(lambda ksl: KsT[:, g, ksl], [KST[g]], g, kt, qt)
                    if qt >= 8:
                        st_["p"] = masked(e_, E_, selm[g][:, kt, :], [SELM[g]])
                    elif kt == qt:
                        st_["p"] = masked(e_, E_, K.causal[:, :], [CONST])
                    else:
                        st_["p"] = (e_, E_)
                else:
                    e_, E_ = qk_exp(lambda ksl: KwT[:, g, ksl], [KWT[g]], g, kt, qt)
                    if kt == qt:
                        st_["p"] = masked(e_, E_, K.causal[:, :], [CONST])
                    elif kt == qt - 4:
                        st_["p"] = masked(e_, E_, K.anti[:, :], [CONST])
                    else:
                        st_["p"] = (e_, E_)

            def back(kt=kt, st_=st_):
                p_, P_ = st_["p"]
                if kt == kts[0]:
                    acc_["po"] = reserve_ps(C)
                poi, po, PO = acc_["po"]
                Vt, VB_ = (Vs, VS) if br == 1 else (Vw, VW)
                for b in range(4):
                    mm2(C, po[:, b * 65:(b + 1) * 65], p_[:, b * 128:(b + 1) * 128], Vt[:, kt, g, :],
                        (kt == kts[0] and b == 0), kt == kts[-1], [P_, VB_], [PO])
                if kt == kts[-1]:
                    evac(po, PO, 65, qt, g, br, False)
                    C.ps_reserved.discard(poi)

            blocks.append((front, back))

    def add_finish(qt):
        s = qt % 2
        qsl = slice(qt * 128, (qt + 1) * 128)

        def front():
            pass

        def back():
            cpy(C, "act", oBb[s][:, :], oB[s][:, :], [OB[s]], [OBB[s]])
            if qt in (0, 1, 3, 7, 9):
                tap(C, f"x_oB{qt}", oB[s][:, :], [128, 512], [OB[s]])
            ptr, PTR = getps(C)
            ptrv = ptr.bitcast(BF16)
            for c in range(4):
                P.op("pe", lambda e, o=ptrv[:, c * 128:(c + 1) * 128], i=oBb[s][:, c * 128:(c + 1) * 128]: e.transpose(o, i, K.ident_b[:, :]),
                     reads=[OBB[s], CONST], writes=[PTR])
            cpy(C, "act", C.oBT[:, :, qsl], ptrv[:, 0:512].rearrange("p (c q) -> p c q", c=4), [PTR], [C.OBT[qt]])

        blocks.append((front, back))

    add_cmp(0, 0)
    add_cmp(0, 1)
    for qt in range(nqt):
        if qt + 1 < nqt:
            add_cmp(qt + 1, 0)
            add_cmp(qt + 1, 1)
        for g in range(2):
            add_branch(qt, g, 1)
            add_branch(qt, g, 2)
        add_finish(qt)
    nb = len(blocks)
    for i in range(nb + DEPTH):
        if i - DEPTH >= 0:
            blocks[i - DEPTH][1]()
        if i < nb:
            blocks[i][0]()
    tap(C, "oBT", C.oBT[:, 0, :], [128, S], C.OBT)


def layer_norm_tile(C, v, V, stats, STATS, mv, MV, gt, bt, out, OUT, mul_eng="dve"):
    P = C.P
    for n in range(2):
        P.op("dve", lambda e, n=n: e.bn_stats(stats[:, n, :], v[:, n * 512:(n + 1) * 512]), reads=[V], writes=[STATS])
    P.op("dve", lambda e: e.bn_aggr(mv[:, 0:2], stats[:].rearrange("p n s -> p (n s)")), reads=[STATS], writes=[MV])
    actf(C, mv[:, 2:3], mv[:, 1:2], AF.Ln, [MV], [MV], bias=LN_EPS)
    actf(C, mv[:, 2:3], mv[:, 2:3], AF.Exp, [MV], [MV], scale=-0.5)
    stt(C, "dve", mv[:, 3:4], mv[:, 0:1], -1.0, mv[:, 2:3], ALU.mult, ALU.mult, [MV], [MV])
    actf(C, v[:, :], v[:, :], AF.Identity, [V, MV], [V], bias=mv[:, 3:4], scale=mv[:, 2:3])
    tt(C, mul_eng, v[:, :], v[:, :], gt[:, :], ALU.mult, [V, C.CONST], [V])
    tt(C, mul_eng, out[:, :], v[:, :], bt[:, :], ALU.add, [V, C.CONST], [OUT])


def phase_mix(C, ph):
    import os
    nc, P, dr, K = C.nc, C.P, C.dr, C.K
    CONST = C.CONST

    def sb(name, shape, dtype):
        return ph.enter_context(nc.sbuf_tensor("m_" + name, list(shape), dtype))

    ln1g = cload(C, sb, "ln1g", "p_ln1g", [128, D])
    ln1b = cload(C, sb, "ln1b", "p_ln1b", [128, D])
    wA = sb("wA", [128, 4, D], BF16)
    wB = sb("wB", [128, 4, D], BF16)
    wo = sb("wo", [128, 8, D], BF16)
    WA, WB, WO = Buf("wA"), Buf("wB"), Buf("wo")
    load_w(C, wA[:], "w_branch_gdn", 0, D, [WA], nrow_chunks=4)
    load_w(C, wB[:], "w_branch_nsa", 0, D, [WB], nrow_chunks=4)
    wg, WG, tlg = C.wst, C.WST, C.wst_tl
    mixT = [sb(f"mixT{i}", [128, 8, 512], BF16) for i in range(2)]
    MIXT = [Buf(f"mixT{i}") for i in range(2)]
    NB = 2
    th = [sb(f"th{i}", [128, 2, 512], F32) for i in range(NB)]
    TH = [Buf(f"th{i}") for i in range(NB)]
    m1 = [sb(f"m1{i}", [128, 512], F32) for i in range(NB)]
    M1 = [Buf(f"m1{i}") for i in range(NB)]
    m2 = [sb(f"m2{i}", [128, 512], F32) for i in range(NB)]
    M2 = [Buf(f"m2{i}") for i in range(NB)]
    xt = [sb(f"xt{i}", [128, D], F32) for i in range(2)]
    XTl = [Buf(f"xt{i}") for i in range(2)]
    tlx = [P.new_dma_tl(f"mxt{i}") for i in range(2)]
    vt = [sb(f"vt{i}", [128, D], F32) for i in range(2)]
    VT = [Buf(f"vt{i}") for i in range(2)]
    ht, HT = vt, VT
    hb = [sb(f"hb{i}", [128, D], BF16) for i in range(2)]
    HB = [Buf(f"hb{i}") for i in range(2)]
    stats = [sb(f"stats{i}", [128, 2, 6], F32) for i in range(2)]
    STATS = [Buf(f"stats{i}") for i in range(2)]
    mv = [sb(f"mv{i}", [128, 4], F32) for i in range(2)]
    MV = [Buf(f"mv{i}") for i in range(2)]
    C.HSCR = [Buf(f"hscr{t}") for t in range(NT)]
    kc = {"k": 0}

    def gates(tb):
        ms = tb % 2
        tsl = slice(tb * 512, (tb + 1) * 512)
        XTB = C.XT[tb * 4:(tb + 1) * 4]
        for j in range(8):
            k = kc["k"]
            ws = C.wk % 3
            C.wk += 1
            bs = k % NB
            kc["k"] += 1
            P.dma("pool", wg[ws][:], dr["pk_mg"][j], writes=[WG[ws]], tl=tlg[ws])
            pga, PGA = getps(C)
            pgb, PGB = getps(C)
            for c in range(8):
                mm(C, pga[:, :], wg[ws][:, c, 0:128], C.xT[:, c, tsl], c == 0, c == 7, [WG[ws]] + XTB, [PGA])
            for c in range(8):
                mm(C, pgb[:, :], wg[ws][:, c, 128:256], C.xT[:, c, tsl], c == 0, c == 7, [WG[ws]] + XTB, [PGB])
            actf(C, th[bs][:, 0, :], pga[:, :], AF.Tanh, [PGA], [TH[bs]], scale=0.5)
            actf(C, th[bs][:, 1, :], pgb[:, :], AF.Tanh, [PGB], [TH[bs]], scale=0.5)
            pa, PA = getps(C)
            pbB, PBB_ = getps(C)
            for c in range(4):
                mm(C, pa[:, :], wA[:, c, j * 128:(j + 1) * 128], C.oAT[:, c, tsl], c == 0, c == 3,
                   [WA] + C.OAT[tb * 4:(tb + 1) * 4], [PA])
            for c in range(4):
                mm(C, pbB[:, :], wB[:, c, j * 128:(j + 1) * 128], C.oBT[:, c, tsl], c == 0, c == 3,
                   [WB] + C.OBT[tb * 4:(tb + 1) * 4], [PBB_])
            stt(C, "dve", m1[bs][:, :], th[bs][:, 0, :], 1.0, pa[:, :], ALU.add, ALU.mult, [TH[bs], PA], [M1[bs]])
            stt(C, "dve", m2[bs][:, :], th[bs][:, 1, :], 1.0, pbB[:, :], ALU.add, ALU.mult, [TH[bs], PBB_], [M2[bs]])
            tt(C, "dve", mixT[ms][:, j, :], m1[bs][:, :], m2[bs][:, :], ALU.add, [M1[bs], M2[bs]], [MIXT[ms]])
            yield

    def epi(tb):
        ms = tb % 2
        for t4 in range(4):
            t = tb * 4 + t4
            s2 = t % 2
            P.dma("sp", xt[s2][:, :], dr["x"][t * 128:(t + 1) * 128, :], writes=[XTl[s2]], tl=tlx[s2])
            for n in range(2):
                py, PY = getps(C)
                for j in range(8):
                    mm(C, py[:, :], mixT[ms][:, j, t4 * 128:(t4 + 1) * 128], wo[:, j, n * 512:(n + 1) * 512], j == 0, j == 7,
                       [MIXT[ms], WO], [PY])
                stt(C, "dve", vt[s2][:, n * 512:(n + 1) * 512], xt[s2][:, n * 512:(n + 1) * 512], DN_ALPHA, py[:, :],
                    ALU.mult, ALU.add, [XTl[s2], PY], [VT[s2]])
            layer_norm_tile(C, vt[s2], VT[s2], stats[s2], STATS[s2], mv[s2], MV[s2], ln1g, ln1b, vt[s2], VT[s2])
            P.dma("sp", C.hscr[t * 128:(t + 1) * 128, :], ht[s2][:, :], reads=[HT[s2]], writes=[C.HSCR[t]])
            cpy(C, "act", hb[s2][:, :], ht[s2][:, :], [HT[s2]], [HB[s2]])
            ptr, PTR = getps(C)
            ptrv = ptr.bitcast(BF16)
            for c in range(8):
                P.op("pe", lambda e, o=ptrv[:, c * 128:(c + 1) * 128], i=hb[s2][:, c * 128:(c + 1) * 128]: e.transpose(o, i, K.ident_b[:, :]),
                     reads=[HB[s2], CONST], writes=[PTR])
            cpy(C, "act", C.xT[:, :, t * 128:(t + 1) * 128], ptrv[:, :].rearrange("p (c q) -> p c q", c=8), [PTR], [C.XT[t]])
            yield

    for j_, _ in enumerate(gates(0)):
        if j_ == 1:
            load_w(C, wo[:], "w_out", 0, D, [WO])
            actf(C, wo[:].rearrange("p c n -> p (c n)"), wo[:].rearrange("p c n -> p (c n)"), AF.Copy, [WO], [WO], scale=0.5)
    for tb in range(4):
        A = gates(tb + 1) if tb + 1 < 4 else iter(())
        B = epi(tb)
        a_done = b_done = False
        while not (a_done and b_done):
            for _ in range(2):
                if not a_done:
                    try:
                        next(A)
                    except StopIteration:
                        a_done = True
            if not b_done:
                try:
                    next(B)
                except StopIteration:
                    b_done = True


def phase_ffn(C, ph):
    import os
    nc, P, dr, K = C.nc, C.P, C.dr, C.K
    CONST = C.CONST
    hT, HTB = C.xT, C.XT

    def sb(name, shape, dtype):
        return ph.enter_context(nc.sbuf_tensor("f_" + name, list(shape), dtype))

    ln2g = cload(C, sb, "ln2g", "p_ln2g", [128, D])
    ln2b = cload(C, sb, "ln2b", "p_ln2b", [128, D])
    fconv = cload(C, sb, "fconv", "p_fconv", [128, 44, 3])
    wd = sb("wd", [128, 22, D], BF16)
    WD = Buf("wd")
    QW = 512
    aT = [sb(f"aT{i}", [128, 22, QW], BF16) for i in range(2)]
    AT = [[Buf(f"aT{q}_{i}") for i in range(22)] for q in range(2)]
    wu, WU, tlu = C.wst, C.WST, C.wst_tl
    raw = [[sb(f"raw{w}{i}", [128, QW + 2], F32) for i in range(2)] for w in range(2)]
    RAW = [[Buf(f"raw{w}{i}") for i in range(2)] for w in range(2)]
    acc = [[sb(f"acc{w}{i}", [128, QW], F32) for i in range(2)] for w in range(2)]
    ACC = [[Buf(f"acc{w}{i}") for i in range(2)] for w in range(2)]
    halo = sb("halo", [128, 44, 2], F32)
    HALO = [Buf(f"halo{c}") for c in range(44)]
    hres = [sb(f"hres{i}", [128, D], F32) for i in range(2)]
    HRES = [Buf(f"hres{i}") for i in range(2)]
    tlh = [P.new_dma_tl(f"fhr{i}") for i in range(2)]
    vt = [sb(f"vt{i}", [128, D], F32) for i in range(2)]
    VT = [Buf(f"vt{i}") for i in range(2)]
    stats = [sb(f"stats{i}", [128, 2, 6], F32) for i in range(2)]
    STATS = [Buf(f"stats{i}") for i in range(2)]
    mv = [sb(f"mv{i}", [128, 4], F32) for i in range(2)]
    MV = [Buf(f"mv{i}") for i in range(2)]
    OUTB = [Buf(f"out{t}") for t in range(NT)]
    kc = {"k": 0}

    def d1(q):
        T0 = q * QW
        qs = q % 2
        for i in range(22):
            k = kc["k"]
            ws = C.wk % 3
            C.wk += 1
            rs = k % 2
            kc["k"] += 1
            P.dma("pool", wu[ws][:], dr["pk_up"][i], writes=[WU[ws]], tl=tlu[ws])
            for w in range(2):
                ck = i + 22 * w
                if q == 0:
                    P.op("dve", lambda e, o=raw[w][rs][:, 0:2]: e.memset(o, 0.0), writes=[RAW[w][rs]])
                else:
                    cpy(C, "dve", raw[w][rs][:, 0:2], halo[:, ck, :], [HALO[ck]], [RAW[w][rs]])
                pu, PU = getps(C)
                for c in range(8):
                    mm(C, pu[:, :], wu[ws][:, c, w * 128:(w + 1) * 128], hT[:, c, T0:T0 + QW],
                       c == 0, c == 7, [WU[ws]] + HTB[T0 // 128:T0 // 128 + 4], [PU])
                cpy(C, "act", raw[w][rs][:, 2:2 + QW], pu[:, :], [PU], [RAW[w][rs]])
                if q < 3:
                    cpy(C, "dve", halo[:, ck, :], raw[w][rs][:, QW:QW + 2], [RAW[w][rs]], [HALO[ck]])
                if FFN_TAP_ACT:
                    actf(C, acc[w][rs][:, :], raw[w][rs][:, 2:QW + 2], AF.Copy, [RAW[w][rs], CONST], [ACC[w][rs]], scale=fconv[:, ck, 2:3])
                else:
                    ts(C, "dve", acc[w][rs][:, :], raw[w][rs][:, 2:QW + 2], fconv[:, ck, 2:3], ALU.mult, [RAW[w][rs], CONST], [ACC[w][rs]])
                for j in (1, 0):
                    stt(C, "dve", acc[w][rs][:, :], raw[w][rs][:, j:QW + j], fconv[:, ck, j:j + 1], acc[w][rs][:, :],
                        ALU.mult, ALU.add, [RAW[w][rs], ACC[w][rs], CONST], [ACC[w][rs]])
            actf(C, acc[0][rs][:, :], acc[0][rs][:, :], AF.Silu, [ACC[0][rs]], [ACC[0][rs]])
            tt(C, "dve", aT[qs][:, i, :], acc[0][rs][:, :], acc[1][rs][:, :], ALU.mult, [ACC[0][rs], ACC[1][rs]], [AT[qs][i]])
            if q == 0 and 3 <= i < 14:
                i2 = (i - 3) * 2
                src = dr["w_down"][i2 * 128:(i2 + 2) * 128, :].rearrange("(c p) n -> p c n", p=128)
                P.dma("pool", wd[:, i2:i2 + 2, :], src, writes=[WD])
            yield

    def d2(q):
        qs = q % 2
        for t4 in range(4):
            t = q * 4 + t4
            s2 = t % 2
            P.dma("sp", hres[s2][:, :], C.hscr[t * 128:(t + 1) * 128, :], reads=[C.HSCR[t]], writes=[HRES[s2]], tl=tlh[s2])
            for n in range(2):
                pf, PF = getps(C)
                for i in range(22):
                    mm(C, pf[:, :], aT[qs][:, i, t4 * 128:(t4 + 1) * 128], wd[:, i, n * 512:(n + 1) * 512], i == 0, i == 21,
                       [AT[qs][i], WD], [PF])
                stt(C, "dve", vt[s2][:, n * 512:(n + 1) * 512], hres[s2][:, n * 512:(n + 1) * 512], DN_ALPHA, pf[:, :],
                    ALU.mult, ALU.add, [HRES[s2], PF], [VT[s2]])
            layer_norm_tile(C, vt[s2], VT[s2], stats[s2], STATS[s2], mv[s2], MV[s2], ln2g, ln2b, vt[s2], VT[s2])
            P.dma("sp", C.out_d[t * 128:(t + 1) * 128, :], vt[s2][:, :], reads=[VT[s2]], writes=[OUTB[t]])
            C.final_bufs.append(OUTB[t])
            yield

    for _ in d1(0):
        pass
    for q in range(4):
        A = d1(q + 1) if q + 1 < 4 else iter(())
        B = d2(q)
        a_done = b_done = False
        while not (a_done and b_done):
            for _ in range(FFN_RATIO):
                if not a_done:
                    try:
                        next(A)
                    except StopIteration:
                        a_done = True
            if not b_done:
                try:
                    next(B)
                except StopIteration:
                    b_done = True


_CACHE = {}


def kernel(**inputs):
    inp = {k: np.asarray(v) for k, v in inputs.items()}
    if "nc" not in _CACHE:
        _CACHE["nc"] = build()[0]
    nc = _CACHE["nc"]
    base = {k: np.ascontiguousarray(inp[k][0], dtype=np.float32) for k in WEIGHT_SHAPES}
    base.update(host_consts())
    base.update(host_params(inp))
    base.update(host_packed(inp))
    n = inp["x"].shape[0]
    in_maps = [dict(base, x=np.ascontiguousarray(inp["x"][b], dtype=np.float32)) for b in range(n)]
    res = run_bass_kernel_spmd(nc, in_maps, core_ids=list(range(n)))
    return np.stack([np.asarray(r["out"], dtype=np.float32) for r in res.results], 0)
```

```python
import math
from contextlib import ExitStack
import numpy as np
import concourse.bass as bass
import concourse.mybir as mybir
from concourse.bass_utils import run_bass_kernel_spmd

F32 = mybir.dt.float32
BF16 = mybir.dt.bfloat16
AF = mybir.ActivationFunctionType
ALU = mybir.AluOpType
AX = mybir.AxisListType

S = 2048
D = 1024
NT = 16
D_IN = 5408
D_FF = 2816
DN_ALPHA = 2.0 ** 0.25
LN_EPS = 1e-5
RMS_EPS = 1e-6
NEG = -30000.0
import os as _os
FFN_RATIO = int(_os.environ.get("FFN_RATIO", "8"))
PE_FIX = float(_os.environ.get("PE_FIX", "45"))
PE_COL = float(_os.environ.get("PE_COL", "0.45"))
FFN_TAP_ACT = int(_os.environ.get("FFN_TAP_ACT", "1"))

C_GQ, C_GK, C_GV, C_GZ, C_GB, C_GA = 0, 512, 1024, 1536, 2048, 2052
C_NQ, C_NKC, C_NVC, C_NKS, C_NVS, C_NKW, C_NVW, C_NG, C_MG = 2056, 2568, 2696, 2824, 2952, 3080, 3208, 3336, 3360

EPOCH = 6000


class Timeline:
    def __init__(self, prog, name, step):
        self.prog = prog
        self.name = name
        self.step = step
        self.count = 0
        self.sems = []

    def sem_for(self, idx):
        ep = (idx - 1) // EPOCH
        while len(self.sems) <= ep:
            self.sems.append(self.prog.new_sem(f"{self.name}_{len(self.sems)}"))
        return self.sems[ep], ((idx - 1) % EPOCH + 1) * self.step

    def next(self):
        self.count += 1
        return self.count


class Buf:
    __slots__ = ("name", "last_write", "reads", "excl", "also", "persist")

    def __init__(self, name, excl=False, persist=False):
        self.name = name
        self.persist = persist
        self.last_write = None
        self.reads = []
        self.excl = excl
        self.also = None


class Op:
    __slots__ = ("idx", "eng", "fn", "tl", "deps", "epoch", "dur", "busy", "pos", "fin", "sched", "final", "bar", "prio")

    def __init__(self, idx, eng, fn, tl, deps, epoch, dur, busy, bar=True):
        self.idx, self.eng, self.fn, self.tl, self.deps = idx, eng, fn, tl, deps
        self.epoch, self.dur, self.busy = epoch, dur, busy
        self.bar = bar
        self.prio = 0
        self.pos = None
        self.fin = None
        self.sched = False
        self.final = False


def _free(ap):
    n = 1
    for d in list(ap.shape)[1:]:
        n *= int(d)
    return n


class Prog:
    ENGS = ("pe", "act", "dve", "pool", "sp")
    WINDOW = int(_os.environ.get("SCHED_WINDOW", "128"))
    SEM_LAT = float(_os.environ.get("SCHED_SEMLAT", "500"))

    def __init__(self, nc):
        self.nc = nc
        self.stack = ExitStack()
        self.tl = {e: Timeline(self, "c_" + e, 1) for e in self.ENGS}
        self.ops = []
        self.epoch = 0
        self.dma_pool = {}
        self.dma_rr = {}
        self.dma_last = {}
        self.all_dma_tl = []
        self.same_engine_sync = bool(int(_os.environ.get("SAME_ENG_SYNC", "1")))
        self.finals = []
        self.cur_prio = 0

    def new_sem(self, name):
        return self.stack.enter_context(self.nc.semaphore(name))

    def new_dma_tl(self, name):
        t = Timeline(self, "d_" + name, 16)
        self.all_dma_tl.append(t)
        return t

    def _deps(self, reads, writes):
        deps = set()
        for b in reads:
            if b.last_write is not None:
                deps.add(b.last_write)
            if b.excl:
                deps.update(b.reads)
            if b.also:
                deps.update(b.also)
        for b in writes:
            if b.last_write is not None:
                deps.add(b.last_write)
            deps.update(b.reads)
        return deps

    def _mark(self, op, reads, writes):
        for b in reads:
            if b.excl:
                b.last_write = op
                b.reads = []
            else:
                b.reads.append(op)
        for b in writes:
            b.last_write = op
            b.reads = []

    def op(self, eng, fn, reads=(), writes=(), cost=150.0):
        deps = self._deps(reads, writes)
        bar = not all(b.persist for b in list(reads) + list(writes))
        o = Op(len(self.ops), eng, fn, self.tl[eng], deps, self.epoch, cost, cost, bar)
        o.prio = self.cur_prio
        self.ops.append(o)
        self._mark(o, reads, writes)
        return o

    def dma(self, eng, out, in_, reads=(), writes=(), tl=None, **kw):
        if tl is None:
            if eng not in self.dma_pool:
                self.dma_pool[eng] = [self.new_dma_tl(f"{eng}{i}") for i in range(6)]
            pool = self.dma_pool[eng]
            i = self.dma_rr.get(eng, 0)
            self.dma_rr[eng] = (i + 1) % len(pool)
            tl = pool[i]
        deps = self._deps(reads, writes)
        if tl in self.dma_last:
            deps.add(self.dma_last[tl])
        nbytes = _free(out) * int(out.shape[0]) * 4
        dur = 2200.0 + nbytes / 150.0

        def fn(e, out=out, in_=in_, kw=kw):
            return e.dma_start(out=out, in_=in_, **kw)

        bar = not all(b.persist for b in list(reads) + list(writes))
        o = Op(len(self.ops), eng, fn, tl, deps, self.epoch, dur, 150.0 if eng == "sp" else 400.0, bar)
        self.ops.append(o)
        self.dma_last[tl] = o
        self._mark(o, reads, writes)
        return o

    def barrier(self):
        self.epoch += 1

    def final_wait(self, eng, bufs):
        deps = self._deps(bufs, bufs)
        self.finals.append((eng, deps))

    def schedule(self):
        pend = {e: [] for e in self.ENGS}
        for o in self.ops:
            pend[o.eng].append(o)
        head = {e: 0 for e in self.ENGS}
        tfree = {e: 0.0 for e in self.ENGS}
        order = {e: [] for e in self.ENGS}
        n_left = len(self.ops)
        ep_left = {}
        for o in self.ops:
            ep_left[o.epoch] = ep_left.get(o.epoch, 0) + 1
        ep_end = {-1: 0.0}
        cur_ep = 0
        ep_fin = 0.0
        while n_left:
            while ep_left.get(cur_ep, 0) == 0:
                ep_end[cur_ep] = ep_fin
                cur_ep += 1
            best = None
            for e in self.ENGS:
                lst = pend[e]
                h = head[e]
                while h < len(lst) and lst[h].sched:
                    h += 1
                head[e] = h
                cnt = 0
                i = h
                while i < len(lst) and cnt < self.WINDOW:
                    o = lst[i]
                    i += 1
                    if o.sched:
                        continue
                    cnt += 1
                    if o.epoch != cur_ep:
                        if o.bar or o.epoch < cur_ep:
                            continue
                        rdy = 0.0
                    else:
                        rdy = ep_end[cur_ep - 1] if o.bar else 0.0
                    ok = True
                    for d in o.deps:
                        if not d.sched:
                            ok = False
                            break
                        f = d.fin + (0.0 if d.eng == e and d.tl is self.tl[e] else self.SEM_LAT)
                        if f > rdy:
                            rdy = f
                    if not ok:
                        continue
                    st = rdy if rdy > tfree[e] else tfree[e]
                    key = (st, -o.prio, o.idx)
                    if best is None or key < best[0]:
                        best = (key, e, o, st)
            if best is None:
                raise RuntimeError("scheduler: no candidate")
            _, e, o, st = best
            o.sched = True
            o.fin = st + o.dur
            tfree[e] = st + o.busy
            order[e].append(o)
            n_left -= 1
            ep_left[o.epoch] -= 1
            if o.fin > ep_fin:
                ep_fin = o.fin
        self.est_ns = ep_fin
        return order

    def emit(self):
        nc = self.nc
        order = self.schedule()
        for e in self.ENGS:
            for o in order[e]:
                o.pos = o.tl.next()
        streams = {}
        for e in self.ENGS:
            known = {}
            out = []
            mytl = self.tl[e]
            last_ep = 0
            done_pos = {}
            for o in order[e]:
                waits = []
                if o.bar and o.epoch > last_ep:
                    for tl, p in self._epoch_max(o.epoch).items():
                        if tl is mytl:
                            continue
                        if known.get(tl, 0) < p:
                            known[tl] = p
                            waits.append(tl.sem_for(p))
                    last_ep = o.epoch
                for d in o.deps:
                    if d.tl is mytl and (e in ("pe", "sp") or not self.same_engine_sync):
                        continue
                    if known.get(d.tl, 0) >= d.pos:
                        continue
                    known[d.tl] = d.pos
                    waits.append(d.tl.sem_for(d.pos))
                sem, _ = o.tl.sem_for(o.pos)
                out.append((waits, o.fn, sem, o.tl.step))
            streams[e] = out
        for (e, deps) in self.finals:
            waits = []
            best = {}
            for d in deps:
                if best.get(d.tl, 0) < d.pos:
                    best[d.tl] = d.pos
            for tl, p in best.items():
                waits.append(tl.sem_for(p))
            streams[e].append((waits, None, None, 0))
        self.streams = streams

        def replay(name):
            def body(e):
                for (waits, fn, sem, inc) in streams[name]:
                    for (s_, v) in waits:
                        e.wait_ge(s_, v)
                    if fn is not None:
                        fn(e).then_inc(sem, inc)
            return body

        with nc.Block() as block:
            block.tensor(replay("pe"))
            block.scalar(replay("act"))
            block.vector(replay("dve"))
            block.gpsimd(replay("pool"))
            block.sync(replay("sp"))

    def _epoch_max(self, epoch):
        if not hasattr(self, "_epmax"):
            self._epmax = {}
        if epoch not in self._epmax:
            m = {}
            for o in self.ops:
                if o.epoch < epoch and m.get(o.tl, 0) < o.pos:
                    m[o.tl] = o.pos
            self._epmax[epoch] = m
        return self._epmax[epoch]


def host_consts():
    c = {}
    i = np.arange(128)
    same = (i[:, None] // 64) == (i[None, :] // 64)
    c["c_ident"] = np.eye(128, dtype=np.float32)
    c["c_tri"] = ((i[:, None] <= i[None, :]) & same).astype(np.float32)
    c["c_blk"] = same.astype(np.float32)
    c["c_nega"] = np.where((i[:, None] > i[None, :]) & same, 0.0, NEG).astype(np.float32)
    c["c_negq"] = np.where((i[None, :] >= i[:, None]) & same, 0.0, NEG).astype(np.float32)
    sel = np.zeros((4, 4, 128), np.float32)
    for h in range(4):
        sel[h, h, :] = 1.0
    c["c_sel4"] = sel
    sel12 = np.zeros((12, 4, 128), np.float32)
    for h in range(4):
        sel12[3 * h:3 * h + 3, h, :] = 1.0
    c["c_sel12"] = sel12
    sr = np.zeros((128, 2, 128), np.float32)
    sr[0, 0, :] = 1.0
    sr[64, 1, :] = 1.0
    c["c_selrow"] = sr
    n = np.arange(127)
    t = np.arange(S)
    c["c_cmpmask"] = np.concatenate([((n[:, None] * 16 + 31) <= t[None, :]).astype(np.float32),
                                     np.zeros((1, S), np.float32)], 0)
    starts = n * 16
    jb = np.arange(32) * 64
    ov = ((starts[:, None] < jb[None] + 64) & (starts[:, None] + 32 > jb[None])).astype(np.float32)
    c["c_overlap"] = np.concatenate([ov, np.zeros((1, 32), np.float32)], 0)
    c["c_causal"] = (i[:, None] <= i[None, :]).astype(np.float32)
    c["c_anti"] = (i[:, None] > i[None, :]).astype(np.float32)
    E = np.zeros((32, 16, 128), np.float32)
    for kt in range(16):
        for k in range(128):
            E[2 * kt + k // 64, kt, k] = 1.0
    c["c_expand"] = E
    fb = np.zeros((128, 8, 32), np.float32)
    for qi in range(8):
        qt = 8 + qi
        pos = qt * 128 + i
        cur = pos // 64
        jj = np.arange(32)
        causal = (jj[None] * 64) <= pos[:, None]
        b_ = np.where(causal, 0.0, -1e30)
        b_ = np.where(jj[None] == 0, 1e9, b_)
        b_ = np.where(jj[None] == cur[:, None], 2e9, b_)
        b_ = np.where(jj[None] == cur[:, None] - 1, 3e9, b_)
        fb[:, qi, :] = b_
    c["c_forced"] = fb
    return c


CONST_SHAPES = {k: v.shape for k, v in host_consts().items()}


def host_params(inp):
    p = {}
    f = np.float32
    p["p_gconv"] = np.ascontiguousarray(inp["gdn_conv_w"][0].reshape(4, 12, 128).transpose(2, 1, 0)).astype(f)
    p["p_alog"] = np.ascontiguousarray(np.broadcast_to(inp["gdn_a_log"][0][None, None, :], (128, NT, 4))).astype(f)
    p["p_dtb"] = np.ascontiguousarray(np.broadcast_to(inp["gdn_dt_bias"][0][None, None, :], (128, NT, 4))).astype(f)
    p["p_gnw"] = np.ascontiguousarray(np.broadcast_to(inp["gdn_norm_w"][0][None, None, :], (128, 4, 128))).astype(f)
    p["p_poskT"] = np.ascontiguousarray(np.concatenate([inp["cmp_pos_k"][0].T] * 2, 0)).astype(f)
    p["p_posvT"] = np.ascontiguousarray(np.concatenate([inp["cmp_pos_v"][0].T] * 2, 0)).astype(f)
    p["p_ln1g"] = np.ascontiguousarray(np.broadcast_to(inp["ln1_g"][0][None, :], (128, D))).astype(f)
    p["p_ln1b"] = np.ascontiguousarray(np.broadcast_to(inp["ln1_b"][0][None, :], (128, D))).astype(f)
    p["p_ln2g"] = np.ascontiguousarray(np.broadcast_to(inp["ln2_g"][0][None, :], (128, D))).astype(f)
    p["p_ln2b"] = np.ascontiguousarray(np.broadcast_to(inp["ln2_b"][0][None, :], (128, D))).astype(f)
    p["p_fconv"] = np.ascontiguousarray(inp["ffn_conv_w"][0].reshape(3, 44, 128).transpose(2, 1, 0)).astype(f)
    return p


def _pk(w, cols):
    sub = w[:, cols]
    return np.ascontiguousarray(sub.reshape(8, 128, sub.shape[1]).transpose(1, 0, 2))


def host_packed(inp):
    f = np.float32
    w_in = np.asarray(inp["w_in"][0], dtype=f)
    w_up = np.asarray(inp["w_up"][0], dtype=f)
    p = {}
    ar = np.arange
    p["pk_up"] = np.stack([_pk(w_up, np.concatenate([ar(i * 128, (i + 1) * 128), ar(D_FF + i * 128, D_FF + (i + 1) * 128)]))
                           for i in range(22)], 0)
    p["pk_mg"] = np.stack([_pk(w_in, np.concatenate([ar(C_MG + j * 128, C_MG + (j + 1) * 128),
                                                     ar(C_MG + 1024 + j * 128, C_MG + 1024 + (j + 1) * 128)]))
                           for j in range(8)], 0)
    p["pk_g"] = np.stack([_pk(w_in, ar(ck * 128, (ck + 1) * 128)) for ck in range(12)], 0)
    jobs = [ar(C_NQ + i * 128, C_NQ + (i + 1) * 128) for i in range(4)]
    jobs += [ar(C_NKS, C_NKS + 128), ar(C_NKW, C_NKW + 128), ar(C_NKC, C_NKC + 128), ar(C_NVC, C_NVC + 128)]
    p["pk_n128"] = np.stack([_pk(w_in, c) for c in jobs], 0)
    return p


PACKED_SHAPES = {"pk_up": (22, 128, 8, 256), "pk_mg": (8, 128, 8, 256), "pk_g": (12, 128, 8, 128),
                 "pk_n128": (8, 128, 8, 128)}

PARAM_SHAPES = {"p_gconv": (128, 12, 4), "p_alog": (128, NT, 4), "p_dtb": (128, NT, 4), "p_gnw": (128, 4, 128),
                "p_poskT": (128, 32), "p_posvT": (128, 32), "p_ln1g": (128, D), "p_ln1b": (128, D),
                "p_ln2g": (128, D), "p_ln2b": (128, D), "p_fconv": (128, 44, 3)}

WEIGHT_SHAPES = {"w_in": (D, D_IN), "cmp_w1_k": (2048, 256), "cmp_w2_k": (256, 64), "cmp_w1_v": (2048, 256),
                 "cmp_w2_v": (256, 64), "w_branch_gdn": (512, D), "w_branch_nsa": (512, D), "w_out": (D, D),
                 "w_up": (D, 2 * D_FF), "w_down": (D_FF, D)}


class Ctx:
    pass


def build(taps=(), phases=("gdn", "nsa", "mix", "ffn")):
    nc = bass.Bass("TRN2", target_bir_lowering=False)
    dr = {}
    dr["x"] = nc.dram_tensor("x", [S, D], F32, kind="ExternalInput").ap()
    for k, shp in list(WEIGHT_SHAPES.items()) + list(CONST_SHAPES.items()) + list(PARAM_SHAPES.items()) + list(PACKED_SHAPES.items()):
        dr[k] = nc.dram_tensor(k, list(shp), F32, kind="ExternalInput").ap()
    out_d = nc.dram_tensor("out", [S, D], F32, kind="ExternalOutput").ap()
    hscr = nc.dram_tensor("hscr", [S, D], F32).ap()
    P = Prog(nc)
    C = Ctx()
    C.nc, C.P, C.dr, C.out_d, C.hscr, C.taps = nc, P, dr, out_d, hscr, set(taps)
    C.tap_out = {}
    with P.stack:
        C.ps = [nc.alloc_psum_tensor(f"ps{i}", [128, 512], F32) for i in range(8)]
        C.PB = [Buf(f"ps{i}", excl=True, persist=True) for i in range(8)]
        C.ps_rr = 0
        C.ps_reserved = set()
        C.xT = nc.alloc_sbuf_tensor("xT", [128, 8, S], BF16)
        C.XT = [Buf(f"xT{t}", persist=True) for t in range(NT)]
        C.OAT = [Buf(f"oAT{t}", persist=True) for t in range(NT)]
        C.OBT = [Buf(f"oBT{t}", persist=True) for t in range(NT)]
        C.wst = [nc.alloc_sbuf_tensor(f"wst{i}", [128, 8, 256], BF16) for i in range(3)]
        C.WST = [Buf(f"wst{i}", persist=True) for i in range(3)]
        C.wst_tl = [P.new_dma_tl(f"wst{i}") for i in range(3)]
        C.wk = 0
        C.CONST = Buf("const")
        C.tl_const = {"sp": P.new_dma_tl("const_sp"), "pool": P.new_dma_tl("const_pool")}
        load_consts(C)
        with ExitStack() as ab:
            C.oAT = ab.enter_context(nc.sbuf_tensor("oAT", [128, 4, S], BF16))
            with ExitStack() as ph:
                phase_x(C, ph)
                if "gdn" in phases:
                    phase_gdn(C, ph)
            P.barrier()
            C.oBT = ab.enter_context(nc.sbuf_tensor("oBT", [128, 4, S], BF16))
            if "nsa" in phases:
                with ExitStack() as ph:
                    phase_nsa(C, ph)
                P.barrier()
            if "mix" in phases:
                with ExitStack() as ph:
                    phase_mix(C, ph)
                P.barrier()
        if "ffn" in phases:
            with ExitStack() as ph:
                phase_ffn(C, ph)
        P.final_wait("sp", C.final_bufs)
        P.emit()
    return nc, C


def getps(C):
    while True:
        i = C.ps_rr
        C.ps_rr = (i + 1) % 8
        if i not in C.ps_reserved:
            return C.ps[i], C.PB[i]


def reserve_ps(C):
    while True:
        i = C.ps_rr
        C.ps_rr = (i + 1) % 8
        if i not in C.ps_reserved:
            C.ps_reserved.add(i)
            return i, C.ps[i], C.PB[i]


def const_dma(C, eng, out, in_):
    P = C.P
    tl = C.tl_const[eng]
    nbytes = _free(out) * int(out.shape[0]) * 4

    def fn(e, out=out, in_=in_):
        return e.dma_start(out=out, in_=in_)

    o = Op(len(P.ops), eng, fn, tl, set(), P.epoch, 2200.0 + nbytes / 150.0, 150.0 if eng == "sp" else 400.0)
    P.ops.append(o)
    if C.CONST.also is None:
        C.CONST.also = []
    C.CONST.also.append(o)


def cload(C, nc_alloc, name, key, shape, dtype=F32):
    t = nc_alloc(name, list(shape), dtype)
    const_dma(C, "pool" if dtype != F32 else "sp", t[:], C.dr[key])
    return t


def load_consts(C):
    nc = C.nc
    K = Ctx()
    C.K = K

    def al(name, shape, dtype):
        return nc.alloc_sbuf_tensor(name, shape, dtype)

    K.ident_f = cload(C, al, "k_identf", "c_ident", [128, 128])
    K.ident_b = cload(C, al, "k_identb", "c_ident", [128, 128], BF16)
    K.ones_b = nc.alloc_sbuf_tensor("k_onesb", [128, 128], BF16)
    C.ONES = Buf("ones")
    C.P.op("dve", lambda e: e.memset(K.ones_b[:], 1.0), writes=[C.ONES])


def tap(C, name, ap, shape, rd):
    if name not in C.taps:
        return
    t = C.nc.dram_tensor("tap_" + name, list(shape), ap.dtype, kind="ExternalOutput").ap()
    b = Buf("tap_" + name)
    C.P.dma("sp", t, ap, reads=rd, writes=[b])
    C.final_bufs.append(b)
    C.tap_out[name] = "tap_" + name


def phase_x(C, phx):
    nc, P, dr, K = C.nc, C.P, C.dr, C.K
    C.final_bufs = []
    xb = [phx.enter_context(nc.sbuf_tensor(f"xb{i}", [128, D], BF16)) for i in range(2)]
    XB = [Buf(f"xb{i}") for i in range(2)]
    tls = [P.new_dma_tl(f"xb{i}") for i in range(2)]
    for tt in range(NT):
        s = tt % 2
        P.dma("pool", xb[s][:, :], dr["x"][tt * 128:(tt + 1) * 128, :], writes=[XB[s]], tl=tls[s])
        pb, PBb = getps(C)
        pbv = pb.bitcast(BF16)
        for c in range(8):
            P.op("pe", lambda e, o=pbv[:, c * 128:(c + 1) * 128], i=xb[s][:, c * 128:(c + 1) * 128]:
                 e.transpose(o, i, K.ident_b[:, :]), reads=[XB[s], C.CONST], writes=[PBb])
        o = C.xT[:, :, tt * 128:(tt + 1) * 128]
        i = pbv[:, :].rearrange("p (c t) -> p c t", c=8)
        if tt % 2 == 0:
            P.op("act", lambda e, o=o, i=i: e.copy(o, i), reads=[PBb], writes=[C.XT[tt]])
        else:
            P.op("dve", lambda e, o=o, i=i: e.tensor_copy(o, i), reads=[PBb], writes=[C.XT[tt]])


def mm(C, out, lhsT, rhs, start, stop, rd, wr):
    C.P.op("pe", lambda e: e.matmul(out, lhsT, rhs, start=start, stop=stop), reads=rd, writes=wr,
           cost=PE_FIX + PE_COL * _free(rhs))


def actf(C, out, in_, func, rd, wr, bias=None, scale=None):
    kw = {}
    if bias is not None:
        kw["bias"] = bias
    if scale is not None:
        kw["scale"] = scale
    C.P.op("act", lambda e: e.activation(out, in_, func, **kw), reads=rd, writes=wr, cost=220.0 + 0.75 * _free(out))


def cpy(C, eng, out, in_, rd, wr):
    if eng == "act":
        C.P.op("act", lambda e: e.copy(out, in_), reads=rd, writes=wr, cost=220.0 + 0.75 * _free(out))
    else:
        C.P.op(eng, lambda e: e.tensor_copy(out, in_), reads=rd, writes=wr, cost=(120.0 + 1.05 * _free(out)) * (6.0 if eng == "pool" else 1.0))


def tt(C, eng, out, in0, in1, op, rd, wr):
    C.P.op(eng, lambda e: e.tensor_tensor(out, in0, in1, op), reads=rd, writes=wr, cost=(120.0 + 1.1 * _free(out)) * (6.0 if eng == "pool" else 1.0))


def ts(C, eng, out, in0, s1, op0, rd, wr, s2=None, op1=None):
    if op1 is None:
        C.P.op(eng, lambda e: e.tensor_scalar(out, in0, s1, None, op0), reads=rd, writes=wr, cost=(120.0 + 1.0 * _free(out)) * (6.0 if eng == "pool" else 1.0))
    else:
        C.P.op(eng, lambda e: e.tensor_scalar(out, in0, s1, s2, op0, op1), reads=rd, writes=wr, cost=(120.0 + 1.0 * _free(out)) * (6.0 if eng == "pool" else 1.0))


def stt(C, eng, out, in0, scalar, in1, op0, op1, rd, wr):
    C.P.op(eng, lambda e: e.scalar_tensor_tensor(out, in0, scalar, in1, op0, op1), reads=rd, writes=wr, cost=120.0 + 1.4 * _free(out))


def load_w(C, dst, key, col0, ncols, wr, tl=None, row0=0, nrow_chunks=8):
    src = C.dr[key][row0:row0 + 128 * nrow_chunks, col0:col0 + ncols].rearrange("(c p) n -> p c n", p=128)
    C.P.dma("pool", dst, src, writes=wr, tl=tl)


def bc_last(ap, n):
    shp = list(ap.shape)
    return ap.unsqueeze(len(shp)).to_broadcast(shp + [n])


def bc_mid(ap, n):
    shp = list(ap.shape)
    return ap.unsqueeze(1).to_broadcast([shp[0], n] + shp[1:])


def phase_gdn(C, ph):
    nc, P, dr, K = C.nc, C.P, C.dr, C.K
    CONST = C.CONST

    def sb(name, shape, dtype):
        return ph.enter_context(nc.sbuf_tensor("g_" + name, list(shape), dtype))

    K.tri = cload(C, sb, "k_tri", "c_tri", [128, 128], BF16)
    K.blk = cload(C, sb, "k_blk", "c_blk", [128, 128], BF16)
    K.nega = cload(C, sb, "k_nega", "c_nega", [128, 128], BF16)
    K.negq = cload(C, sb, "k_negq", "c_negq", [128, 128], BF16)
    K.sel12 = cload(C, sb, "k_sel12", "c_sel12", [12, 4, 128], BF16)
    K.selrow = cload(C, sb, "k_selrow", "c_selrow", [128, 2, 128], BF16)
    K.gconv = cload(C, sb, "k_gconv", "p_gconv", [128, 12, 4])
    K.alog = cload(C, sb, "k_alog", "p_alog", [128, NT, 4])
    K.dtb = cload(C, sb, "k_dtb", "p_dtb", [128, NT, 4])
    K.gnw = cload(C, sb, "k_gnw", "p_gnw", [128, 4, 128])

    qT = sb("qT", [128, 4, S], BF16)
    kT = sb("kT", [128, 4, S], BF16)
    vT = sb("vT", [128, 4, S], BF16)
    QKV = {"q": [Buf(f"qT{h}") for h in range(4)], "k": [Buf(f"kT{h}") for h in range(4)],
           "v": [Buf(f"vT{h}") for h in range(4)]}
    qkvT = {"q": qT, "k": kT, "v": vT}

    wba = sb("wba", [128, 8, 8], BF16)
    WBA = Buf("wba")
    load_w(C, wba[:], "w_in", C_GB, 8, [WBA])
    ba = sb("ba", [128, NT, 8], F32)
    BA = Buf("ba")
    pb, PBb = getps(C)
    for t in range(NT):
        for c in range(8):
            mm(C, pb[:, t * 8:(t + 1) * 8], C.xT[:, c, t * 128:(t + 1) * 128], wba[:, c, :], c == 0, c == 7,
               [C.XT[t], WBA], [PBb])
    cpy(C, "dve", ba[:].rearrange("p t c -> p (t c)"), pb[:, 0:NT * 8], [PBb], [BA])

    import os
    stop = os.environ.get("GDN_STOP", "")
    if stop == "a0":
        return
    ss = sb("ss", [128, NT, 8], F32)
    SSb = Buf("ss")
    ph1 = ExitStack()
    sb_outer = sb

    def sb(name, shape, dtype):
        return ph1.enter_context(nc.sbuf_tensor("g_" + name, list(shape), dtype))

    wq = [t[:, :, 0:128] for t in C.wst]
    WQ = C.WST
    tlw = C.wst_tl
    raw = [sb(f"raw{i}", [128, S + 3], F32) for i in range(2)]
    RAW = [Buf(f"raw{i}") for i in range(2)]
    acc = [sb(f"acc{i}", [128, S], F32) for i in range(2)]
    ACC = [Buf(f"acc{i}") for i in range(2)]
    sq = [sb(f"sq{i}", [128, S], BF16) for i in range(2)]
    SQ = [Buf(f"sq{i}") for i in range(2)]
    for i in range(2):
        P.op("dve", lambda e, o=raw[i][:, 0:3]: e.memset(o, 0.0), writes=[RAW[i]])
    ss_i, ss_pb, SS_PB = reserve_ps(C)
    n_ss = 0
    for ck in range(12):
        which = "qkv"[ck // 4]
        h = ck % 4
        ws = C.wk % 3
        C.wk += 1
        P.dma("pool", wq[ws], dr["pk_g"][ck], writes=[WQ[ws]], tl=tlw[ws])
        rs = ck % 2
        for tb in range(4):
            pb, PBb = getps(C)
            for c in range(8):
                mm(C, pb[:, :], wq[ws][:, c, :], C.xT[:, c, tb * 512:(tb + 1) * 512], c == 0, c == 7,
                   [WQ[ws]] + C.XT[tb * 4:(tb + 1) * 4], [PBb])
            cpy(C, "act", raw[rs][:, 3 + tb * 512:3 + (tb + 1) * 512], pb[:, :], [PBb], [RAW[rs]])
        ceng = "dve"
        ts(C, ceng, acc[rs][:, :], raw[rs][:, 3:S + 3], K.gconv[:, ck, 3:4], ALU.mult, [RAW[rs], CONST], [ACC[rs]])
        for j in (2, 1, 0):
            stt(C, ceng, acc[rs][:, :], raw[rs][:, j:S + j], K.gconv[:, ck, j:j + 1], acc[rs][:, :], ALU.mult, ALU.add,
                [RAW[rs], ACC[rs], CONST], [ACC[rs]])
        dst = qkvT[which][:, h, :]
        actf(C, dst, acc[rs][:, :], AF.Silu, [ACC[rs]], [QKV[which][h]])
        if which in "qk":
            col = (0 if which == "q" else 4) + h
            actf(C, sq[rs][:, :], dst, AF.Square, [QKV[which][h]], [SQ[rs]])
            for t in range(NT):
                mm(C, ss_pb[:, t * 8 + col:t * 8 + col + 1], sq[rs][:, t * 128:(t + 1) * 128], K.ones_b[:, 0:1],
                   True, True, [SQ[rs], C.ONES], [SS_PB])
    cpy(C, "dve", ss[:].rearrange("p t c -> p (t c)"), ss_pb[:, 0:NT * 8], [SS_PB], [SSb])
    C.ps_reserved.discard(ss_i)
    P.barrier()
    ph1.close()
    sb = sb_outer
    tap(C, "qT", qT[:, 0, :], [128, S], QKV["q"])
    tap(C, "vT", vT[:, 1, :], [128, S], QKV["v"])

    if stop == "a1":
        return
    names = ["t1", "t2", "t3", "spa", "spb", "g", "lb", "gc", "gl", "lrk", "lrq", "biasA", "biasQ", "rowQ",
             "s_kbg", "s_kdec", "beta", "s_o", "ea", "ya", "yb"]
    st = {n: sb("st_" + n, [128, NT, 4], F32) for n in names}
    SB_ = {n: Buf("st_" + n) for n in names}

    def softplus(dst, y):
        actf(C, st["t1"][:], st[y][:], AF.Abs, [SB_[y]], [SB_["t1"]])
        actf(C, st["t2"][:], st["t1"][:], AF.Exp, [SB_["t1"]], [SB_["t2"]], scale=-1.0)
        actf(C, st["t3"][:], st["t2"][:], AF.Ln, [SB_["t2"]], [SB_["t3"]], bias=1.0)
        stt(C, "dve", st[dst][:], st[y][:], 0.0, st["t3"][:], ALU.max, ALU.add, [SB_[y], SB_["t3"]], [SB_[dst]])

    tt(C, "dve", st["ya"][:], ba[:, :, 4:8], K.dtb[:], ALU.add, [BA, CONST], [SB_["ya"]])
    softplus("spa", "ya")
    actf(C, st["ea"][:], K.alog[:], AF.Exp, [CONST], [SB_["ea"]])
    stt(C, "dve", st["g"][:], st["spa"][:], -1.0, st["ea"][:], ALU.mult, ALU.mult, [SB_["spa"], SB_["ea"]], [SB_["g"]])
    ts(C, "dve", st["yb"][:], ba[:, :, 0:4], -1.0, ALU.mult, [BA], [SB_["yb"]])
    softplus("spb", "yb")
    ts(C, "dve", st["lb"][:], st["spb"][:], -1.0, ALU.mult, [SB_["spb"]], [SB_["lb"]])
    lrt = sb("lrt", [128, NT, 8], F32)
    LRT = Buf("lrt")
    actf(C, lrt[:], ss[:], AF.Ln, [SSb], [LRT], bias=RMS_EPS)
    ts(C, "dve", st["lrq"][:], lrt[:, :, 0:4], -0.5, ALU.mult, [LRT], [SB_["lrq"]], s2=-0.5 * math.log(128.0), op1=ALU.add)
    ts(C, "dve", st["lrk"][:], lrt[:, :, 4:8], -0.5, ALU.mult, [LRT], [SB_["lrk"]])
    spl_r = sb("spl_r", [128, NT, 4], F32)
    SPLR = Buf("spl_r")

    def split3(name, src):
        x3 = sb("x3_" + name, [128, 3, NT, 4], BF16)
        X3 = Buf("x3_" + name)
        cur, CUR = st[src], SB_[src]
        for k in range(3):
            cpy(C, "dve", x3[:, k, :, :], cur[:], [CUR], [X3])
            if k < 2:
                tt(C, "dve", spl_r[:], cur[:], x3[:, k, :, :], ALU.subtract, [CUR, X3], [SPLR])
                cur, CUR = spl_r, SPLR
        return x3, X3

    g3, G3 = split3("g", "g")
    pb, PBb = getps(C)
    for t in range(NT):
        for k in range(3):
            mm(C, pb[:, t * 4:(t + 1) * 4], K.tri[:, :], g3[:, k, t, :], k == 0, k == 2, [CONST, G3], [PBb])
        for k in range(3):
            mm(C, pb[:, 64 + t * 4:64 + (t + 1) * 4], K.blk[:, :], g3[:, k, t, :], k == 0, k == 2, [CONST, G3], [PBb])
    cpy(C, "dve", st["gc"][:].rearrange("p t c -> p (t c)"), pb[:, 0:64], [PBb], [SB_["gc"]])
    cpy(C, "dve", st["gl"][:].rearrange("p t c -> p (t c)"), pb[:, 64:128], [PBb], [SB_["gl"]])
    tt(C, "dve", st["biasQ"][:], st["lrk"][:], st["gc"][:], ALU.subtract, [SB_["lrk"], SB_["gc"]], [SB_["biasQ"]])
    tt(C, "dve", st["biasA"][:], st["gc"][:], st["lb"][:], ALU.add, [SB_["gc"], SB_["lb"]], [SB_["biasA"]])
    tt(C, "dve", st["biasA"][:], st["biasA"][:], st["lrk"][:], ALU.add, [SB_["biasA"], SB_["lrk"]], [SB_["biasA"]])
    tt(C, "dve", st["rowQ"][:], st["gc"][:], st["lrq"][:], ALU.add, [SB_["gc"], SB_["lrq"]], [SB_["rowQ"]])
    actf(C, st["s_kbg"][:], st["biasA"][:], AF.Exp, [SB_["biasA"]], [SB_["s_kbg"]])
    tt(C, "dve", st["t1"][:], st["biasQ"][:], st["gl"][:], ALU.add, [SB_["biasQ"], SB_["gl"]], [SB_["t1"]])
    actf(C, st["s_kdec"][:], st["t1"][:], AF.Exp, [SB_["t1"]], [SB_["s_kdec"]])
    actf(C, st["beta"][:], st["lb"][:], AF.Exp, [SB_["lb"]], [SB_["beta"]])
    actf(C, st["s_o"][:], st["rowQ"][:], AF.Exp, [SB_["rowQ"]], [SB_["s_o"]])
    tap(C, "g", st["g"][:], [128, NT, 4], [SB_["g"]])
    tap(C, "gc", st["gc"][:], [128, NT, 4], [SB_["gc"]])
    tap(C, "beta", st["beta"][:], [128, NT, 4], [SB_["beta"]])
    tap(C, "lrk", st["lrk"][:], [128, NT, 4], [SB_["lrk"]])

    if stop == "stats":
        return
    eglb = sb("eglb", [128, 2, NT * 4], F32)
    EGLB = Buf("eglb")
    gl3, GL3 = split3("gl", "gl")
    pb, PBb = getps(C)
    for c in range(2):
        for k in range(3):
            mm(C, pb[:, c * 64:(c + 1) * 64], K.selrow[:, c, :], gl3[:, k, :, :].rearrange("p t h -> p (t h)"),
               k == 0, k == 2, [CONST, GL3], [PBb])
    actf(C, eglb[:].rearrange("p c n -> p (c n)"), pb[:, 0:128], AF.Exp, [PBb], [EGLB])
    bq3, BQ3 = split3("bq", "biasQ")
    rq3, RQ3 = split3("rq", "rowQ")

    if stop == "eglb":
        return
    def dbl(name, shape, dtype, n=2):
        return [sb(f"{name}{i}", shape, dtype) for i in range(n)], [Buf(f"{name}{i}") for i in range(n)]

    H4 = [128, 4, 128]
    kbg, KBG = dbl("kbg", H4, BF16)
    kdec, KDEC = dbl("kdec", H4, BF16, 3)
    vb, VB = dbl("vb", H4, BF16)
    expA, EXPA = dbl("expA", H4, F32)
    expQ, EXPQ = dbl("expQ", H4, F32)
    Xs, XS = dbl("X", H4, BF16, 6)
    Ys, YS = dbl("Y", H4, BF16, 6)
    Ws, WS = dbl("W", H4, BF16, 6)
    qkT, QKT = dbl("qkT", H4, BF16, 3)
    u0, U0 = dbl("u0", H4, F32, 3)
    kcdT, KCDT = dbl("kcdT", H4, BF16, 3)
    ut, UT = dbl("u", H4, BF16)
    ot, OT = dbl("o", H4, F32)
    tmpo, TMPO = dbl("tmpo", H4, F32)
    Sst = sb("S", H4, F32)
    Sb = sb("Sb", H4, BF16)
    SST, SBB = Buf("S"), Buf("Sb")
    P.op("dve", lambda e: e.memset(Sst[:], 0.0), writes=[SST])
    P.op("pool", lambda e: e.memset(Sb[:], 0.0), writes=[SBB])
    wz = sb("wz", [128, 8, 512], BF16)
    WZ = Buf("wz")
    load_w(C, wz[:], "w_in", C_GZ, 512, [WZ])
    sz, SZ = dbl("sz", [128, 512], F32)
    osq, OSQ = dbl("osq", H4, F32)
    rr_, RR = dbl("rr", [128, 4], F32)
    oab, OAB = dbl("oab", H4, BF16)
    ident4 = sb("ident4", H4, F32)
    ID4 = Buf("ident4")
    for h in range(4):
        cpy(C, "dve", ident4[:, h, :], K.ident_b[:, :], [CONST], [ID4])
    rowA, ROWA = dbl("rowA", [12, 128], BF16)
    rowQ, ROWQ = dbl("rowQ", [12, 128], BF16)
    r12, R12 = dbl("r12", [128, 2, 4, 3], BF16)

    def prep(p):
        s = p % 2
        s3 = p % 3
        tsl = slice(p * 128, (p + 1) * 128)
        pbk, PBK = getps(C)
        pbkv = pbk.bitcast(BF16)
        for h in range(4):
            P.op("pe", lambda e, o=pbkv[:, h * 128:(h + 1) * 128], i=kT[:, h, tsl]: e.transpose(o, i, K.ident_b[:, :]),
                 reads=[QKV["k"][h], CONST], writes=[PBK])
        kin = pbkv[:, 0:512].rearrange("p (h d) -> p h d", h=4)
        tt(C, "dve", kbg[s][:], kin, bc_last(st["s_kbg"][:, p, :], 128), ALU.mult, [PBK, SB_["s_kbg"]], [KBG[s]])
        tt(C, "dve", kdec[s3][:], kin, bc_last(st["s_kdec"][:, p, :], 128), ALU.mult, [PBK, SB_["s_kdec"]], [KDEC[s3]])
        pbv_, PBV = getps(C)
        pbvv = pbv_.bitcast(BF16)
        for h in range(4):
            P.op("pe", lambda e, o=pbvv[:, h * 128:(h + 1) * 128], i=vT[:, h, tsl]: e.transpose(o, i, K.ident_b[:, :]),
                 reads=[QKV["v"][h], CONST], writes=[PBV])
        vin = pbvv[:, 0:512].rearrange("p (h d) -> p h d", h=4)
        tt(C, "dve", vb[s][:], vin, bc_last(st["beta"][:, p, :], 128), ALU.mult, [PBV, SB_["beta"]], [VB[s]])
        yield
        cpy(C, "dve", r12[s][:, 0, :, :], bq3[:, :, p, :].rearrange("p k h -> p h k"), [BQ3], [R12[s]])
        cpy(C, "dve", r12[s][:, 1, :, :], rq3[:, :, p, :].rearrange("p k h -> p h k"), [RQ3], [R12[s]])
        prw, PRW = getps(C)
        prwv = prw.bitcast(BF16)
        P.op("pe", lambda e: e.transpose(prwv[0:12, 0:128], r12[s][:, 0, :, :].rearrange("p h k -> p (h k)"), K.ident_b[:, :]),
             reads=[R12[s], CONST], writes=[PRW])
        P.op("pe", lambda e: e.transpose(prwv[0:12, 128:256], r12[s][:, 1, :, :].rearrange("p h k -> p (h k)"), K.ident_b[:, :]),
             reads=[R12[s], CONST], writes=[PRW])
        cpy(C, "dve", rowA[s][0:12, :], prwv[0:12, 0:128], [PRW], [ROWA[s]])
        cpy(C, "dve", rowQ[s][0:12, :], prwv[0:12, 128:256], [PRW], [ROWQ[s]])
        pea, PEA = getps(C)
        peq, PEQ = getps(C)
        for h in range(4):
            hs = slice(h * 128, (h + 1) * 128)
            mm(C, pea[:, hs], K.sel12[0:12, h, :], rowA[s][0:12, :], True, False, [CONST, ROWA[s]], [PEA])
            mm(C, pea[:, hs], K.ident_b[:, :], K.nega[:, :], False, True, [CONST], [PEA])
            mm(C, peq[:, hs], K.sel12[0:12, h, :], rowQ[s][0:12, :], True, False, [CONST, ROWQ[s]], [PEQ])
            mm(C, peq[:, hs], K.ident_b[:, :], K.negq[:, :], False, True, [CONST], [PEQ])
        pkk, PKK = getps(C)
        pkq, PKQ = getps(C)
        for h in range(4):
            hs = slice(h * 128, (h + 1) * 128)
            mm(C, pkk[:, hs], kT[:, h, tsl], kT[:, h, tsl], True, True, [QKV["k"][h]], [PKK])
            mm(C, pkq[:, hs], kT[:, h, tsl], qT[:, h, tsl], True, True, [QKV["k"][h], QKV["q"][h]], [PKQ])
        for h in range(4):
            hs = slice(h * 128, (h + 1) * 128)
            actf(C, expA[s][:, h, :], pea[:, hs], AF.Exp, [PEA, SB_["biasA"]], [EXPA[s]], bias=st["biasA"][:, p, h:h + 1])
            actf(C, expQ[s][:, h, :], peq[:, hs], AF.Exp, [PEQ, SB_["biasQ"]], [EXPQ[s]], bias=st["biasQ"][:, p, h:h + 1])
        x0 = s * 3
        tt(C, "dve", Xs[x0][:].rearrange("p h d -> p (h d)"), pkk[:, :], expA[s][:].rearrange("p h d -> p (h d)"),
           ALU.mult, [PKK, EXPA[s]], [XS[x0]])
        tt(C, "dve", qkT[s3][:].rearrange("p h d -> p (h d)"), pkq[:, :], expQ[s][:].rearrange("p h d -> p (h d)"),
           ALU.mult, [PKQ, EXPQ[s]], [QKT[s3]])
        yield
        pbt, PBT = getps(C)
        pbtv = pbt.bitcast(BF16)
        for h in range(4):
            P.op("pe", lambda e, o=pbtv[:, h * 128:(h + 1) * 128], i=Xs[x0][:, h, :]: e.transpose(o, i, K.ident_b[:, :]),
                 reads=[XS[x0], CONST], writes=[PBT])
        bin_ = pbtv[:, 0:512].rearrange("p (h d) -> p h d", h=4)
        cpy(C, "act", Ys[x0][:], bin_, [PBT], [YS[x0]])
        tt(C, "dve", Ws[x0][:], ident4[:], bin_, ALU.subtract, [PBT, ID4], [WS[x0]])
        yield
        for lvl in range(1, 6):
            xi, xo = s * 3 + (lvl - 1) % 3, s * 3 + lvl % 3
            pa, PA = getps(C)
            for h in range(4):
                hs = slice(h * 128, (h + 1) * 128)
                mm(C, pa[:, hs], Ys[xi][:, h, :], Xs[xi][:, h, :], True, True, [YS[xi], XS[xi]], [PA])
            cpy(C, "act", Xs[xo][:].rearrange("p h d -> p (h d)"), pa[:, :], [PA], [XS[xo]])
            if lvl < 5:
                pbb, PBB_ = getps(C)
                for h in range(4):
                    hs = slice(h * 128, (h + 1) * 128)
                    mm(C, pbb[:, hs], Xs[xi][:, h, :], Ys[xi][:, h, :], True, True, [YS[xi], XS[xi]], [PBB_])
                cpy(C, "act", Ys[xo][:].rearrange("p h d -> p (h d)"), pbb[:, :], [PBB_], [YS[xo]])
            yield
            pw, PW = getps(C)
            for h in range(4):
                hs = slice(h * 128, (h + 1) * 128)
                mm(C, pw[:, hs], Xs[xo][:, h, :], Ws[xi][:, h, :], True, True, [XS[xo], WS[xi]], [PW])
            tt(C, "dve", Ws[xo][:].rearrange("p h d -> p (h d)"), pw[:, :], Ws[xi][:].rearrange("p h d -> p (h d)"),
               ALU.add, [PW, WS[xi]], [WS[xo]])
            yield
        wf = s * 3 + 5 % 3
        pu, PU = getps(C)
        pk, PK = getps(C)
        for h in range(4):
            hs = slice(h * 128, (h + 1) * 128)
            mm(C, pu[:, hs], Ws[wf][:, h, :], vb[s][:, h, :], True, True, [WS[wf], VB[s]], [PU])
            mm(C, pk[:, hs], kbg[s][:, h, :], Ws[wf][:, h, :], True, True, [WS[wf], KBG[s]], [PK])
        cpy(C, "act", u0[s3][:].rearrange("p h d -> p (h d)"), pu[:, :], [PU], [U0[s3]])
        cpy(C, "dve", kcdT[s3][:].rearrange("p h d -> p (h d)"), pk[:, :], [PK], [KCDT[s3]])
        yield

    def scan(p):
        s = p % 2
        s3 = p % 3
        tsl = slice(p * 128, (p + 1) * 128)
        for c in range(2):
            r = slice(64 * c, 64 * c + 64)
            n = 2 * p + c
            pm1, PM1 = getps(C)
            for h in range(4):
                hs = slice(h * 128, (h + 1) * 128)
                mm(C, pm1[:, hs], kcdT[s3][:, h, :], Sb[:, h, :], True, True, [KCDT[s3], SBB], [PM1])
            tt(C, "dve", ut[s][r, :, :].rearrange("p h d -> p (h d)"), u0[s3][r, :, :].rearrange("p h d -> p (h d)"),
               pm1[r, :], ALU.subtract, [U0[s3], PM1], [UT[s]])
            pm2i, pm2, PM2 = reserve_ps(C)
            for h in range(4):
                hs = slice(h * 128, (h + 1) * 128)
                mm(C, pm2[:, hs], qT[:, h, tsl], Sb[:, h, :], True, True, [QKV["q"][h], SBB], [PM2])
            yield
            pm3, PM3 = getps(C)
            pm4, PM4 = getps(C)
            for h in range(4):
                hs = slice(h * 128, (h + 1) * 128)
                mm(C, pm3[:, hs], qkT[s3][r, h, :], ut[s][r, h, :], True, True, [QKT[s3], UT[s]], [PM3])
                mm(C, pm4[:, hs], kdec[s3][r, h, :], ut[s][r, h, :], True, True, [KDEC[s3], UT[s]], [PM4])
            for h in range(4):
                hs = slice(h * 128, (h + 1) * 128)
                actf(C, tmpo[s][r, h, :], pm2[r, hs], AF.Identity, [PM2, SB_["s_o"]], [TMPO[s]], scale=st["s_o"][r, p, h:h + 1])
            C.ps_reserved.discard(pm2i)
            tt(C, "dve", ot[s][r, :, :].rearrange("p h d -> p (h d)"), tmpo[s][r, :, :].rearrange("p h d -> p (h d)"),
               pm3[r, :], ALU.add, [TMPO[s], PM3], [OT[s]])
            for h in range(4):
                hs = slice(h * 128, (h + 1) * 128)
                stt(C, "dve", Sst[:, h, :], Sst[:, h, :], eglb[:, c, p * 4 + h:p * 4 + h + 1], pm4[:, hs], ALU.mult, ALU.add,
                    [SST, EGLB, PM4], [SST])
            cpy(C, "act", Sb[:].rearrange("p h d -> p (h d)"), Sst[:].rearrange("p h d -> p (h d)"), [SST], [SBB])
            yield

    def outp(p):
        s = p % 2
        tsl = slice(p * 128, (p + 1) * 128)
        pz, PZ = getps(C)
        for c in range(8):
            mm(C, pz[:, :], C.xT[:, c, tsl], wz[:, c, :], c == 0, c == 7, [C.XT[p], WZ], [PZ])
        actf(C, sz[s][:, :], pz[:, :], AF.Silu, [PZ], [SZ[s]])
        o2 = ot[s][:].rearrange("p h d -> p (h d)")
        tt(C, "dve", osq[s][:].rearrange("p h d -> p (h d)"), o2, o2, ALU.mult, [OT[s]], [OSQ[s]])
        P.op("dve", lambda e: e.tensor_reduce(rr_[s][:, :], osq[s][:], AX.X, ALU.add), reads=[OSQ[s]], writes=[RR[s]])
        actf(C, rr_[s][:, :], rr_[s][:, :], AF.Ln, [RR[s]], [RR[s]], scale=1.0 / 128.0, bias=RMS_EPS)
        actf(C, rr_[s][:, :], rr_[s][:, :], AF.Exp, [RR[s]], [RR[s]], scale=-0.5)
        tt(C, "dve", osq[s][:], ot[s][:], bc_last(rr_[s][:, :], 128), ALU.mult, [OT[s], RR[s]], [OSQ[s]])
        tt(C, "dve", osq[s][:], osq[s][:], K.gnw[:], ALU.mult, [OSQ[s], CONST], [OSQ[s]])
        tt(C, "dve", oab[s][:].rearrange("p h d -> p (h d)"), osq[s][:].rearrange("p h d -> p (h d)"), sz[s][:, :],
           ALU.mult, [OSQ[s], SZ[s]], [OAB[s]])
        if p == 0:
            tap(C, "o_raw0", ot[s][:], [128, 4, 128], [OT[s]])
            tap(C, "oab0", oab[s][:], [128, 4, 128], [OAB[s]])
        if p == 9:
            tap(C, "o_raw9", ot[s][:], [128, 4, 128], [OT[s]])
        pt, PT = getps(C)
        ptv = pt.bitcast(BF16)
        for h in range(4):
            P.op("pe", lambda e, o=ptv[:, h * 128:(h + 1) * 128], i=oab[s][:, h, :]: e.transpose(o, i, K.ident_b[:, :]),
                 reads=[OAB[s], CONST], writes=[PT])
        cpy(C, "act", C.oAT[:, :, tsl], ptv[:, 0:512].rearrange("p (h d) -> p h d", h=4), [PT], [C.OAT[p]])
        yield

    SCAN_PRIO = int(os.environ.get("GDN_SCAN_PRIO", "1"))

    def scan_out(p):
        g_ = scan(p)
        while True:
            P.cur_prio = SCAN_PRIO
            try:
                next(g_)
            except StopIteration:
                P.cur_prio = 0
                break
            P.cur_prio = 0
            yield
        yield from outp(p)

    if stop != "":
        for p in range(1):
            nst = int(stop[4:]) if stop.startswith("prep") and len(stop) > 4 else 999
            for i_, _ in enumerate(prep(p)):
                if i_ + 1 >= nst:
                    break
            if stop.startswith("prep"):
                continue
            for _ in scan(p):
                pass
            if stop == "scan":
                continue
            for _ in outp(p):
                pass
    else:
        npar = int(os.environ.get("GDN_NPAR", "2"))
        active = []
        next_prep = 0
        next_scan = 0
        prep_done = set()
        scan_gen = None
        while next_scan < NT:
            while len(active) < npar and next_prep < NT and next_prep <= next_scan + 2:
                active.append([next_prep, prep(next_prep)])
                next_prep += 1
            if scan_gen is None and next_scan in prep_done:
                scan_gen = scan_out(next_scan)
            for ent in list(active):
                try:
                    next(ent[1])
                except StopIteration:
                    prep_done.add(ent[0])
                    active.remove(ent)
            if scan_gen is not None:
                try:
                    next(scan_gen)
                except StopIteration:
                    scan_gen = None
                    next_scan += 1
    tap(C, "oAT", C.oAT[:, 0, :], [128, S], C.OAT)


def mm2(C, out, lhsT, rhs, start, stop, rd, wr):
    C.P.op("pe", lambda e: e.matmul(out, lhsT, rhs, start=start, stop=stop, skip_group_check=True), reads=rd, writes=wr,
           cost=PE_FIX + PE_COL * _free(rhs))


def phase_nsa(C, ph):
    import os
    nc, P, dr, K = C.nc, C.P, C.dr, C.K
    CONST = C.CONST
    stop = os.environ.get("NSA_STOP", "")

    def sb(name, shape, dtype):
        return ph.enter_context(nc.sbuf_tensor("n_" + name, list(shape), dtype))

    phs1 = ExitStack()

    def sb1(name, shape, dtype):
        return phs1.enter_context(nc.sbuf_tensor("n_" + name, list(shape), dtype))

    K.cmpmask = cload(C, sb, "k_cmpmask", "c_cmpmask", [128, S], BF16)
    K.overlap = cload(C, sb, "k_overlap", "c_overlap", [128, 32], BF16)
    K.causal = cload(C, sb, "k_causal", "c_causal", [128, 128], BF16)
    K.anti = cload(C, sb, "k_anti", "c_anti", [128, 128], BF16)
    K.expand = cload(C, sb, "k_expand", "c_expand", [32, 16, 128], BF16)
    K.forced = cload(C, sb, "k_forced", "c_forced", [128, 8, 32])
    K.poskT = cload(C, sb, "k_poskT", "p_poskT", [128, 32], BF16)
    K.posvT = cload(C, sb, "k_posvT", "p_posvT", [128, 32], BF16)

    QT = sb("QT", [64, 8, S], BF16)
    QTB = [Buf(f"QT{i}") for i in range(8)]
    KsT = sb("KsT", [64, 2, S], BF16)
    KwT = sb("KwT", [64, 2, S], BF16)
    KST = [Buf(f"KsT{g}") for g in range(2)]
    KWT = [Buf(f"KwT{g}") for g in range(2)]
    KcTc = sb("KcTc", [64, 2, 128], BF16)
    Vca = sb("Vca", [128, 2, 97], BF16)
    Vs = sb("Vs", [128, NT, 2, 65], BF16)
    Vw = sb("Vw", [128, NT, 2, 65], BF16)
    VS, VW = Buf("Vs"), Buf("Vw")
    gts = sb("gates", [128, NT, 24], F32)
    KcT = sb1("KcT", [128, S], BF16)
    VcT = sb1("VcT", [128, S], BF16)
    KCT, VCT = Buf("KcT"), Buf("VcT")
    GTS = Buf("gates")
    P.op("pool", lambda e: e.memset(Vs[:], 1.0), writes=[VS])
    P.op("pool", lambda e: e.memset(Vw[:], 1.0), writes=[VW])

    wt = [t[:, :, 0:128] for t in C.wst]
    WT = C.WST
    tlw = C.wst_tl
    hi = [sb1(f"hi{i}", [128, 512], BF16) for i in range(2)]
    HI = [Buf(f"hi{i}") for i in range(2)]
    nhi = 0
    jobs = [("q", 0), ("q", 1), ("q", 2), ("q", 3), ("ks", 0), ("kw", 0), ("kc", 0), ("vc", 0)]
    for ji, (kind, idx) in enumerate(jobs):
        ws = C.wk % 3
        C.wk += 1
        P.dma("pool", C.wst[ws][:, :, 0:128], dr["pk_n128"][ji], writes=[WT[ws]], tl=tlw[ws])
        for tb in range(4):
            pb, PBb = getps(C)
            for c in range(8):
                mm(C, pb[:, :], wt[ws][:, c, :], C.xT[:, c, tb * 512:(tb + 1) * 512], c == 0, c == 7,
                   [WT[ws]] + C.XT[tb * 4:(tb + 1) * 4], [PBb])
            tsl = slice(tb * 512, (tb + 1) * 512)
            if kind == "kc":
                cpy(C, "act", KcT[:, tsl], pb[:, :], [PBb], [KCT])
            elif kind == "vc":
                cpy(C, "dve", VcT[:, tsl], pb[:, :], [PBb], [VCT])
            else:
                hs_ = nhi % 2
                nhi += 1
                if kind == "q":
                    actf(C, QT[:, 2 * idx, tsl], pb[0:64, :], AF.Copy, [PBb], [QTB[2 * idx]], scale=0.125)
                    actf(C, hi[hs_][64:128, :], pb[64:128, :], AF.Copy, [PBb], [HI[hs_]], scale=0.125)
                    P.dma("sp", QT[:, 2 * idx + 1, tsl], hi[hs_][64:128, :], reads=[HI[hs_]], writes=[QTB[2 * idx + 1]])
                else:
                    dst, DST = (KsT, KST) if kind == "ks" else (KwT, KWT)
                    cpy(C, "dve", dst[:, 0, tsl], pb[0:64, :], [PBb], [DST[0]])
                    cpy(C, "dve", hi[hs_][64:128, :], pb[64:128, :], [PBb], [HI[hs_]])
                    P.dma("sp", dst[:, 1, tsl], hi[hs_][64:128, :], reads=[HI[hs_]], writes=[DST[1]])
    wv = sb1("wv", [128, 8, 280], BF16)
    WV = Buf("wv")
    tlv = P.new_dma_tl("nwv")
    for (c0, ncol, dcol) in ((C_NVS, 128, 0), (C_NVW, 128, 128), (C_NG, 24, 256)):
        src = dr["w_in"][:, c0:c0 + ncol].rearrange("(c p) n -> p c n", p=128)
        P.dma("pool", wv[:, :, dcol:dcol + ncol], src, writes=[WV], tl=tlv)
    gtmp = sb1("gtmp", [128, NT, 24], F32)
    GTMP = Buf("gtmp")
    for t in range(NT):
        pb, PBb = getps(C)
        for c in range(8):
            mm(C, pb[:, 0:280], C.xT[:, c, t * 128:(t + 1) * 128], wv[:, c, :], c == 0, c == 7, [C.XT[t], WV], [PBb])
        cpy(C, "act", Vs[:, t, :, 0:64], pb[:, 0:128].rearrange("p (g d) -> p g d", g=2), [PBb], [VS])
        cpy(C, "dve", Vw[:, t, :, 0:64], pb[:, 128:256].rearrange("p (g d) -> p g d", g=2), [PBb], [VW])
        actf(C, gtmp[:, t, :], pb[:, 256:280], AF.Tanh, [PBb], [GTMP], scale=0.5)
    ts(C, "dve", gts[:], gtmp[:], 0.5, ALU.mult, [GTMP], [GTS], s2=0.5, op1=ALU.add)
    if stop == "proj":
        tap(C, "x_QT", QT[:, 3, :], [64, S], QTB)
        tap(C, "x_KsT", KsT[:, 1, :], [64, S], KST)
        tap(C, "x_Vw", Vw[:], [128, NT, 2, 65], [VW])
        tap(C, "x_gates", gts[:], [128, NT, 24], [GTS])
        return

    KCTC, VCA = Buf("KcTc"), Buf("Vca")
    P.op("pool", lambda e: e.memset(KcTc[:], 0.0), writes=[KCTC])
    P.op("pool", lambda e: e.memset(Vca[:], 0.0), writes=[VCA])
    w1 = sb1("w1", [128, 32, 256], BF16)
    W1B = Buf("w1")
    tl1 = P.new_dma_tl("nw1")
    w2k = sb1("w2k", [128, 2, 64], BF16)
    w2v = sb1("w2v", [128, 2, 64], BF16)
    W2K, W2V = Buf("w2k"), Buf("w2v")
    P.dma("pool", w2k[:, :, :], dr["cmp_w2_k"].rearrange("(j c) d -> c j d", c=128), writes=[W2K])
    P.dma("pool", w2v[:, :, :], dr["cmp_w2_v"].rearrange("(j c) d -> c j d", c=128), writes=[W2V])
    hx = sb1("hx", [128, 128], F32)
    hx2 = sb1("hx2", [128, 128], F32)
    hth = sb1("hth", [128, 128], F32)
    h1 = sb1("h1", [128, 2, 128], BF16)
    b1 = sb1("b1", [128, 2], F32)
    HX, HX2, HTH, H1, B1 = Buf("hx"), Buf("hx2"), Buf("hth"), Buf("h1"), Buf("b1")
    for kv in ("k", "v"):
        key = "cmp_w1_" + kv
        src = dr[key].rearrange("(l d) c -> d l c", d=64)
        P.dma("pool", w1[0:64, :, :], src, writes=[W1B], tl=tl1)
        P.dma("pool", w1[64:128, :, :], src, writes=[W1B], tl=tl1)
        XcT, XCT = (KcT, KCT) if kv == "k" else (VcT, VCT)
        posT = K.poskT if kv == "k" else K.posvT
        for g in range(2):
            hr = slice(64 * g, 64 * g + 64)
            pbb, PBB_ = getps(C)
            for j in range(2):
                for l in range(32):
                    mm(C, pbb[:, j:j + 1], w1[hr, l, j * 128:(j + 1) * 128], posT[hr, l:l + 1], l == 0, l == 31,
                       [W1B, CONST], [PBB_])
            cpy(C, "dve", b1[:, :], pbb[:, 0:2], [PBB_], [B1])
            P.op("dve", lambda e: e.memset(h1[:], 0.0), writes=[H1])
            for j in range(2):
                ph1, PH1 = getps(C)
                xv = XcT[hr, :].rearrange("p (n r) -> p n r", r=16)
                for l in range(32):
                    rhs = xv[:, (l // 16):(l // 16) + 127, l % 16]
                    mm(C, ph1[:, 0:127], w1[hr, l, j * 128:(j + 1) * 128], rhs, l == 0, l == 31, [W1B, XCT], [PH1])
                ts(C, "dve", hx[:, 0:127], ph1[:, 0:127], b1[:, j:j + 1], ALU.add, [PH1, B1], [HX])
                tt(C, "dve", hx2[:, 0:127], hx[:, 0:127], hx[:, 0:127], ALU.mult, [HX], [HX2])
                ts(C, "dve", hx2[:, 0:127], hx2[:, 0:127], 0.044715, ALU.mult, [HX2], [HX2], s2=1.0, op1=ALU.add)
                tt(C, "dve", hx2[:, 0:127], hx2[:, 0:127], hx[:, 0:127], ALU.mult, [HX2, HX], [HX2])
                actf(C, hth[:, 0:127], hx2[:, 0:127], AF.Tanh, [HX2], [HTH], scale=0.7978845608028654)
                stt(C, "dve", hth[:, 0:127], hth[:, 0:127], 1.0, hx[:, 0:127], ALU.add, ALU.mult, [HTH, HX], [HTH])
                ts(C, "dve", h1[:, j, 0:127], hth[:, 0:127], 0.5, ALU.mult, [HTH], [H1])
            po, PO = getps(C)
            if kv == "k":
                for j in range(2):
                    mm(C, po[0:64, 0:128], w2k[:, j, :], h1[:, j, :], j == 0, j == 1, [W2K, H1], [PO])
                cpy(C, "dve", KcTc[:, g, 0:127], po[0:64, 0:127], [PO], [KCTC])
            else:
                for j in range(2):
                    mm(C, po[:, 0:64], h1[:, j, :], w2v[:, j, :], j == 0, j == 1, [W2V, H1], [PO])
                cpy(C, "dve", Vca[0:127, g, 0:64], po[0:127, 0:64], [PO], [VCA])
    for g in range(2):
        P.op("dve", lambda e, g=g: e.memset(Vca[0:127, g, 64:65], 1.0), reads=[], writes=[VCA])
        cpy(C, "dve", Vca[:, g, 65:97], K.overlap[:, :], [CONST], [VCA])
    if stop == "cmp":
        tap(C, "x_KcTc", KcTc[:], [64, 2, 128], [KCTC])
        tap(C, "x_Vca", Vca[:], [128, 2, 97], [VCA])
        return

    P.barrier()
    phs1.close()
    NE = 4
    et = [sb(f"e{i}", [128, 512], BF16) for i in range(NE)]
    ET = [Buf(f"e{i}") for i in range(NE)]
    pt = [sb(f"p{i}", [128, 512], BF16) for i in range(NE)]
    PT_ = [Buf(f"p{i}") for i in range(NE)]
    selm4 = [sb(f"selm{i}", [128, 16, 128], BF16) for i in range(4)]
    SELM4 = [Buf(f"selm{i}") for i in range(4)]
    oB = [sb(f"oB{i}", [128, 512], F32) for i in range(2)]
    OB = [Buf(f"oB{i}") for i in range(2)]
    oBb = [sb(f"oBb{i}", [128, 512], BF16) for i in range(2)]
    OBB = [Buf(f"oBb{i}") for i in range(2)]
    rden = [sb(f"rden{i}", [128, 4], F32) for i in range(2)]
    RDEN = [Buf(f"rden{i}") for i in range(2)]
    fac = [sb(f"fac{i}", [128, 4], F32) for i in range(2)]
    FAC = [Buf(f"fac{i}") for i in range(2)]
    obr = [sb(f"obr{i}", [128, 4, 64], F32) for i in range(2)]
    OBR = [Buf(f"obr{i}") for i in range(2)]
    impt = sb("impt", [128, 4, 32], F32)
    imp = sb("imp", [128, 32], F32)
    imp2 = sb("imp2", [128, 32], F32)
    mx8 = sb("mx8", [128, 8], F32)
    thr = sb("thr", [128, 1], F32)
    bmf = sb("bmf", [128, 32], BF16)
    bmT = sb("bmT", [32, 128], BF16)
    IMPT, IMP, IMP2, MX8, THR, BMF, BMT = (Buf(n) for n in ("impt", "imp", "imp2", "mx8", "thr", "bmf", "bmT"))
    cnt = {"e": 0, "ev": 0, "m": 0}

    def gate_view(t, g, br):
        v = gts[:, t, g * 12:(g + 1) * 12].rearrange("p (b br) -> p b br", b=4)
        return v[:, :, br]

    def qk_exp(kT_, KB, g, kt, qt):
        ps_, PS_ = getps(C)
        ksl = slice(kt * 128, (kt + 1) * 128)
        qsl = slice(qt * 128, (qt + 1) * 128)
        mm(C, ps_[:, :], kT_(ksl), QT[:, 4 * g:4 * g + 4, qsl], True, True, KB + QTB[4 * g:4 * g + 4], [PS_])
        i = cnt["e"] % NE
        cnt["e"] += 1
        actf(C, et[i][:, :], ps_[:, :], AF.Exp, [PS_], [ET[i]])
        return et[i], ET[i]

    def masked(e_, E_, mask_ap, MB):
        i = cnt["m"] % NE
        cnt["m"] += 1
        eng = "dve"
        tt(C, eng, pt[i][:].rearrange("p (b q) -> p b q", b=4), e_[:].rearrange("p (b q) -> p b q", b=4),
           bc_mid(mask_ap, 4), ALU.mult, [E_] + MB, [PT_[i]])
        return pt[i], PT_[i]

    def evac(po, PO, width, t, g, br, first):
        s = t % 2
        i = cnt["ev"] % 2
        cnt["ev"] += 1
        pov = po[:, 0:4 * width].rearrange("p (b w) -> p b w", b=4)
        ts(C, "dve", rden[i][:, :], pov[:, :, 64], 1e-30, ALU.add, [PO], [RDEN[i]])
        P.op("dve", lambda e: e.reciprocal(rden[i][:, :], rden[i][:, :]), reads=[RDEN[i]], writes=[RDEN[i]])
        tt(C, "dve", fac[i][:, :], rden[i][:, :], gate_view(t, g, br), ALU.mult, [RDEN[i], GTS], [FAC[i]])
        ov = oB[s][:, g * 256:(g + 1) * 256].rearrange("p (b d) -> p b d", b=4)
        if first:
            tt(C, "dve", ov, pov[:, :, 0:64], bc_last(fac[i][:, :], 64), ALU.mult, [PO, FAC[i]], [OB[s]])
        else:
            tt(C, "dve", obr[i][:], pov[:, :, 0:64], bc_last(fac[i][:, :], 64), ALU.mult, [PO, FAC[i]], [OBR[i]])
            tt(C, "dve", ov, ov, obr[i][:], ALU.add, [OB[s], OBR[i]], [OB[s]])
        return i

    nqt = NT if stop == "" else int(os.environ.get("NSA_NQT", "16"))
    DEPTH = int(os.environ.get("NSA_DEPTH", "4"))
    blocks = []

    def add_cmp(qt, g):
        st_ = {}
        qsl = slice(qt * 128, (qt + 1) * 128)
        selm = selm4[(qt % 2) * 2:(qt % 2) * 2 + 2]
        SELM = SELM4[(qt % 2) * 2:(qt % 2) * 2 + 2]

        def front():
            e_, E_ = qk_exp(lambda ksl: KcTc[:, g, :], [KCTC], g, 0, qt)
            st_["p"] = masked(e_, E_, K.cmpmask[:, qsl], [CONST])

        def back():
            p_, P_ = st_["p"]
            po, PO = getps(C)
            for b in range(4):
                mm(C, po[:, b * 97:(b + 1) * 97], p_[:, b * 128:(b + 1) * 128], Vca[:, g, :], True, True, [P_, VCA], [PO])
            ri = evac(po, PO, 97, qt, g, 0, True)
            if qt < 8:
                return
            pov = po[:, 0:388].rearrange("p (b w) -> p b w", b=4)
            tt(C, "dve", impt[:], pov[:, :, 65:97], bc_last(rden[ri][:, :], 32), ALU.mult, [PO, RDEN[ri]], [IMPT])
            P.op("dve", lambda e: e.tensor_reduce(imp[:, :], impt[:].rearrange("p b j -> p j b"), AX.X, ALU.add),
                 reads=[IMPT], writes=[IMP])
            tt(C, "dve", imp[:, :], imp[:, :], K.forced[:, qt - 8, :], ALU.add, [IMP, CONST], [IMP])
            P.op("dve", lambda e: e.max(mx8[:, :], imp[:, :]), reads=[IMP], writes=[MX8])
            P.op("dve", lambda e: e.match_replace(imp2[:, :], mx8[:, :], imp[:, :], -3.0e38), reads=[IMP, MX8], writes=[IMP2])
            P.op("dve", lambda e: e.max(mx8[:, :], imp2[:, :]), reads=[IMP2], writes=[MX8])
            P.op("dve", lambda e: e.tensor_reduce(thr[:, :], mx8[:, :], AX.X, ALU.min), reads=[MX8], writes=[THR])
            ts(C, "dve", bmf[:, :], imp[:, :], thr[:, 0:1], ALU.is_ge, [IMP, THR], [BMF])
            pbt, PBT = getps(C)
            pbtv = pbt.bitcast(BF16)
            P.op("pe", lambda e, o=pbtv[0:32, 0:128]: e.transpose(o, bmf[:, :], K.ident_b[:, :]), reads=[BMF, CONST], writes=[PBT])
            cpy(C, "dve", bmT[:, :], pbtv[0:32, 0:128], [PBT], [BMT])
            for k4 in range(0, qt + 1, 4):
                pe_, PE_ = getps(C)
                nk = min(4, qt + 1 - k4)
                for j in range(nk):
                    mm(C, pe_[:, j * 128:(j + 1) * 128], K.expand[0:32, k4 + j, :], bmT[0:32, :], True, True, [CONST, BMT], [PE_])
                if k4 + nk - 1 == qt:
                    if nk > 1:
                        cpy(C, "act", selm[g][:, k4:k4 + nk - 1, :], pe_[:, 0:(nk - 1) * 128].rearrange("p (k q) -> p k q", q=128),
                            [PE_], [SELM[g]])
                    tt(C, "dve", selm[g][:, qt, :], pe_[:, (nk - 1) * 128:nk * 128], K.causal[:, :], ALU.mult, [PE_, CONST], [SELM[g]])
                else:
                    cpy(C, "act", selm[g][:, k4:k4 + nk, :], pe_[:, 0:nk * 128].rearrange("p (k q) -> p k q", q=128), [PE_], [SELM[g]])

        blocks.append((front, back))

    def add_branch(qt, g, br):
        acc_ = {}
        selm = selm4[(qt % 2) * 2:(qt % 2) * 2 + 2]
        SELM = SELM4[(qt % 2) * 2:(qt % 2) * 2 + 2]
        if br == 1:
            kts = list(range(qt + 1))
        else:
            kts = list(range(max(0, qt - 4), qt + 1))
        for kt in kts:
            st_ = {}

            def front(kt=kt, st_=st_):
                if br == 1:
                    e_, E_ = qk_exp(lambda ksl: KsT[:, g, ksl], [KST[g]], g, kt, qt)
                    if qt >= 8:
                        st_["p"] = masked(e_, E_, selm[g][:, kt, :], [SELM[g]])
                    elif kt == qt:
                        st_["p"] = masked(e_, E_, K.causal[:, :], [CONST])
                    else:
                        st_["p"] = (e_, E_)
                else:
                    e_, E_ = qk_exp(lambda ksl: KwT[:, g, ksl], [KWT[g]], g, kt, qt)
                    if kt == qt:
                        st_["p"] = masked(e_, E_, K.causal[:, :], [CONST])
                    elif kt == qt - 4:
                        st_["p"] = masked(e_, E_, K.anti[:, :], [CONST])
                    else:
                        st_["p"] = (e_, E_)

            def back(kt=kt, st_=st_):
                p_, P_ = st_["p"]
                if kt == kts[0]:
                    acc_["po"] = reserve_ps(C)
                poi, po, PO = acc_["po"]
                Vt, VB_ = (Vs, VS) if br == 1 else (Vw, VW)
                for b in range(4):
                    mm2(C, po[:, b * 65:(b + 1) * 65], p_[:, b * 128:(b + 1) * 128], Vt[:, kt, g, :],
                        (kt == kts[0] and b == 0), kt == kts[-1], [P_, VB_], [PO])
                if kt == kts[-1]:
                    evac(po, PO, 65, qt, g, br, False)
                    C.ps_reserved.discard(poi)

            blocks.append((front, back))

    def add_finish(qt):
        s = qt % 2
        qsl = slice(qt * 128, (qt + 1) * 128)

        def front():
            pass

        def back():
            cpy(C, "act", oBb[s][:, :], oB[s][:, :], [OB[s]], [OBB[s]])
            if qt in (0, 1, 3, 7, 9):
                tap(C, f"x_oB{qt}", oB[s][:, :], [128, 512], [OB[s]])
            ptr, PTR = getps(C)
            ptrv = ptr.bitcast(BF16)
            for c in range(4):
                P.op("pe", lambda e, o=ptrv[:, c * 128:(c + 1) * 128], i=oBb[s][:, c * 128:(c + 1) * 128]: e.transpose(o, i, K.ident_b[:, :]),
                     reads=[OBB[s], CONST], writes=[PTR])
            cpy(C, "act", C.oBT[:, :, qsl], ptrv[:, 0:512].rearrange("p (c q) -> p c q", c=4), [PTR], [C.OBT[qt]])

        blocks.append((front, back))

    add_cmp(0, 0)
    add_cmp(0, 1)
    for qt in range(nqt):
        if qt + 1 < nqt:
            add_cmp(qt + 1, 0)
            add_cmp(qt + 1, 1)
        for g in range(2):
            add_branch(qt, g, 1)
            add_branch(qt, g, 2)
        add_finish(qt)
    nb = len(blocks)
    for i in range(nb + DEPTH):
        if i - DEPTH >= 0:
            blocks[i - DEPTH][1]()
        if i < nb:
            blocks[i][0]()
    tap(C, "oBT", C.oBT[:, 0, :], [128, S], C.OBT)


def layer_norm_tile(C, v, V, stats, STATS, mv, MV, gt, bt, out, OUT, mul_eng="dve"):
    P = C.P
    for n in range(2):
        P.op("dve", lambda e, n=n: e.bn_stats(stats[:, n, :], v[:, n * 512:(n + 1) * 512]), reads=[V], writes=[STATS])
    P.op("dve", lambda e: e.bn_aggr(mv[:, 0:2], stats[:].rearrange("p n s -> p (n s)")), reads=[STATS], writes=[MV])
    actf(C, mv[:, 2:3], mv[:, 1:2], AF.Ln, [MV], [MV], bias=LN_EPS)
    actf(C, mv[:, 2:3], mv[:, 2:3], AF.Exp, [MV], [MV], scale=-0.5)
    stt(C, "dve", mv[:, 3:4], mv[:, 0:1], -1.0, mv[:, 2:3], ALU.mult, ALU.mult, [MV], [MV])
    actf(C, v[:, :], v[:, :], AF.Identity, [V, MV], [V], bias=mv[:, 3:4], scale=mv[:, 2:3])
    tt(C, mul_eng, v[:, :], v[:, :], gt[:, :], ALU.mult, [V, C.CONST], [V])
    tt(C, mul_eng, out[:, :], v[:, :], bt[:, :], ALU.add, [V, C.CONST], [OUT])


def phase_mix(C, ph):
    import os
    nc, P, dr, K = C.nc, C.P, C.dr, C.K
    CONST = C.CONST

    def sb(name, shape, dtype):
        return ph.enter_context(nc.sbuf_tensor("m_" + name, list(shape), dtype))

    ln1g = cload(C, sb, "ln1g", "p_ln1g", [128, D])
    ln1b = cload(C, sb, "ln1b", "p_ln1b", [128, D])
    wA = sb("wA", [128, 4, D], BF16)
    wB = sb("wB", [128, 4, D], BF16)
    wo = sb("wo", [128, 8, D], BF16)
    WA, WB, WO = Buf("wA"), Buf("wB"), Buf("wo")
    load_w(C, wA[:], "w_branch_gdn", 0, D, [WA], nrow_chunks=4)
    load_w(C, wB[:], "w_branch_nsa", 0, D, [WB], nrow_chunks=4)
    wg, WG, tlg = C.wst, C.WST, C.wst_tl
    mixT = [sb(f"mixT{i}", [128, 8, 512], BF16) for i in range(2)]
    MIXT = [Buf(f"mixT{i}") for i in range(2)]
    NB = 2
    th = [sb(f"th{i}", [128, 2, 512], F32) for i in range(NB)]
    TH = [Buf(f"th{i}") for i in range(NB)]
    m1 = [sb(f"m1{i}", [128, 512], F32) for i in range(NB)]
    M1 = [Buf(f"m1{i}") for i in range(NB)]
    m2 = [sb(f"m2{i}", [128, 512], F32) for i in range(NB)]
    M2 = [Buf(f"m2{i}") for i in range(NB)]
    xt = [sb(f"xt{i}", [128, D], F32) for i in range(2)]
    XTl = [Buf(f"xt{i}") for i in range(2)]
    tlx = [P.new_dma_tl(f"mxt{i}") for i in range(2)]
    vt = [sb(f"vt{i}", [128, D], F32) for i in range(2)]
    VT = [Buf(f"vt{i}") for i in range(2)]
    ht, HT = vt, VT
    hb = [sb(f"hb{i}", [128, D], BF16) for i in range(2)]
    HB = [Buf(f"hb{i}") for i in range(2)]
    stats = [sb(f"stats{i}", [128, 2, 6], F32) for i in range(2)]
    STATS = [Buf(f"stats{i}") for i in range(2)]
    mv = [sb(f"mv{i}", [128, 4], F32) for i in range(2)]
    MV = [Buf(f"mv{i}") for i in range(2)]
    C.HSCR = [Buf(f"hscr{t}") for t in range(NT)]
    kc = {"k": 0}

    def gates(tb):
        ms = tb % 2
        tsl = slice(tb * 512, (tb + 1) * 512)
        XTB = C.XT[tb * 4:(tb + 1) * 4]
        for j in range(8):
            k = kc["k"]
            ws = C.wk % 3
            C.wk += 1
            bs = k % NB
            kc["k"] += 1
            P.dma("pool", wg[ws][:], dr["pk_mg"][j], writes=[WG[ws]], tl=tlg[ws])
            pga, PGA = getps(C)
            pgb, PGB = getps(C)
            for c in range(8):
                mm(C, pga[:, :], wg[ws][:, c, 0:128], C.xT[:, c, tsl], c == 0, c == 7, [WG[ws]] + XTB, [PGA])
            for c in range(8):
                mm(C, pgb[:, :], wg[ws][:, c, 128:256], C.xT[:, c, tsl], c == 0, c == 7, [WG[ws]] + XTB, [PGB])
            actf(C, th[bs][:, 0, :], pga[:, :], AF.Tanh, [PGA], [TH[bs]], scale=0.5)
            actf(C, th[bs][:, 1, :], pgb[:, :], AF.Tanh, [PGB], [TH[bs]], scale=0.5)
            pa, PA = getps(C)
            pbB, PBB_ = getps(C)
            for c in range(4):
                mm(C, pa[:, :], wA[:, c, j * 128:(j + 1) * 128], C.oAT[:, c, tsl], c == 0, c == 3,
                   [WA] + C.OAT[tb * 4:(tb + 1) * 4], [PA])
            for c in range(4):
                mm(C, pbB[:, :], wB[:, c, j * 128:(j + 1) * 128], C.oBT[:, c, tsl], c == 0, c == 3,
                   [WB] + C.OBT[tb * 4:(tb + 1) * 4], [PBB_])
            stt(C, "dve", m1[bs][:, :], th[bs][:, 0, :], 1.0, pa[:, :], ALU.add, ALU.mult, [TH[bs], PA], [M1[bs]])
            stt(C, "dve", m2[bs][:, :], th[bs][:, 1, :], 1.0, pbB[:, :], ALU.add, ALU.mult, [TH[bs], PBB_], [M2[bs]])
            tt(C, "dve", mixT[ms][:, j, :], m1[bs][:, :], m2[bs][:, :], ALU.add, [M1[bs], M2[bs]], [MIXT[ms]])
            yield

    def epi(tb):
        ms = tb % 2
        for t4 in range(4):
            t = tb * 4 + t4
            s2 = t % 2
            P.dma("sp", xt[s2][:, :], dr["x"][t * 128:(t + 1) * 128, :], writes=[XTl[s2]], tl=tlx[s2])
            for n in range(2):
                py, PY = getps(C)
                for j in range(8):
                    mm(C, py[:, :], mixT[ms][:, j, t4 * 128:(t4 + 1) * 128], wo[:, j, n * 512:(n + 1) * 512], j == 0, j == 7,
                       [MIXT[ms], WO], [PY])
                stt(C, "dve", vt[s2][:, n * 512:(n + 1) * 512], xt[s2][:, n * 512:(n + 1) * 512], DN_ALPHA, py[:, :],
                    ALU.mult, ALU.add, [XTl[s2], PY], [VT[s2]])
            layer_norm_tile(C, vt[s2], VT[s2], stats[s2], STATS[s2], mv[s2], MV[s2], ln1g, ln1b, vt[s2], VT[s2])
            P.dma("sp", C.hscr[t * 128:(t + 1) * 128, :], ht[s2][:, :], reads=[HT[s2]], writes=[C.HSCR[t]])
            cpy(C, "act", hb[s2][:, :], ht[s2][:, :], [HT[s2]], [HB[s2]])
            ptr, PTR = getps(C)
            ptrv = ptr.bitcast(BF16)
            for c in range(8):
                P.op("pe", lambda e, o=ptrv[:, c * 128:(c + 1) * 128], i=hb[s2][:, c * 128:(c + 1) * 128]: e.transpose(o, i, K.ident_b[:, :]),
                     reads=[HB[s2], CONST], writes=[PTR])
            cpy(C, "act", C.xT[:, :, t * 128:(t + 1) * 128], ptrv[:, :].rearrange("p (c q) -> p c q", c=8), [PTR], [C.XT[t]])
            yield

    for j_, _ in enumerate(gates(0)):
        if j_ == 1:
            load_w(C, wo[:], "w_out", 0, D, [WO])
            actf(C, wo[:].rearrange("p c n -> p (c n)"), wo[:].rearrange("p c n -> p (c n)"), AF.Copy, [WO], [WO], scale=0.5)
    for tb in range(4):
        A = gates(tb + 1) if tb + 1 < 4 else iter(())
        B = epi(tb)
        a_done = b_done = False
        while not (a_done and b_done):
            for _ in range(2):
                if not a_done:
                    try:
                        next(A)
                    except StopIteration:
                        a_done = True
            if not b_done:
                try:
                    next(B)
                except StopIteration:
                    b_done = True


def phase_ffn(C, ph):
    import os
    nc, P, dr, K = C.nc, C.P, C.dr, C.K
    CONST = C.CONST
    hT, HTB = C.xT, C.XT

    def sb(name, shape, dtype):
        return ph.enter_context(nc.sbuf_tensor("f_" + name, list(shape), dtype))

    ln2g = cload(C, sb, "ln2g", "p_ln2g", [128, D])
    ln2b = cload(C, sb, "ln2b", "p_ln2b", [128, D])
    fconv = cload(C, sb, "fconv", "p_fconv", [128, 44, 3])
    wd = sb("wd", [128, 22, D], BF16)
    WD = Buf("wd")
    QW = 512
    aT = [sb(f"aT{i}", [128, 22, QW], BF16) for i in range(2)]
    AT = [[Buf(f"aT{q}_{i}") for i in range(22)] for q in range(2)]
    wu, WU, tlu = C.wst, C.WST, C.wst_tl
    raw = [[sb(f"raw{w}{i}", [128, QW + 2], F32) for i in range(2)] for w in range(2)]
    RAW = [[Buf(f"raw{w}{i}") for i in range(2)] for w in range(2)]
    acc = [[sb(f"acc{w}{i}", [128, QW], F32) for i in range(2)] for w in range(2)]
    ACC = [[Buf(f"acc{w}{i}") for i in range(2)] for w in range(2)]
    halo = sb("halo", [128, 44, 2], F32)
    HALO = [Buf(f"halo{c}") for c in range(44)]
    hres = [sb(f"hres{i}", [128, D], F32) for i in range(2)]
    HRES = [Buf(f"hres{i}") for i in range(2)]
    tlh = [P.new_dma_tl(f"fhr{i}") for i in range(2)]
    vt = [sb(f"vt{i}", [128, D], F32) for i in range(2)]
    VT = [Buf(f"vt{i}") for i in range(2)]
    stats = [sb(f"stats{i}", [128, 2, 6], F32) for i in range(2)]
    STATS = [Buf(f"stats{i}") for i in range(2)]
    mv = [sb(f"mv{i}", [128, 4], F32) for i in range(2)]
    MV = [Buf(f"mv{i}") for i in range(2)]
    OUTB = [Buf(f"out{t}") for t in range(NT)]
    kc = {"k": 0}

    def d1(q):
        T0 = q * QW
        qs = q % 2
        for i in range(22):
            k = kc["k"]
            ws = C.wk % 3
            C.wk += 1
            rs = k % 2
            kc["k"] += 1
            P.dma("pool", wu[ws][:], dr["pk_up"][i], writes=[WU[ws]], tl=tlu[ws])
            for w in range(2):
                ck = i + 22 * w
                if q == 0:
                    P.op("dve", lambda e, o=raw[w][rs][:, 0:2]: e.memset(o, 0.0), writes=[RAW[w][rs]])
                else:
                    cpy(C, "dve", raw[w][rs][:, 0:2], halo[:, ck, :], [HALO[ck]], [RAW[w][rs]])
                pu, PU = getps(C)
                for c in range(8):
                    mm(C, pu[:, :], wu[ws][:, c, w * 128:(w + 1) * 128], hT[:, c, T0:T0 + QW],
                       c == 0, c == 7, [WU[ws]] + HTB[T0 // 128:T0 // 128 + 4], [PU])
                cpy(C, "act", raw[w][rs][:, 2:2 + QW], pu[:, :], [PU], [RAW[w][rs]])
                if q < 3:
                    cpy(C, "dve", halo[:, ck, :], raw[w][rs][:, QW:QW + 2], [RAW[w][rs]], [HALO[ck]])
                if FFN_TAP_ACT:
                    actf(C, acc[w][rs][:, :], raw[w][rs][:, 2:QW + 2], AF.Copy, [RAW[w][rs], CONST], [ACC[w][rs]], scale=fconv[:, ck, 2:3])
                else:
                    ts(C, "dve", acc[w][rs][:, :], raw[w][rs][:, 2:QW + 2], fconv[:, ck, 2:3], ALU.mult, [RAW[w][rs], CONST], [ACC[w][rs]])
                for j in (1, 0):
                    stt(C, "dve", acc[w][rs][:, :], raw[w][rs][:, j:QW + j], fconv[:, ck, j:j + 1], acc[w][rs][:, :],
                        ALU.mult, ALU.add, [RAW[w][rs], ACC[w][rs], CONST], [ACC[w][rs]])
            actf(C, acc[0][rs][:, :], acc[0][rs][:, :], AF.Silu, [ACC[0][rs]], [ACC[0][rs]])
            tt(C, "dve", aT[qs][:, i, :], acc[0][rs][:, :], acc[1][rs][:, :], ALU.mult, [ACC[0][rs], ACC[1][rs]], [AT[qs][i]])
            if q == 0 and 3 <= i < 14:
                i2 = (i - 3) * 2
                src = dr["w_down"][i2 * 128:(i2 + 2) * 128, :].rearrange("(c p) n -> p c n", p=128)
                P.dma("pool", wd[:, i2:i2 + 2, :], src, writes=[WD])
            yield

    def d2(q):
        qs = q % 2
        for t4 in range(4):
            t = q * 4 + t4
            s2 = t % 2
            P.dma("sp", hres[s2][:, :], C.hscr[t * 128:(t + 1) * 128, :], reads=[C.HSCR[t]], writes=[HRES[s2]], tl=tlh[s2])
            for n in range(2):
                pf, PF = getps(C)
                for i in range(22):
                    mm(C, pf[:, :], aT[qs][:, i, t4 * 128:(t4 + 1) * 128], wd[:, i, n * 512:(n + 1) * 512], i == 0, i == 21,
                       [AT[qs][i], WD], [PF])
                stt(C, "dve", vt[s2][:, n * 512:(n + 1) * 512], hres[s2][:, n * 512:(n + 1) * 512], DN_ALPHA, pf[:, :],
                    ALU.mult, ALU.add, [HRES[s2], PF], [VT[s2]])
            layer_norm_tile(C, vt[s2], VT[s2], stats[s2], STATS[s2], mv[s2], MV[s2], ln2g, ln2b, vt[s2], VT[s2])
            P.dma("sp", C.out_d[t * 128:(t + 1) * 128, :], vt[s2][:, :], reads=[VT[s2]], writes=[OUTB[t]])
            C.final_bufs.append(OUTB[t])
            yield

    for _ in d1(0):
        pass
    for q in range(4):
        A = d1(q + 1) if q + 1 < 4 else iter(())
        B = d2(q)
        a_done = b_done = False
        while not (a_done and b_done):
            for _ in range(FFN_RATIO):
                if not a_done:
                    try:
                        next(A)
                    except StopIteration:
                        a_done = True
            if not b_done:
                try:
                    next(B)
                except StopIteration:
                    b_done = True


_CACHE = {}


def kernel(**inputs):
    inp = {k: np.asarray(v) for k, v in inputs.items()}
    if "nc" not in _CACHE:
        _CACHE["nc"] = build()[0]
    nc = _CACHE["nc"]
    base = {k: np.ascontiguousarray(inp[k][0], dtype=np.float32) for k in WEIGHT_SHAPES}
    base.update(host_consts())
    base.update(host_params(inp))
    base.update(host_packed(inp))
    n = inp["x"].shape[0]
    in_maps = [dict(base, x=np.ascontiguousarray(inp["x"][b], dtype=np.float32)) for b in range(n)]
    res = run_bass_kernel_spmd(nc, in_maps, core_ids=list(range(n)))
    return np.stack([np.asarray(r["out"], dtype=np.float32) for r in res.results], 0)
```

```python
import math
from contextlib import ExitStack
import numpy as np
import concourse.bass as bass
import concourse.mybir as mybir
from concourse.bass_utils import run_bass_kernel_spmd

F32 = mybir.dt.float32
BF16 = mybir.dt.bfloat16
AF = mybir.ActivationFunctionType
ALU = mybir.AluOpType
AX = mybir.AxisListType

S = 2048
D = 1024
NT = 16
D_IN = 5408
D_FF = 2816
DN_ALPHA = 2.0 ** 0.25
LN_EPS = 1e-5
RMS_EPS = 1e-6
NEG = -30000.0
import os as _os
FFN_RATIO = int(_os.environ.get("FFN_RATIO", "8"))
PE_FIX = float(_os.environ.get("PE_FIX", "45"))
PE_COL = float(_os.environ.get("PE_COL", "0.45"))
FFN_TAP_ACT = int(_os.environ.get("FFN_TAP_ACT", "1"))

C_GQ, C_GK, C_GV, C_GZ, C_GB, C_GA = 0, 512, 1024, 1536, 2048, 2052
C_NQ, C_NKC, C_NVC, C_NKS, C_NVS, C_NKW, C_NVW, C_NG, C_MG = 2056, 2568, 2696, 2824, 2952, 3080, 3208, 3336, 3360

EPOCH = 6000


class Timeline:
    def __init__(self, prog, name, step):
        self.prog = prog
        self.name = name
        self.step = step
        self.count = 0
        self.sems = []

    def sem_for(self, idx):
        ep = (idx - 1) // EPOCH
        while len(self.sems) <= ep:
            self.sems.append(self.prog.new_sem(f"{self.name}_{len(self.sems)}"))
        return self.sems[ep], ((idx - 1) % EPOCH + 1) * self.step

    def next(self):
        self.count += 1
        return self.count


class Buf:
    __slots__ = ("name", "last_write", "reads", "excl", "also", "persist")

    def __init__(self, name, excl=False, persist=False):
        self.name = name
        self.persist = persist
        self.last_write = None
        self.reads = []
        self.excl = excl
        self.also = None


class Op:
    __slots__ = ("idx", "eng", "fn", "tl", "deps", "epoch", "dur", "busy", "pos", "fin", "sched", "final", "bar", "prio")

    def __init__(self, idx, eng, fn, tl, deps, epoch, dur, busy, bar=True):
        self.idx, self.eng, self.fn, self.tl, self.deps = idx, eng, fn, tl, deps
        self.epoch, self.dur, self.busy = epoch, dur, busy
        self.bar = bar
        self.prio = 0
        self.pos = None
        self.fin = None
        self.sched = False
        self.final = False


def _free(ap):
    n = 1
    for d in list(ap.shape)[1:]:
        n *= int(d)
    return n


class Prog:
    ENGS = ("pe", "act", "dve", "pool", "sp")
    WINDOW = int(_os.environ.get("SCHED_WINDOW", "128"))
    SEM_LAT = float(_os.environ.get("SCHED_SEMLAT", "1500"))

    def __init__(self, nc):
        self.nc = nc
        self.stack = ExitStack()
        self.tl = {e: Timeline(self, "c_" + e, 1) for e in self.ENGS}
        self.ops = []
        self.epoch = 0
        self.dma_pool = {}
        self.dma_rr = {}
        self.dma_last = {}
        self.all_dma_tl = []
        self.same_engine_sync = bool(int(_os.environ.get("SAME_ENG_SYNC", "1")))
        self.finals = []
        self.cur_prio = 0

    def new_sem(self, name):
        return self.stack.enter_context(self.nc.semaphore(name))

    def new_dma_tl(self, name):
        t = Timeline(self, "d_" + name, 16)
        self.all_dma_tl.append(t)
        return t

    def _deps(self, reads, writes):
        deps = set()
        for b in reads:
            if b.last_write is not None:
                deps.add(b.last_write)
            if b.excl:
                deps.update(b.reads)
            if b.also:
                deps.update(b.also)
        for b in writes:
            if b.last_write is not None:
                deps.add(b.last_write)
            deps.update(b.reads)
        return deps

    def _mark(self, op, reads, writes):
        for b in reads:
            if b.excl:
                b.last_write = op
                b.reads = []
            else:
                b.reads.append(op)
        for b in writes:
            b.last_write = op
            b.reads = []

    def op(self, eng, fn, reads=(), writes=(), cost=150.0):
        deps = self._deps(reads, writes)
        bar = not all(b.persist for b in list(reads) + list(writes))
        o = Op(len(self.ops), eng, fn, self.tl[eng], deps, self.epoch, cost, cost, bar)
        o.prio = self.cur_prio
        self.ops.append(o)
        self._mark(o, reads, writes)
        return o

    def dma(self, eng, out, in_, reads=(), writes=(), tl=None, **kw):
        if tl is None:
            if eng not in self.dma_pool:
                self.dma_pool[eng] = [self.new_dma_tl(f"{eng}{i}") for i in range(6)]
            pool = self.dma_pool[eng]
            i = self.dma_rr.get(eng, 0)
            self.dma_rr[eng] = (i + 1) % len(pool)
            tl = pool[i]
        deps = self._deps(reads, writes)
        if tl in self.dma_last:
            deps.add(self.dma_last[tl])
        nbytes = _free(out) * int(out.shape[0]) * 4
        dur = 2200.0 + nbytes / 150.0

        def fn(e, out=out, in_=in_, kw=kw):
            return e.dma_start(out=out, in_=in_, **kw)

        bar = not all(b.persist for b in list(reads) + list(writes))
        o = Op(len(self.ops), eng, fn, tl, deps, self.epoch, dur, 150.0 if eng == "sp" else 400.0, bar)
        self.ops.append(o)
        self.dma_last[tl] = o
        self._mark(o, reads, writes)
        return o

    def barrier(self):
        self.epoch += 1

    def final_wait(self, eng, bufs):
        deps = self._deps(bufs, bufs)
        self.finals.append((eng, deps))

    def schedule(self):
        pend = {e: [] for e in self.ENGS}
        for o in self.ops:
            pend[o.eng].append(o)
        head = {e: 0 for e in self.ENGS}
        tfree = {e: 0.0 for e in self.ENGS}
        order = {e: [] for e in self.ENGS}
        n_left = len(self.ops)
        ep_left = {}
        for o in self.ops:
            ep_left[o.epoch] = ep_left.get(o.epoch, 0) + 1
        ep_end = {-1: 0.0}
        cur_ep = 0
        ep_fin = 0.0
        while n_left:
            while ep_left.get(cur_ep, 0) == 0:
                ep_end[cur_ep] = ep_fin
                cur_ep += 1
            best = None
            for e in self.ENGS:
                lst = pend[e]
                h = head[e]
                while h < len(lst) and lst[h].sched:
                    h += 1
                head[e] = h
                cnt = 0
                i = h
                while i < len(lst) and cnt < self.WINDOW:
                    o = lst[i]
                    i += 1
                    if o.sched:
                        continue
                    cnt += 1
                    if o.epoch != cur_ep:
                        if o.bar or o.epoch < cur_ep:
                            continue
                        rdy = 0.0
                    else:
                        rdy = ep_end[cur_ep - 1] if o.bar else 0.0
                    ok = True
                    for d in o.deps:
                        if not d.sched:
                            ok = False
                            break
                        f = d.fin + (0.0 if d.eng == e and d.tl is self.tl[e] else self.SEM_LAT)
                        if f > rdy:
                            rdy = f
                    if not ok:
                        continue
                    st = rdy if rdy > tfree[e] else tfree[e]
                    key = (st, -o.prio, o.idx)
                    if best is None or key < best[0]:
                        best = (key, e, o, st)
            if best is None:
                raise RuntimeError("scheduler: no candidate")
            _, e, o, st = best
            o.sched = True
            o.fin = st + o.dur
            tfree[e] = st + o.busy
            order[e].append(o)
            n_left -= 1
            ep_left[o.epoch] -= 1
            if o.fin > ep_fin:
                ep_fin = o.fin
        self.est_ns = ep_fin
        return order

    def emit(self):
        nc = self.nc
        order = self.schedule()
        for e in self.ENGS:
            for o in order[e]:
                o.pos = o.tl.next()
        streams = {}
        for e in self.ENGS:
            known = {}
            out = []
            mytl = self.tl[e]
            last_ep = 0
            done_pos = {}
            for o in order[e]:
                waits = []
                if o.bar and o.epoch > last_ep:
                    for tl, p in self._epoch_max(o.epoch).items():
                        if tl is mytl:
                            continue
                        if known.get(tl, 0) < p:
                            known[tl] = p
                            waits.append(tl.sem_for(p))
                    last_ep = o.epoch
                for d in o.deps:
                    if d.tl is mytl and (e in ("pe", "sp") or not self.same_engine_sync):
                        continue
                    if known.get(d.tl, 0) >= d.pos:
                        continue
                    known[d.tl] = d.pos
                    waits.append(d.tl.sem_for(d.pos))
                sem, _ = o.tl.sem_for(o.pos)
                out.append((waits, o.fn, sem, o.tl.step))
            streams[e] = out
        for (e, deps) in self.finals:
            waits = []
            best = {}
            for d in deps:
                if best.get(d.tl, 0) < d.pos:
                    best[d.tl] = d.pos
            for tl, p in best.items():
                waits.append(tl.sem_for(p))
            streams[e].append((waits, None, None, 0))
        self.streams = streams

        def replay(name):
            def body(e):
                for (waits, fn, sem, inc) in streams[name]:
                    for (s_, v) in waits:
                        e.wait_ge(s_, v)
                    if fn is not None:
                        fn(e).then_inc(sem, inc)
            return body

        with nc.Block() as block:
            block.tensor(replay("pe"))
            block.scalar(replay("act"))
            block.vector(replay("dve"))
            block.gpsimd(replay("pool"))
            block.sync(replay("sp"))

    def _epoch_max(self, epoch):
        if not hasattr(self, "_epmax"):
            self._epmax = {}
        if epoch not in self._epmax:
            m = {}
            for o in self.ops:
                if o.epoch < epoch and m.get(o.tl, 0) < o.pos:
                    m[o.tl] = o.pos
            self._epmax[epoch] = m
        return self._epmax[epoch]


def host_consts():
    c = {}
    i = np.arange(128)
    same = (i[:, None] // 64) == (i[None, :] // 64)
    c["c_ident"] = np.eye(128, dtype=np.float32)
    c["c_tri"] = ((i[:, None] <= i[None, :]) & same).astype(np.float32)
    c["c_blk"] = same.astype(np.float32)
    c["c_nega"] = np.where((i[:, None] > i[None, :]) & same, 0.0, NEG).astype(np.float32)
    c["c_negq"] = np.where((i[None, :] >= i[:, None]) & same, 0.0, NEG).astype(np.float32)
    sel = np.zeros((4, 4, 128), np.float32)
    for h in range(4):
        sel[h, h, :] = 1.0
    c["c_sel4"] = sel
    sel12 = np.zeros((12, 4, 128), np.float32)
    for h in range(4):
        sel12[3 * h:3 * h + 3, h, :] = 1.0
    c["c_sel12"] = sel12
    sr = np.zeros((128, 2, 128), np.float32)
    sr[0, 0, :] = 1.0
    sr[64, 1, :] = 1.0
    c["c_selrow"] = sr
    n = np.arange(127)
    t = np.arange(S)
    c["c_cmpmask"] = np.concatenate([((n[:, None] * 16 + 31) <= t[None, :]).astype(np.float32),
                                     np.zeros((1, S), np.float32)], 0)
    starts = n * 16
    jb = np.arange(32) * 64
    ov = ((starts[:, None] < jb[None] + 64) & (starts[:, None] + 32 > jb[None])).astype(np.float32)
    c["c_overlap"] = np.concatenate([ov, np.zeros((1, 32), np.float32)], 0)
    c["c_causal"] = (i[:, None] <= i[None, :]).astype(np.float32)
    c["c_anti"] = (i[:, None] > i[None, :]).astype(np.float32)
    E = np.zeros((32, 16, 128), np.float32)
    for kt in range(16):
        for k in range(128):
            E[2 * kt + k // 64, kt, k] = 1.0
    c["c_expand"] = E
    fb = np.zeros((128, 8, 32), np.float32)
    for qi in range(8):
        qt = 8 + qi
        pos = qt * 128 + i
        cur = pos // 64
        jj = np.arange(32)
        causal = (jj[None] * 64) <= pos[:, None]
        b_ = np.where(causal, 0.0, -1e30)
        b_ = np.where(jj[None] == 0, 1e9, b_)
        b_ = np.where(jj[None] == cur[:, None], 2e9, b_)
        b_ = np.where(jj[None] == cur[:, None] - 1, 3e9, b_)
        fb[:, qi, :] = b_
    c["c_forced"] = fb
    return c


CONST_SHAPES = {k: v.shape for k, v in host_consts().items()}


def host_params(inp):
    p = {}
    f = np.float32
    p["p_gconv"] = np.ascontiguousarray(inp["gdn_conv_w"][0].reshape(4, 12, 128).transpose(2, 1, 0)).astype(f)
    p["p_alog"] = np.ascontiguousarray(np.broadcast_to(inp["gdn_a_log"][0][None, None, :], (128, NT, 4))).astype(f)
    p["p_dtb"] = np.ascontiguousarray(np.broadcast_to(inp["gdn_dt_bias"][0][None, None, :], (128, NT, 4))).astype(f)
    p["p_gnw"] = np.ascontiguousarray(np.broadcast_to(inp["gdn_norm_w"][0][None, None, :], (128, 4, 128))).astype(f)
    p["p_poskT"] = np.ascontiguousarray(np.concatenate([inp["cmp_pos_k"][0].T] * 2, 0)).astype(f)
    p["p_posvT"] = np.ascontiguousarray(np.concatenate([inp["cmp_pos_v"][0].T] * 2, 0)).astype(f)
    p["p_ln1g"] = np.ascontiguousarray(np.broadcast_to(inp["ln1_g"][0][None, :], (128, D))).astype(f)
    p["p_ln1b"] = np.ascontiguousarray(np.broadcast_to(inp["ln1_b"][0][None, :], (128, D))).astype(f)
    p["p_ln2g"] = np.ascontiguousarray(np.broadcast_to(inp["ln2_g"][0][None, :], (128, D))).astype(f)
    p["p_ln2b"] = np.ascontiguousarray(np.broadcast_to(inp["ln2_b"][0][None, :], (128, D))).astype(f)
    p["p_fconv"] = np.ascontiguousarray(inp["ffn_conv_w"][0].reshape(3, 44, 128).transpose(2, 1, 0)).astype(f)
    return p


def _pk(w, cols):
    sub = w[:, cols]
    return np.ascontiguousarray(sub.reshape(8, 128, sub.shape[1]).transpose(1, 0, 2))


def host_packed(inp):
    f = np.float32
    w_in = np.asarray(inp["w_in"][0], dtype=f)
    w_up = np.asarray(inp["w_up"][0], dtype=f)
    p = {}
    ar = np.arange
    p["pk_up"] = np.stack([_pk(w_up, np.concatenate([ar(i * 128, (i + 1) * 128), ar(D_FF + i * 128, D_FF + (i + 1) * 128)]))
                           for i in range(22)], 0)
    p["pk_mg"] = np.stack([_pk(w_in, np.concatenate([ar(C_MG + j * 128, C_MG + (j + 1) * 128),
                                                     ar(C_MG + 1024 + j * 128, C_MG + 1024 + (j + 1) * 128)]))
                           for j in range(8)], 0)
    p["pk_g"] = np.stack([_pk(w_in, ar(ck * 128, (ck + 1) * 128)) for ck in range(12)], 0)
    jobs = [ar(C_NQ + i * 128, C_NQ + (i + 1) * 128) for i in range(4)]
    jobs += [ar(C_NKS, C_NKS + 128), ar(C_NKW, C_NKW + 128), ar(C_NKC, C_NKC + 128), ar(C_NVC, C_NVC + 128)]
    p["pk_n128"] = np.stack([_pk(w_in, c) for c in jobs], 0)
    return p


PACKED_SHAPES = {"pk_up": (22, 128, 8, 256), "pk_mg": (8, 128, 8, 256), "pk_g": (12, 128, 8, 128),
                 "pk_n128": (8, 128, 8, 128)}

PARAM_SHAPES = {"p_gconv": (128, 12, 4), "p_alog": (128, NT, 4), "p_dtb": (128, NT, 4), "p_gnw": (128, 4, 128),
                "p_poskT": (128, 32), "p_posvT": (128, 32), "p_ln1g": (128, D), "p_ln1b": (128, D),
                "p_ln2g": (128, D), "p_ln2b": (128, D), "p_fconv": (128, 44, 3)}

WEIGHT_SHAPES = {"w_in": (D, D_IN), "cmp_w1_k": (2048, 256), "cmp_w2_k": (256, 64), "cmp_w1_v": (2048, 256),
                 "cmp_w2_v": (256, 64), "w_branch_gdn": (512, D), "w_branch_nsa": (512, D), "w_out": (D, D),
                 "w_up": (D, 2 * D_FF), "w_down": (D_FF, D)}


class Ctx:
    pass


def build(taps=(), phases=("gdn", "nsa", "mix", "ffn")):
    nc = bass.Bass("TRN2", target_bir_lowering=False)
    dr = {}
    dr["x"] = nc.dram_tensor("x", [S, D], F32, kind="ExternalInput").ap()
    for k, shp in list(WEIGHT_SHAPES.items()) + list(CONST_SHAPES.items()) + list(PARAM_SHAPES.items()) + list(PACKED_SHAPES.items()):
        dr[k] = nc.dram_tensor(k, list(shp), F32, kind="ExternalInput").ap()
    out_d = nc.dram_tensor("out", [S, D], F32, kind="ExternalOutput").ap()
    hscr = nc.dram_tensor("hscr", [S, D], F32).ap()
    P = Prog(nc)
    C = Ctx()
    C.nc, C.P, C.dr, C.out_d, C.hscr, C.taps = nc, P, dr, out_d, hscr, set(taps)
    C.tap_out = {}
    with P.stack:
        C.ps = [nc.alloc_psum_tensor(f"ps{i}", [128, 512], F32) for i in range(8)]
        C.PB = [Buf(f"ps{i}", excl=True, persist=True) for i in range(8)]
        C.ps_rr = 0
        C.ps_reserved = set()
        C.xT = nc.alloc_sbuf_tensor("xT", [128, 8, S], BF16)
        C.XT = [Buf(f"xT{t}", persist=True) for t in range(NT)]
        C.OAT = [Buf(f"oAT{t}", persist=True) for t in range(NT)]
        C.OBT = [Buf(f"oBT{t}", persist=True) for t in range(NT)]
        C.wst = [nc.alloc_sbuf_tensor(f"wst{i}", [128, 8, 256], BF16) for i in range(3)]
        C.WST = [Buf(f"wst{i}", persist=True) for i in range(3)]
        C.wst_tl = [P.new_dma_tl(f"wst{i}") for i in range(3)]
        C.wk = 0
        C.CONST = Buf("const")
        C.tl_const = {"sp": P.new_dma_tl("const_sp"), "pool": P.new_dma_tl("const_pool")}
        load_consts(C)
        with ExitStack() as ab:
            C.oAT = ab.enter_context(nc.sbuf_tensor("oAT", [128, 4, S], BF16))
            with ExitStack() as ph:
                phase_x(C, ph)
                if "gdn" in phases:
                    phase_gdn(C, ph)
            P.barrier()
            C.oBT = ab.enter_context(nc.sbuf_tensor("oBT", [128, 4, S], BF16))
            if "nsa" in phases:
                with ExitStack() as ph:
                    phase_nsa(C, ph)
                P.barrier()
            if "mix" in phases:
                with ExitStack() as ph:
                    phase_mix(C, ph)
                P.barrier()
        if "ffn" in phases:
            with ExitStack() as ph:
                phase_ffn(C, ph)
        P.final_wait("sp", C.final_bufs)
        P.emit()
    return nc, C


def getps(C):
    while True:
        i = C.ps_rr
        C.ps_rr = (i + 1) % 8
        if i not in C.ps_reserved:
            return C.ps[i], C.PB[i]


def reserve_ps(C):
    while True:
        i = C.ps_rr
        C.ps_rr = (i + 1) % 8
        if i not in C.ps_reserved:
            C.ps_reserved.add(i)
            return i, C.ps[i], C.PB[i]


def const_dma(C, eng, out, in_):
    P = C.P
    tl = C.tl_const[eng]
    nbytes = _free(out) * int(out.shape[0]) * 4

    def fn(e, out=out, in_=in_):
        return e.dma_start(out=out, in_=in_)

    o = Op(len(P.ops), eng, fn, tl, set(), P.epoch, 2200.0 + nbytes / 150.0, 150.0 if eng == "sp" else 400.0)
    P.ops.append(o)
    if C.CONST.also is None:
        C.CONST.also = []
    C.CONST.also.append(o)


def cload(C, nc_alloc, name, key, shape, dtype=F32):
    t = nc_alloc(name, list(shape), dtype)
    const_dma(C, "pool" if dtype != F32 else "sp", t[:], C.dr[key])
    return t


def load_consts(C):
    nc = C.nc
    K = Ctx()
    C.K = K

    def al(name, shape, dtype):
        return nc.alloc_sbuf_tensor(name, shape, dtype)

    K.ident_f = cload(C, al, "k_identf", "c_ident", [128, 128])
    K.ident_b = cload(C, al, "k_identb", "c_ident", [128, 128], BF16)
    K.ones_b = nc.alloc_sbuf_tensor("k_onesb", [128, 128], BF16)
    C.ONES = Buf("ones")
    C.P.op("dve", lambda e: e.memset(K.ones_b[:], 1.0), writes=[C.ONES])


def tap(C, name, ap, shape, rd):
    if name not in C.taps:
        return
    t = C.nc.dram_tensor("tap_" + name, list(shape), ap.dtype, kind="ExternalOutput").ap()
    b = Buf("tap_" + name)
    C.P.dma("sp", t, ap, reads=rd, writes=[b])
    C.final_bufs.append(b)
    C.tap_out[name] = "tap_" + name


def phase_x(C, phx):
    nc, P, dr, K = C.nc, C.P, C.dr, C.K
    C.final_bufs = []
    xb = [phx.enter_context(nc.sbuf_tensor(f"xb{i}", [128, D], BF16)) for i in range(2)]
    XB = [Buf(f"xb{i}") for i in range(2)]
    tls = [P.new_dma_tl(f"xb{i}") for i in range(2)]
    for tt in range(NT):
        s = tt % 2
        P.dma("pool", xb[s][:, :], dr["x"][tt * 128:(tt + 1) * 128, :], writes=[XB[s]], tl=tls[s])
        pb, PBb = getps(C)
        pbv = pb.bitcast(BF16)
        for c in range(8):
            P.op("pe", lambda e, o=pbv[:, c * 128:(c + 1) * 128], i=xb[s][:, c * 128:(c + 1) * 128]:
                 e.transpose(o, i, K.ident_b[:, :]), reads=[XB[s], C.CONST], writes=[PBb])
        o = C.xT[:, :, tt * 128:(tt + 1) * 128]
        i = pbv[:, :].rearrange("p (c t) -> p c t", c=8)
        if tt % 2 == 0:
            P.op("act", lambda e, o=o, i=i: e.copy(o, i), reads=[PBb], writes=[C.XT[tt]])
        else:
            P.op("dve", lambda e, o=o, i=i: e.tensor_copy(o, i), reads=[PBb], writes=[C.XT[tt]])


def mm(C, out, lhsT, rhs, start, stop, rd, wr):
    C.P.op("pe", lambda e: e.matmul(out, lhsT, rhs, start=start, stop=stop), reads=rd, writes=wr,
           cost=PE_FIX + PE_COL * _free(rhs))


def actf(C, out, in_, func, rd, wr, bias=None, scale=None):
    kw = {}
    if bias is not None:
        kw["bias"] = bias
    if scale is not None:
        kw["scale"] = scale
    C.P.op("act", lambda e: e.activation(out, in_, func, **kw), reads=rd, writes=wr, cost=220.0 + 0.75 * _free(out))


def cpy(C, eng, out, in_, rd, wr):
    if eng == "act":
        C.P.op("act", lambda e: e.copy(out, in_), reads=rd, writes=wr, cost=220.0 + 0.75 * _free(out))
    else:
        C.P.op(eng, lambda e: e.tensor_copy(out, in_), reads=rd, writes=wr, cost=(120.0 + 1.05 * _free(out)) * (6.0 if eng == "pool" else 1.0))


def tt(C, eng, out, in0, in1, op, rd, wr):
    C.P.op(eng, lambda e: e.tensor_tensor(out, in0, in1, op), reads=rd, writes=wr, cost=(120.0 + 1.1 * _free(out)) * (6.0 if eng == "pool" else 1.0))


def ts(C, eng, out, in0, s1, op0, rd, wr, s2=None, op1=None):
    if op1 is None:
        C.P.op(eng, lambda e: e.tensor_scalar(out, in0, s1, None, op0), reads=rd, writes=wr, cost=(120.0 + 1.0 * _free(out)) * (6.0 if eng == "pool" else 1.0))
    else:
        C.P.op(eng, lambda e: e.tensor_scalar(out, in0, s1, s2, op0, op1), reads=rd, writes=wr, cost=(120.0 + 1.0 * _free(out)) * (6.0 if eng == "pool" else 1.0))


def stt(C, eng, out, in0, scalar, in1, op0, op1, rd, wr):
    C.P.op(eng, lambda e: e.scalar_tensor_tensor(out, in0, scalar, in1, op0, op1), reads=rd, writes=wr, cost=120.0 + 1.4 * _free(out))


def load_w(C, dst, key, col0, ncols, wr, tl=None, row0=0, nrow_chunks=8):
    src = C.dr[key][row0:row0 + 128 * nrow_chunks, col0:col0 + ncols].rearrange("(c p) n -> p c n", p=128)
    C.P.dma("pool", dst, src, writes=wr, tl=tl)


def bc_last(ap, n):
    shp = list(ap.shape)
    return ap.unsqueeze(len(shp)).to_broadcast(shp + [n])


def bc_mid(ap, n):
    shp = list(ap.shape)
    return ap.unsqueeze(1).to_broadcast([shp[0], n] + shp[1:])


def phase_gdn(C, ph):
    nc, P, dr, K = C.nc, C.P, C.dr, C.K
    CONST = C.CONST

    def sb(name, shape, dtype):
        return ph.enter_context(nc.sbuf_tensor("g_" + name, list(shape), dtype))

    K.tri = cload(C, sb, "k_tri", "c_tri", [128, 128], BF16)
    K.blk = cload(C, sb, "k_blk", "c_blk", [128, 128], BF16)
    K.nega = cload(C, sb, "k_nega", "c_nega", [128, 128], BF16)
    K.negq = cload(C, sb, "k_negq", "c_negq", [128, 128], BF16)
    K.sel12 = cload(C, sb, "k_sel12", "c_sel12", [12, 4, 128], BF16)
    K.selrow = cload(C, sb, "k_selrow", "c_selrow", [128, 2, 128], BF16)
    K.gconv = cload(C, sb, "k_gconv", "p_gconv", [128, 12, 4])
    K.alog = cload(C, sb, "k_alog", "p_alog", [128, NT, 4])
    K.dtb = cload(C, sb, "k_dtb", "p_dtb", [128, NT, 4])
    K.gnw = cload(C, sb, "k_gnw", "p_gnw", [128, 4, 128])

    qT = sb("qT", [128, 4, S], BF16)
    kT = sb("kT", [128, 4, S], BF16)
    vT = sb("vT", [128, 4, S], BF16)
    QKV = {"q": [Buf(f"qT{h}") for h in range(4)], "k": [Buf(f"kT{h}") for h in range(4)],
           "v": [Buf(f"vT{h}") for h in range(4)]}
    qkvT = {"q": qT, "k": kT, "v": vT}

    wba = sb("wba", [128, 8, 8], BF16)
    WBA = Buf("wba")
    load_w(C, wba[:], "w_in", C_GB, 8, [WBA])
    ba = sb("ba", [128, NT, 8], F32)
    BA = Buf("ba")
    pb, PBb = getps(C)
    for t in range(NT):
        for c in range(8):
            mm(C, pb[:, t * 8:(t + 1) * 8], C.xT[:, c, t * 128:(t + 1) * 128], wba[:, c, :], c == 0, c == 7,
               [C.XT[t], WBA], [PBb])
    cpy(C, "dve", ba[:].rearrange("p t c -> p (t c)"), pb[:, 0:NT * 8], [PBb], [BA])

    import os
    stop = os.environ.get("GDN_STOP", "")
    if stop == "a0":
        return
    ss = sb("ss", [128, NT, 8], F32)
    SSb = Buf("ss")
    ph1 = ExitStack()
    sb_outer = sb

    def sb(name, shape, dtype):
        return ph1.enter_context(nc.sbuf_tensor("g_" + name, list(shape), dtype))

    wq = [t[:, :, 0:128] for t in C.wst]
    WQ = C.WST
    tlw = C.wst_tl
    raw = [sb(f"raw{i}", [128, S + 3], F32) for i in range(2)]
    RAW = [Buf(f"raw{i}") for i in range(2)]
    acc = [sb(f"acc{i}", [128, S], F32) for i in range(2)]
    ACC = [Buf(f"acc{i}") for i in range(2)]
    sq = [sb(f"sq{i}", [128, S], BF16) for i in range(2)]
    SQ = [Buf(f"sq{i}") for i in range(2)]
    for i in range(2):
        P.op("dve", lambda e, o=raw[i][:, 0:3]: e.memset(o, 0.0), writes=[RAW[i]])
    ss_i, ss_pb, SS_PB = reserve_ps(C)
    n_ss = 0
    for ck in range(12):
        which = "qkv"[ck // 4]
        h = ck % 4
        ws = C.wk % 3
        C.wk += 1
        P.dma("pool", wq[ws], dr["pk_g"][ck], writes=[WQ[ws]], tl=tlw[ws])
        rs = ck % 2
        for tb in range(4):
            pb, PBb = getps(C)
            for c in range(8):
                mm(C, pb[:, :], wq[ws][:, c, :], C.xT[:, c, tb * 512:(tb + 1) * 512], c == 0, c == 7,
                   [WQ[ws]] + C.XT[tb * 4:(tb + 1) * 4], [PBb])
            cpy(C, "act", raw[rs][:, 3 + tb * 512:3 + (tb + 1) * 512], pb[:, :], [PBb], [RAW[rs]])
        ceng = "dve"
        ts(C, ceng, acc[rs][:, :], raw[rs][:, 3:S + 3], K.gconv[:, ck, 3:4], ALU.mult, [RAW[rs], CONST], [ACC[rs]])
        for j in (2, 1, 0):
            stt(C, ceng, acc[rs][:, :], raw[rs][:, j:S + j], K.gconv[:, ck, j:j + 1], acc[rs][:, :], ALU.mult, ALU.add,
                [RAW[rs], ACC[rs], CONST], [ACC[rs]])
        dst = qkvT[which][:, h, :]
        actf(C, dst, acc[rs][:, :], AF.Silu, [ACC[rs]], [QKV[which][h]])
        if which in "qk":
            col = (0 if which == "q" else 4) + h
            actf(C, sq[rs][:, :], dst, AF.Square, [QKV[which][h]], [SQ[rs]])
            for t in range(NT):
                mm(C, ss_pb[:, t * 8 + col:t * 8 + col + 1], sq[rs][:, t * 128:(t + 1) * 128], K.ones_b[:, 0:1],
                   True, True, [SQ[rs], C.ONES], [SS_PB])
    cpy(C, "dve", ss[:].rearrange("p t c -> p (t c)"), ss_pb[:, 0:NT * 8], [SS_PB], [SSb])
    C.ps_reserved.discard(ss_i)
    P.barrier()
    ph1.close()
    sb = sb_outer
    tap(C, "qT", qT[:, 0, :], [128, S], QKV["q"])
    tap(C, "vT", vT[:, 1, :], [128, S], QKV["v"])

    if stop == "a1":
        return
    names = ["t1", "t2", "t3", "spa", "spb", "g", "lb", "gc", "gl", "lrk", "lrq", "biasA", "biasQ", "rowQ",
             "s_kbg", "s_kdec", "beta", "s_o", "ea", "ya", "yb"]
    st = {n: sb("st_" + n, [128, NT, 4], F32) for n in names}
    SB_ = {n: Buf("st_" + n) for n in names}

    def softplus(dst, y):
        actf(C, st["t1"][:], st[y][:], AF.Abs, [SB_[y]], [SB_["t1"]])
        actf(C, st["t2"][:], st["t1"][:], AF.Exp, [SB_["t1"]], [SB_["t2"]], scale=-1.0)
        actf(C, st["t3"][:], st["t2"][:], AF.Ln, [SB_["t2"]], [SB_["t3"]], bias=1.0)
        stt(C, "dve", st[dst][:], st[y][:], 0.0, st["t3"][:], ALU.max, ALU.add, [SB_[y], SB_["t3"]], [SB_[dst]])

    tt(C, "dve", st["ya"][:], ba[:, :, 4:8], K.dtb[:], ALU.add, [BA, CONST], [SB_["ya"]])
    softplus("spa", "ya")
    actf(C, st["ea"][:], K.alog[:], AF.Exp, [CONST], [SB_["ea"]])
    stt(C, "dve", st["g"][:], st["spa"][:], -1.0, st["ea"][:], ALU.mult, ALU.mult, [SB_["spa"], SB_["ea"]], [SB_["g"]])
    ts(C, "dve", st["yb"][:], ba[:, :, 0:4], -1.0, ALU.mult, [BA], [SB_["yb"]])
    softplus("spb", "yb")
    ts(C, "dve", st["lb"][:], st["spb"][:], -1.0, ALU.mult, [SB_["spb"]], [SB_["lb"]])
    lrt = sb("lrt", [128, NT, 8], F32)
    LRT = Buf("lrt")
    actf(C, lrt[:], ss[:], AF.Ln, [SSb], [LRT], bias=RMS_EPS)
    ts(C, "dve", st["lrq"][:], lrt[:, :, 0:4], -0.5, ALU.mult, [LRT], [SB_["lrq"]], s2=-0.5 * math.log(128.0), op1=ALU.add)
    ts(C, "dve", st["lrk"][:], lrt[:, :, 4:8], -0.5, ALU.mult, [LRT], [SB_["lrk"]])
    spl_r = sb("spl_r", [128, NT, 4], F32)
    SPLR = Buf("spl_r")

    def split3(name, src):
        x3 = sb("x3_" + name, [128, 3, NT, 4], BF16)
        X3 = Buf("x3_" + name)
        cur, CUR = st[src], SB_[src]
        for k in range(3):
            cpy(C, "dve", x3[:, k, :, :], cur[:], [CUR], [X3])
            if k < 2:
                tt(C, "dve", spl_r[:], cur[:], x3[:, k, :, :], ALU.subtract, [CUR, X3], [SPLR])
                cur, CUR = spl_r, SPLR
        return x3, X3

    g3, G3 = split3("g", "g")
    pb, PBb = getps(C)
    for t in range(NT):
        for k in range(3):
            mm(C, pb[:, t * 4:(t + 1) * 4], K.tri[:, :], g3[:, k, t, :], k == 0, k == 2, [CONST, G3], [PBb])
        for k in range(3):
            mm(C, pb[:, 64 + t * 4:64 + (t + 1) * 4], K.blk[:, :], g3[:, k, t, :], k == 0, k == 2, [CONST, G3], [PBb])
    cpy(C, "dve", st["gc"][:].rearrange("p t c -> p (t c)"), pb[:, 0:64], [PBb], [SB_["gc"]])
    cpy(C, "dve", st["gl"][:].rearrange("p t c -> p (t c)"), pb[:, 64:128], [PBb], [SB_["gl"]])
    tt(C, "dve", st["biasQ"][:], st["lrk"][:], st["gc"][:], ALU.subtract, [SB_["lrk"], SB_["gc"]], [SB_["biasQ"]])
    tt(C, "dve", st["biasA"][:], st["gc"][:], st["lb"][:], ALU.add, [SB_["gc"], SB_["lb"]], [SB_["biasA"]])
    tt(C, "dve", st["biasA"][:], st["biasA"][:], st["lrk"][:], ALU.add, [SB_["biasA"], SB_["lrk"]], [SB_["biasA"]])
    tt(C, "dve", st["rowQ"][:], st["gc"][:], st["lrq"][:], ALU.add, [SB_["gc"], SB_["lrq"]], [SB_["rowQ"]])
    actf(C, st["s_kbg"][:], st["biasA"][:], AF.Exp, [SB_["biasA"]], [SB_["s_kbg"]])
    tt(C, "dve", st["t1"][:], st["biasQ"][:], st["gl"][:], ALU.add, [SB_["biasQ"], SB_["gl"]], [SB_["t1"]])
    actf(C, st["s_kdec"][:], st["t1"][:], AF.Exp, [SB_["t1"]], [SB_["s_kdec"]])
    actf(C, st["beta"][:], st["lb"][:], AF.Exp, [SB_["lb"]], [SB_["beta"]])
    actf(C, st["s_o"][:], st["rowQ"][:], AF.Exp, [SB_["rowQ"]], [SB_["s_o"]])
    tap(C, "g", st["g"][:], [128, NT, 4], [SB_["g"]])
    tap(C, "gc", st["gc"][:], [128, NT, 4], [SB_["gc"]])
    tap(C, "beta", st["beta"][:], [128, NT, 4], [SB_["beta"]])
    tap(C, "lrk", st["lrk"][:], [128, NT, 4], [SB_["lrk"]])

    if stop == "stats":
        return
    eglb = sb("eglb", [128, 2, NT * 4], F32)
    EGLB = Buf("eglb")
    gl3, GL3 = split3("gl", "gl")
    pb, PBb = getps(C)
    for c in range(2):
        for k in range(3):
            mm(C, pb[:, c * 64:(c + 1) * 64], K.selrow[:, c, :], gl3[:, k, :, :].rearrange("p t h -> p (t h)"),
               k == 0, k == 2, [CONST, GL3], [PBb])
    actf(C, eglb[:].rearrange("p c n -> p (c n)"), pb[:, 0:128], AF.Exp, [PBb], [EGLB])
    bq3, BQ3 = split3("bq", "biasQ")
    rq3, RQ3 = split3("rq", "rowQ")

    if stop == "eglb":
        return
    def dbl(name, shape, dtype, n=2):
        return [sb(f"{name}{i}", shape, dtype) for i in range(n)], [Buf(f"{name}{i}") for i in range(n)]

    H4 = [128, 4, 128]
    kbg, KBG = dbl("kbg", H4, BF16)
    kdec, KDEC = dbl("kdec", H4, BF16, 3)
    vb, VB = dbl("vb", H4, BF16)
    expA, EXPA = dbl("expA", H4, F32)
    expQ, EXPQ = dbl("expQ", H4, F32)
    Xs, XS = dbl("X", H4, BF16, 6)
    Ys, YS = dbl("Y", H4, BF16, 6)
    Ws, WS = dbl("W", H4, BF16, 6)
    qkT, QKT = dbl("qkT", H4, BF16, 3)
    u0, U0 = dbl("u0", H4, F32, 3)
    kcdT, KCDT = dbl("kcdT", H4, BF16, 3)
    ut, UT = dbl("u", H4, BF16)
    ot, OT = dbl("o", H4, F32)
    tmpo, TMPO = dbl("tmpo", H4, F32)
    Sst = sb("S", H4, F32)
    Sb = sb("Sb", H4, BF16)
    SST, SBB = Buf("S"), Buf("Sb")
    P.op("dve", lambda e: e.memset(Sst[:], 0.0), writes=[SST])
    P.op("pool", lambda e: e.memset(Sb[:], 0.0), writes=[SBB])
    wz = sb("wz", [128, 8, 512], BF16)
    WZ = Buf("wz")
    load_w(C, wz[:], "w_in", C_GZ, 512, [WZ])
    sz, SZ = dbl("sz", [128, 512], F32)
    osq, OSQ = dbl("osq", H4, F32)
    rr_, RR = dbl("rr", [128, 4], F32)
    oab, OAB = dbl("oab", H4, BF16)
    ident4 = sb("ident4", H4, F32)
    ID4 = Buf("ident4")
    for h in range(4):
        cpy(C, "dve", ident4[:, h, :], K.ident_b[:, :], [CONST], [ID4])
    rowA, ROWA = dbl("rowA", [12, 128], BF16)
    rowQ, ROWQ = dbl("rowQ", [12, 128], BF16)
    r12, R12 = dbl("r12", [128, 2, 4, 3], BF16)

    def prep(p):
        s = p % 2
        s3 = p % 3
        tsl = slice(p * 128, (p + 1) * 128)
        pbk, PBK = getps(C)
        pbkv = pbk.bitcast(BF16)
        for h in range(4):
            P.op("pe", lambda e, o=pbkv[:, h * 128:(h + 1) * 128], i=kT[:, h, tsl]: e.transpose(o, i, K.ident_b[:, :]),
                 reads=[QKV["k"][h], CONST], writes=[PBK])
        kin = pbkv[:, 0:512].rearrange("p (h d) -> p h d", h=4)
        tt(C, "dve", kbg[s][:], kin, bc_last(st["s_kbg"][:, p, :], 128), ALU.mult, [PBK, SB_["s_kbg"]], [KBG[s]])
        tt(C, "dve", kdec[s3][:], kin, bc_last(st["s_kdec"][:, p, :], 128), ALU.mult, [PBK, SB_["s_kdec"]], [KDEC[s3]])
        pbv_, PBV = getps(C)
        pbvv = pbv_.bitcast(BF16)
        for h in range(4):
            P.op("pe", lambda e, o=pbvv[:, h * 128:(h + 1) * 128], i=vT[:, h, tsl]: e.transpose(o, i, K.ident_b[:, :]),
                 reads=[QKV["v"][h], CONST], writes=[PBV])
        vin = pbvv[:, 0:512].rearrange("p (h d) -> p h d", h=4)
        tt(C, "dve", vb[s][:], vin, bc_last(st["beta"][:, p, :], 128), ALU.mult, [PBV, SB_["beta"]], [VB[s]])
        yield
        cpy(C, "dve", r12[s][:, 0, :, :], bq3[:, :, p, :].rearrange("p k h -> p h k"), [BQ3], [R12[s]])
        cpy(C, "dve", r12[s][:, 1, :, :], rq3[:, :, p, :].rearrange("p k h -> p h k"), [RQ3], [R12[s]])
        prw, PRW = getps(C)
        prwv = prw.bitcast(BF16)
        P.op("pe", lambda e: e.transpose(prwv[0:12, 0:128], r12[s][:, 0, :, :].rearrange("p h k -> p (h k)"), K.ident_b[:, :]),
             reads=[R12[s], CONST], writes=[PRW])
        P.op("pe", lambda e: e.transpose(prwv[0:12, 128:256], r12[s][:, 1, :, :].rearrange("p h k -> p (h k)"), K.ident_b[:, :]),
             reads=[R12[s], CONST], writes=[PRW])
        cpy(C, "dve", rowA[s][0:12, :], prwv[0:12, 0:128], [PRW], [ROWA[s]])
        cpy(C, "dve", rowQ[s][0:12, :], prwv[0:12, 128:256], [PRW], [ROWQ[s]])
        pea, PEA = getps(C)
        peq, PEQ = getps(C)
        for h in range(4):
            hs = slice(h * 128, (h + 1) * 128)
            mm(C, pea[:, hs], K.sel12[0:12, h, :], rowA[s][0:12, :], True, False, [CONST, ROWA[s]], [PEA])
            mm(C, pea[:, hs], K.ident_b[:, :], K.nega[:, :], False, True, [CONST], [PEA])
            mm(C, peq[:, hs], K.sel12[0:12, h, :], rowQ[s][0:12, :], True, False, [CONST, ROWQ[s]], [PEQ])
            mm(C, peq[:, hs], K.ident_b[:, :], K.negq[:, :], False, True, [CONST], [PEQ])
        pkk, PKK = getps(C)
        pkq, PKQ = getps(C)
        for h in range(4):
            hs = slice(h * 128, (h + 1) * 128)
            mm(C, pkk[:, hs], kT[:, h, tsl], kT[:, h, tsl], True, True, [QKV["k"][h]], [PKK])
            mm(C, pkq[:, hs], kT[:, h, tsl], qT[:, h, tsl], True, True, [QKV["k"][h], QKV["q"][h]], [PKQ])
        for h in range(4):
            hs = slice(h * 128, (h + 1) * 128)
            actf(C, expA[s][:, h, :], pea[:, hs], AF.Exp, [PEA, SB_["biasA"]], [EXPA[s]], bias=st["biasA"][:, p, h:h + 1])
            actf(C, expQ[s][:, h, :], peq[:, hs], AF.Exp, [PEQ, SB_["biasQ"]], [EXPQ[s]], bias=st["biasQ"][:, p, h:h + 1])
        x0 = s * 3
        tt(C, "dve", Xs[x0][:].rearrange("p h d -> p (h d)"), pkk[:, :], expA[s][:].rearrange("p h d -> p (h d)"),
           ALU.mult, [PKK, EXPA[s]], [XS[x0]])
        tt(C, "dve", qkT[s3][:].rearrange("p h d -> p (h d)"), pkq[:, :], expQ[s][:].rearrange("p h d -> p (h d)"),
           ALU.mult, [PKQ, EXPQ[s]], [QKT[s3]])
        yield
        pbt, PBT = getps(C)
        pbtv = pbt.bitcast(BF16)
        for h in range(4):
            P.op("pe", lambda e, o=pbtv[:, h * 128:(h + 1) * 128], i=Xs[x0][:, h, :]: e.transpose(o, i, K.ident_b[:, :]),
                 reads=[XS[x0], CONST], writes=[PBT])
        bin_ = pbtv[:, 0:512].rearrange("p (h d) -> p h d", h=4)
        cpy(C, "act", Ys[x0][:], bin_, [PBT], [YS[x0]])
        tt(C, "dve", Ws[x0][:], ident4[:], bin_, ALU.subtract, [PBT, ID4], [WS[x0]])
        yield
        for lvl in range(1, 6):
            xi, xo = s * 3 + (lvl - 1) % 3, s * 3 + lvl % 3
            pa, PA = getps(C)
            for h in range(4):
                hs = slice(h * 128, (h + 1) * 128)
                mm(C, pa[:, hs], Ys[xi][:, h, :], Xs[xi][:, h, :], True, True, [YS[xi], XS[xi]], [PA])
            cpy(C, "act", Xs[xo][:].rearrange("p h d -> p (h d)"), pa[:, :], [PA], [XS[xo]])
            if lvl < 5:
                pbb, PBB_ = getps(C)
                for h in range(4):
                    hs = slice(h * 128, (h + 1) * 128)
                    mm(C, pbb[:, hs], Xs[xi][:, h, :], Ys[xi][:, h, :], True, True, [YS[xi], XS[xi]], [PBB_])
                cpy(C, "act", Ys[xo][:].rearrange("p h d -> p (h d)"), pbb[:, :], [PBB_], [YS[xo]])
            yield
            pw, PW = getps(C)
            for h in range(4):
                hs = slice(h * 128, (h + 1) * 128)
                mm(C, pw[:, hs], Xs[xo][:, h, :], Ws[xi][:, h, :], True, True, [XS[xo], WS[xi]], [PW])
            tt(C, "dve", Ws[xo][:].rearrange("p h d -> p (h d)"), pw[:, :], Ws[xi][:].rearrange("p h d -> p (h d)"),
               ALU.add, [PW, WS[xi]], [WS[xo]])
            yield
        wf = s * 3 + 5 % 3
        pu, PU = getps(C)
        pk, PK = getps(C)
        for h in range(4):
            hs = slice(h * 128, (h + 1) * 128)
            mm(C, pu[:, hs], Ws[wf][:, h, :], vb[s][:, h, :], True, True, [WS[wf], VB[s]], [PU])
            mm(C, pk[:, hs], kbg[s][:, h, :], Ws[wf][:, h, :], True, True, [WS[wf], KBG[s]], [PK])
        cpy(C, "act", u0[s3][:].rearrange("p h d -> p (h d)"), pu[:, :], [PU], [U0[s3]])
        cpy(C, "dve", kcdT[s3][:].rearrange("p h d -> p (h d)"), pk[:, :], [PK], [KCDT[s3]])
        yield

    def scan(p):
        s = p % 2
        s3 = p % 3
        tsl = slice(p * 128, (p + 1) * 128)
        for c in range(2):
            r = slice(64 * c, 64 * c + 64)
            n = 2 * p + c
            pm1, PM1 = getps(C)
            for h in range(4):
                hs = slice(h * 128, (h + 1) * 128)
                mm(C, pm1[:, hs], kcdT[s3][:, h, :], Sb[:, h, :], True, True, [KCDT[s3], SBB], [PM1])
            tt(C, "dve", ut[s][r, :, :].rearrange("p h d -> p (h d)"), u0[s3][r, :, :].rearrange("p h d -> p (h d)"),
               pm1[r, :], ALU.subtract, [U0[s3], PM1], [UT[s]])
            pm2i, pm2, PM2 = reserve_ps(C)
            for h in range(4):
                hs = slice(h * 128, (h + 1) * 128)
                mm(C, pm2[:, hs], qT[:, h, tsl], Sb[:, h, :], True, True, [QKV["q"][h], SBB], [PM2])
            yield
            pm3, PM3 = getps(C)
            pm4, PM4 = getps(C)
            for h in range(4):
                hs = slice(h * 128, (h + 1) * 128)
                mm(C, pm3[:, hs], qkT[s3][r, h, :], ut[s][r, h, :], True, True, [QKT[s3], UT[s]], [PM3])
                mm(C, pm4[:, hs], kdec[s3][r, h, :], ut[s][r, h, :], True, True, [KDEC[s3], UT[s]], [PM4])
            for h in range(4):
                hs = slice(h * 128, (h + 1) * 128)
                actf(C, tmpo[s][r, h, :], pm2[r, hs], AF.Identity, [PM2, SB_["s_o"]], [TMPO[s]], scale=st["s_o"][r, p, h:h + 1])
            C.ps_reserved.discard(pm2i)
            tt(C, "dve", ot[s][r, :, :].rearrange("p h d -> p (h d)"), tmpo[s][r, :, :].rearrange("p h d -> p (h d)"),
               pm3[r, :], ALU.add, [TMPO[s], PM3], [OT[s]])
            for h in range(4):
                hs = slice(h * 128, (h + 1) * 128)
                stt(C, "dve", Sst[:, h, :], Sst[:, h, :], eglb[:, c, p * 4 + h:p * 4 + h + 1], pm4[:, hs], ALU.mult, ALU.add,
                    [SST, EGLB, PM4], [SST])
            cpy(C, "act", Sb[:].rearrange("p h d -> p (h d)"), Sst[:].rearrange("p h d -> p (h d)"), [SST], [SBB])
            yield

    def outp(p):
        s = p % 2
        tsl = slice(p * 128, (p + 1) * 128)
        pz, PZ = getps(C)
        for c in range(8):
            mm(C, pz[:, :], C.xT[:, c, tsl], wz[:, c, :], c == 0, c == 7, [C.XT[p], WZ], [PZ])
        actf(C, sz[s][:, :], pz[:, :], AF.Silu, [PZ], [SZ[s]])
        o2 = ot[s][:].rearrange("p h d -> p (h d)")
        tt(C, "dve", osq[s][:].rearrange("p h d -> p (h d)"), o2, o2, ALU.mult, [OT[s]], [OSQ[s]])
        P.op("dve", lambda e: e.tensor_reduce(rr_[s][:, :], osq[s][:], AX.X, ALU.add), reads=[OSQ[s]], writes=[RR[s]])
        actf(C, rr_[s][:, :], rr_[s][:, :], AF.Ln, [RR[s]], [RR[s]], scale=1.0 / 128.0, bias=RMS_EPS)
        actf(C, rr_[s][:, :], rr_[s][:, :], AF.Exp, [RR[s]], [RR[s]], scale=-0.5)
        tt(C, "dve", osq[s][:], ot[s][:], bc_last(rr_[s][:, :], 128), ALU.mult, [OT[s], RR[s]], [OSQ[s]])
        tt(C, "dve", osq[s][:], osq[s][:], K.gnw[:], ALU.mult, [OSQ[s], CONST], [OSQ[s]])
        tt(C, "dve", oab[s][:].rearrange("p h d -> p (h d)"), osq[s][:].rearrange("p h d -> p (h d)"), sz[s][:, :],
           ALU.mult, [OSQ[s], SZ[s]], [OAB[s]])
        if p == 0:
            tap(C, "o_raw0", ot[s][:], [128, 4, 128], [OT[s]])
            tap(C, "oab0", oab[s][:], [128, 4, 128], [OAB[s]])
        if p == 9:
            tap(C, "o_raw9", ot[s][:], [128, 4, 128], [OT[s]])
        pt, PT = getps(C)
        ptv = pt.bitcast(BF16)
        for h in range(4):
            P.op("pe", lambda e, o=ptv[:, h * 128:(h + 1) * 128], i=oab[s][:, h, :]: e.transpose(o, i, K.ident_b[:, :]),
                 reads=[OAB[s], CONST], writes=[PT])
        cpy(C, "act", C.oAT[:, :, tsl], ptv[:, 0:512].rearrange("p (h d) -> p h d", h=4), [PT], [C.OAT[p]])
        yield

    SCAN_PRIO = int(os.environ.get("GDN_SCAN_PRIO", "1"))

    def scan_out(p):
        g_ = scan(p)
        while True:
            P.cur_prio = SCAN_PRIO
            try:
                next(g_)
            except StopIteration:
                P.cur_prio = 0
                break
            P.cur_prio = 0
            yield
        yield from outp(p)

    if stop != "":
        for p in range(1):
            nst = int(stop[4:]) if stop.startswith("prep") and len(stop) > 4 else 999
            for i_, _ in enumerate(prep(p)):
                if i_ + 1 >= nst:
                    break
            if stop.startswith("prep"):
                continue
            for _ in scan(p):
                pass
            if stop == "scan":
                continue
            for _ in outp(p):
                pass
    else:
        npar = int(os.environ.get("GDN_NPAR", "2"))
        active = []
        next_prep = 0
        next_scan = 0
        prep_done = set()
        scan_gen = None
        while next_scan < NT:
            while len(active) < npar and next_prep < NT and next_prep <= next_scan + 2:
                active.append([next_prep, prep(next_prep)])
                next_prep += 1
            if scan_gen is None and next_scan in prep_done:
                scan_gen = scan_out(next_scan)
            for ent in list(active):
                try:
                    next(ent[1])
                except StopIteration:
                    prep_done.add(ent[0])
                    active.remove(ent)
            if scan_gen is not None:
                try:
                    next(scan_gen)
                except StopIteration:
                    scan_gen = None
                    next_scan += 1
    tap(C, "oAT", C.oAT[:, 0, :], [128, S], C.OAT)


def mm2(C, out, lhsT, rhs, start, stop, rd, wr):
    C.P.op("pe", lambda e: e.matmul(out, lhsT, rhs, start=start, stop=stop, skip_group_check=True), reads=rd, writes=wr,
           cost=PE_FIX + PE_COL * _free(rhs))


def phase_nsa(C, ph):
    import os
    nc, P, dr, K = C.nc, C.P, C.dr, C.K
    CONST = C.CONST
    stop = os.environ.get("NSA_STOP", "")

    def sb(name, shape, dtype):
        return ph.enter_context(nc.sbuf_tensor("n_" + name, list(shape), dtype))

    phs1 = ExitStack()

    def sb1(name, shape, dtype):
        return phs1.enter_context(nc.sbuf_tensor("n_" + name, list(shape), dtype))

    K.cmpmask = cload(C, sb, "k_cmpmask", "c_cmpmask", [128, S], BF16)
    K.overlap = cload(C, sb, "k_overlap", "c_overlap", [128, 32], BF16)
    K.causal = cload(C, sb, "k_causal", "c_causal", [128, 128], BF16)
    K.anti = cload(C, sb, "k_anti", "c_anti", [128, 128], BF16)
    K.expand = cload(C, sb, "k_expand", "c_expand", [32, 16, 128], BF16)
    K.forced = cload(C, sb, "k_forced", "c_forced", [128, 8, 32])
    K.poskT = cload(C, sb, "k_poskT", "p_poskT", [128, 32], BF16)
    K.posvT = cload(C, sb, "k_posvT", "p_posvT", [128, 32], BF16)

    QT = sb("QT", [64, 8, S], BF16)
    QTB = [Buf(f"QT{i}") for i in range(8)]
    KsT = sb("KsT", [64, 2, S], BF16)
    KwT = sb("KwT", [64, 2, S], BF16)
    KST = [Buf(f"KsT{g}") for g in range(2)]
    KWT = [Buf(f"KwT{g}") for g in range(2)]
    KcTc = sb("KcTc", [64, 2, 128], BF16)
    Vca = sb("Vca", [128, 2, 97], BF16)
    Vs = sb("Vs", [128, NT, 2, 65], BF16)
    Vw = sb("Vw", [128, NT, 2, 65], BF16)
    VS, VW = Buf("Vs"), Buf("Vw")
    gts = sb("gates", [128, NT, 24], F32)
    KcT = sb1("KcT", [128, S], BF16)
    VcT = sb1("VcT", [128, S], BF16)
    KCT, VCT = Buf("KcT"), Buf("VcT")
    GTS = Buf("gates")
    P.op("pool", lambda e: e.memset(Vs[:], 1.0), writes=[VS])
    P.op("pool", lambda e: e.memset(Vw[:], 1.0), writes=[VW])

    wt = [t[:, :, 0:128] for t in C.wst]
    WT = C.WST
    tlw = C.wst_tl
    hi = [sb1(f"hi{i}", [128, 512], BF16) for i in range(2)]
    HI = [Buf(f"hi{i}") for i in range(2)]
    nhi = 0
    jobs = [("q", 0), ("q", 1), ("q", 2), ("q", 3), ("ks", 0), ("kw", 0), ("kc", 0), ("vc", 0)]
    for ji, (kind, idx) in enumerate(jobs):
        ws = C.wk % 3
        C.wk += 1
        P.dma("pool", C.wst[ws][:, :, 0:128], dr["pk_n128"][ji], writes=[WT[ws]], tl=tlw[ws])
        for tb in range(4):
            pb, PBb = getps(C)
            for c in range(8):
                mm(C, pb[:, :], wt[ws][:, c, :], C.xT[:, c, tb * 512:(tb + 1) * 512], c == 0, c == 7,
                   [WT[ws]] + C.XT[tb * 4:(tb + 1) * 4], [PBb])
            tsl = slice(tb * 512, (tb + 1) * 512)
            if kind == "kc":
                cpy(C, "act", KcT[:, tsl], pb[:, :], [PBb], [KCT])
            elif kind == "vc":
                cpy(C, "dve", VcT[:, tsl], pb[:, :], [PBb], [VCT])
            else:
                hs_ = nhi % 2
                nhi += 1
                if kind == "q":
                    actf(C, QT[:, 2 * idx, tsl], pb[0:64, :], AF.Copy, [PBb], [QTB[2 * idx]], scale=0.125)
                    actf(C, hi[hs_][64:128, :], pb[64:128, :], AF.Copy, [PBb], [HI[hs_]], scale=0.125)
                    P.dma("sp", QT[:, 2 * idx + 1, tsl], hi[hs_][64:128, :], reads=[HI[hs_]], writes=[QTB[2 * idx + 1]])
                else:
                    dst, DST = (KsT, KST) if kind == "ks" else (KwT, KWT)
                    cpy(C, "dve", dst[:, 0, tsl], pb[0:64, :], [PBb], [DST[0]])
                    cpy(C, "dve", hi[hs_][64:128, :], pb[64:128, :], [PBb], [HI[hs_]])
                    P.dma("sp", dst[:, 1, tsl], hi[hs_][64:128, :], reads=[HI[hs_]], writes=[DST[1]])
    wv = sb1("wv", [128, 8, 280], BF16)
    WV = Buf("wv")
    tlv = P.new_dma_tl("nwv")
    for (c0, ncol, dcol) in ((C_NVS, 128, 0), (C_NVW, 128, 128), (C_NG, 24, 256)):
        src = dr["w_in"][:, c0:c0 + ncol].rearrange("(c p) n -> p c n", p=128)
        P.dma("pool", wv[:, :, dcol:dcol + ncol], src, writes=[WV], tl=tlv)
    gtmp = sb1("gtmp", [128, NT, 24], F32)
    GTMP = Buf("gtmp")
    for t in range(NT):
        pb, PBb = getps(C)
        for c in range(8):
            mm(C, pb[:, 0:280], C.xT[:, c, t * 128:(t + 1) * 128], wv[:, c, :], c == 0, c == 7, [C.XT[t], WV], [PBb])
        cpy(C, "act", Vs[:, t, :, 0:64], pb[:, 0:128].rearrange("p (g d) -> p g d", g=2), [PBb], [VS])
        cpy(C, "dve", Vw[:, t, :, 0:64], pb[:, 128:256].rearrange("p (g d) -> p g d", g=2), [PBb], [VW])
        actf(C, gtmp[:, t, :], pb[:, 256:280], AF.Tanh, [PBb], [GTMP], scale=0.5)
    ts(C, "dve", gts[:], gtmp[:], 0.5, ALU.mult, [GTMP], [GTS], s2=0.5, op1=ALU.add)
    if stop == "proj":
        tap(C, "x_QT", QT[:, 3, :], [64, S], QTB)
        tap(C, "x_KsT", KsT[:, 1, :], [64, S], KST)
        tap(C, "x_Vw", Vw[:], [128, NT, 2, 65], [VW])
        tap(C, "x_gates", gts[:], [128, NT, 24], [GTS])
        return

    KCTC, VCA = Buf("KcTc"), Buf("Vca")
    P.op("pool", lambda e: e.memset(KcTc[:], 0.0), writes=[KCTC])
    P.op("pool", lambda e: e.memset(Vca[:], 0.0), writes=[VCA])
    w1 = sb1("w1", [128, 32, 256], BF16)
    W1B = Buf("w1")
    tl1 = P.new_dma_tl("nw1")
    w2k = sb1("w2k", [128, 2, 64], BF16)
    w2v = sb1("w2v", [128, 2, 64], BF16)
    W2K, W2V = Buf("w2k"), Buf("w2v")
    P.dma("pool", w2k[:, :, :], dr["cmp_w2_k"].rearrange("(j c) d -> c j d", c=128), writes=[W2K])
    P.dma("pool", w2v[:, :, :], dr["cmp_w2_v"].rearrange("(j c) d -> c j d", c=128), writes=[W2V])
    hx = sb1("hx", [128, 128], F32)
    hx2 = sb1("hx2", [128, 128], F32)
    hth = sb1("hth", [128, 128], F32)
    h1 = sb1("h1", [128, 2, 128], BF16)
    b1 = sb1("b1", [128, 2], F32)
    HX, HX2, HTH, H1, B1 = Buf("hx"), Buf("hx2"), Buf("hth"), Buf("h1"), Buf("b1")
    for kv in ("k", "v"):
        key = "cmp_w1_" + kv
        src = dr[key].rearrange("(l d) c -> d l c", d=64)
        P.dma("pool", w1[0:64, :, :], src, writes=[W1B], tl=tl1)
        P.dma("pool", w1[64:128, :, :], src, writes=[W1B], tl=tl1)
        XcT, XCT = (KcT, KCT) if kv == "k" else (VcT, VCT)
        posT = K.poskT if kv == "k" else K.posvT
        for g in range(2):
            hr = slice(64 * g, 64 * g + 64)
            pbb, PBB_ = getps(C)
            for j in range(2):
                for l in range(32):
                    mm(C, pbb[:, j:j + 1], w1[hr, l, j * 128:(j + 1) * 128], posT[hr, l:l + 1], l == 0, l == 31,
                       [W1B, CONST], [PBB_])
            cpy(C, "dve", b1[:, :], pbb[:, 0:2], [PBB_], [B1])
            P.op("dve", lambda e: e.memset(h1[:], 0.0), writes=[H1])
            for j in range(2):
                ph1, PH1 = getps(C)
                xv = XcT[hr, :].rearrange("p (n r) -> p n r", r=16)
                for l in range(32):
                    rhs = xv[:, (l // 16):(l // 16) + 127, l % 16]
                    mm(C, ph1[:, 0:127], w1[hr, l, j * 128:(j + 1) * 128], rhs, l == 0, l == 31, [W1B, XCT], [PH1])
                ts(C, "dve", hx[:, 0:127], ph1[:, 0:127], b1[:, j:j + 1], ALU.add, [PH1, B1], [HX])
                tt(C, "dve", hx2[:, 0:127], hx[:, 0:127], hx[:, 0:127], ALU.mult, [HX], [HX2])
                ts(C, "dve", hx2[:, 0:127], hx2[:, 0:127], 0.044715, ALU.mult, [HX2], [HX2], s2=1.0, op1=ALU.add)
                tt(C, "dve", hx2[:, 0:127], hx2[:, 0:127], hx[:, 0:127], ALU.mult, [HX2, HX], [HX2])
                actf(C, hth[:, 0:127], hx2[:, 0:127], AF.Tanh, [HX2], [HTH], scale=0.7978845608028654)
                stt(C, "dve", hth[:, 0:127], hth[:, 0:127], 1.0, hx[:, 0:127], ALU.add, ALU.mult, [HTH, HX], [HTH])
                ts(C, "dve", h1[:, j, 0:127], hth[:, 0:127], 0.5, ALU.mult, [HTH], [H1])
            po, PO = getps(C)
            if kv == "k":
                for j in range(2):
                    mm(C, po[0:64, 0:128], w2k[:, j, :], h1[:, j, :], j == 0, j == 1, [W2K, H1], [PO])
                cpy(C, "dve", KcTc[:, g, 0:127], po[0:64, 0:127], [PO], [KCTC])
            else:
                for j in range(2):
                    mm(C, po[:, 0:64], h1[:, j, :], w2v[:, j, :], j == 0, j == 1, [W2V, H1], [PO])
                cpy(C, "dve", Vca[0:127, g, 0:64], po[0:127, 0:64], [PO], [VCA])
    for g in range(2):
        P.op("dve", lambda e, g=g: e.memset(Vca[0:127, g, 64:65], 1.0), reads=[], writes=[VCA])
        cpy(C, "dve", Vca[:, g, 65:97], K.overlap[:, :], [CONST], [VCA])
    if stop == "cmp":
        tap(C, "x_KcTc", KcTc[:], [64, 2, 128], [KCTC])
        tap(C, "x_Vca", Vca[:], [128, 2, 97], [VCA])
        return

    P.barrier()
    phs1.close()
    NE = 4
    et = [sb(f"e{i}", [128, 512], BF16) for i in range(NE)]
    ET = [Buf(f"e{i}") for i in range(NE)]
    pt = [sb(f"p{i}", [128, 512], BF16) for i in range(NE)]
    PT_ = [Buf(f"p{i}") for i in range(NE)]
    selm4 = [sb(f"selm{i}", [128, 16, 128], BF16) for i in range(4)]
    SELM4 = [Buf(f"selm{i}") for i in range(4)]
    oB = [sb(f"oB{i}", [128, 512], F32) for i in range(2)]
    OB = [Buf(f"oB{i}") for i in range(2)]
    oBb = [sb(f"oBb{i}", [128, 512], BF16) for i in range(2)]
    OBB = [Buf(f"oBb{i}") for i in range(2)]
    rden = [sb(f"rden{i}", [128, 4], F32) for i in range(2)]
    RDEN = [Buf(f"rden{i}") for i in range(2)]
    fac = [sb(f"fac{i}", [128, 4], F32) for i in range(2)]
    FAC = [Buf(f"fac{i}") for i in range(2)]
    obr = [sb(f"obr{i}", [128, 4, 64], F32) for i in range(2)]
    OBR = [Buf(f"obr{i}") for i in range(2)]
    impt = sb("impt", [128, 4, 32], F32)
    imp = sb("imp", [128, 32], F32)
    imp2 = sb("imp2", [128, 32], F32)
    mx8 = sb("mx8", [128, 8], F32)
    thr = sb("thr", [128, 1], F32)
    bmf = sb("bmf", [128, 32], BF16)
    bmT = sb("bmT", [32, 128], BF16)
    IMPT, IMP, IMP2, MX8, THR, BMF, BMT = (Buf(n) for n in ("impt", "imp", "imp2", "mx8", "thr", "bmf", "bmT"))
    cnt = {"e": 0, "ev": 0, "m": 0}

    def gate_view(t, g, br):
        v = gts[:, t, g * 12:(g + 1) * 12].rearrange("p (b br) -> p b br", b=4)
        return v[:, :, br]

    def qk_exp(kT_, KB, g, kt, qt):
        ps_, PS_ = getps(C)
        ksl = slice(kt * 128, (kt + 1) * 128)
        qsl = slice(qt * 128, (qt + 1) * 128)
        mm(C, ps_[:, :], kT_(ksl), QT[:, 4 * g:4 * g + 4, qsl], True, True, KB + QTB[4 * g:4 * g + 4], [PS_])
        i = cnt["e"] % NE
        cnt["e"] += 1
        actf(C, et[i][:, :], ps_[:, :], AF.Exp, [PS_], [ET[i]])
        return et[i], ET[i]

    def masked(e_, E_, mask_ap, MB):
        i = cnt["m"] % NE
        cnt["m"] += 1
        eng = "dve"
        tt(C, eng, pt[i][:].rearrange("p (b q) -> p b q", b=4), e_[:].rearrange("p (b q) -> p b q", b=4),
           bc_mid(mask_ap, 4), ALU.mult, [E_] + MB, [PT_[i]])
        return pt[i], PT_[i]

    def evac(po, PO, width, t, g, br, first):
        s = t % 2
        i = cnt["ev"] % 2
        cnt["ev"] += 1
        pov = po[:, 0:4 * width].rearrange("p (b w) -> p b w", b=4)
        ts(C, "dve", rden[i][:, :], pov[:, :, 64], 1e-30, ALU.add, [PO], [RDEN[i]])
        P.op("dve", lambda e: e.reciprocal(rden[i][:, :], rden[i][:, :]), reads=[RDEN[i]], writes=[RDEN[i]])
        tt(C, "dve", fac[i][:, :], rden[i][:, :], gate_view(t, g, br), ALU.mult, [RDEN[i], GTS], [FAC[i]])
        ov = oB[s][:, g * 256:(g + 1) * 256].rearrange("p (b d) -> p b d", b=4)
        if first:
            tt(C, "dve", ov, pov[:, :, 0:64], bc_last(fac[i][:, :], 64), ALU.mult, [PO, FAC[i]], [OB[s]])
        else:
            tt(C, "dve", obr[i][:], pov[:, :, 0:64], bc_last(fac[i][:, :], 64), ALU.mult, [PO, FAC[i]], [OBR[i]])
            tt(C, "dve", ov, ov, obr[i][:], ALU.add, [OB[s], OBR[i]], [OB[s]])
        return i

    nqt = NT if stop == "" else int(os.environ.get("NSA_NQT", "16"))
    DEPTH = int(os.environ.get("NSA_DEPTH", "4"))
    blocks = []

    def add_cmp(qt, g):
        st_ = {}
        qsl = slice(qt * 128, (qt + 1) * 128)
        selm = selm4[(qt % 2) * 2:(qt % 2) * 2 + 2]
        SELM = SELM4[(qt % 2) * 2:(qt % 2) * 2 + 2]

        def front():
            e_, E_ = qk_exp(lambda ksl: KcTc[:, g, :], [KCTC], g, 0, qt)
            st_["p"] = masked(e_, E_, K.cmpmask[:, qsl], [CONST])

        def back():
            p_, P_ = st_["p"]
            po, PO = getps(C)
            for b in range(4):
                mm(C, po[:, b * 97:(b + 1) * 97], p_[:, b * 128:(b + 1) * 128], Vca[:, g, :], True, True, [P_, VCA], [PO])
            ri = evac(po, PO, 97, qt, g, 0, True)
            if qt < 8:
                return
            pov = po[:, 0:388].rearrange("p (b w) -> p b w", b=4)
            tt(C, "dve", impt[:], pov[:, :, 65:97], bc_last(rden[ri][:, :], 32), ALU.mult, [PO, RDEN[ri]], [IMPT])
            P.op("dve", lambda e: e.tensor_reduce(imp[:, :], impt[:].rearrange("p b j -> p j b"), AX.X, ALU.add),
                 reads=[IMPT], writes=[IMP])
            tt(C, "dve", imp[:, :], imp[:, :], K.forced[:, qt - 8, :], ALU.add, [IMP, CONST], [IMP])
            P.op("dve", lambda e: e.max(mx8[:, :], imp[:, :]), reads=[IMP], writes=[MX8])
            P.op("dve", lambda e: e.match_replace(imp2[:, :], mx8[:, :], imp[:, :], -3.0e38), reads=[IMP, MX8], writes=[IMP2])
            P.op("dve", lambda e: e.max(mx8[:, :], imp2[:, :]), reads=[IMP2], writes=[MX8])
            P.op("dve", lambda e: e.tensor_reduce(thr[:, :], mx8[:, :], AX.X, ALU.min), reads=[MX8], writes=[THR])
            ts(C, "dve", bmf[:, :], imp[:, :], thr[:, 0:1], ALU.is_ge, [IMP, THR], [BMF])
            pbt, PBT = getps(C)
            pbtv = pbt.bitcast(BF16)
            P.op("pe", lambda e, o=pbtv[0:32, 0:128]: e.transpose(o, bmf[:, :], K.ident_b[:, :]), reads=[BMF, CONST], writes=[PBT])
            cpy(C, "dve", bmT[:, :], pbtv[0:32, 0:128], [PBT], [BMT])
            for k4 in range(0, qt + 1, 4):
                pe_, PE_ = getps(C)
                nk = min(4, qt + 1 - k4)
                for j in range(nk):
                    mm(C, pe_[:, j * 128:(j + 1) * 128], K.expand[0:32, k4 + j, :], bmT[0:32, :], True, True, [CONST, BMT], [PE_])
                if k4 + nk - 1 == qt:
                    if nk > 1:
                        cpy(C, "act", selm[g][:, k4:k4 + nk - 1, :], pe_[:, 0:(nk - 1) * 128].rearrange("p (k q) -> p k q", q=128),
                            [PE_], [SELM[g]])
                    tt(C, "dve", selm[g][:, qt, :], pe_[:, (nk - 1) * 128:nk * 128], K.causal[:, :], ALU.mult, [PE_, CONST], [SELM[g]])
                else:
                    cpy(C, "act", selm[g][:, k4:k4 + nk, :], pe_[:, 0:nk * 128].rearrange("p (k q) -> p k q", q=128), [PE_], [SELM[g]])

        blocks.append((front, back))

    def add_branch(qt, g, br):
        acc_ = {}
        selm = selm4[(qt % 2) * 2:(qt % 2) * 2 + 2]
        SELM = SELM4[(qt % 2) * 2:(qt % 2) * 2 + 2]
        if br == 1:
            kts = list(range(qt + 1))
        else:
            kts = list(range(max(0, qt - 4), qt + 1))
        for kt in kts:
            st_ = {}

            def front(kt=kt, st_=st_):
                if br == 1:
                    e_, E_ = qk_exp(lambda ksl: KsT[:, g, ksl], [KST[g]], g, kt, qt)
                    if qt >= 8:
                        st_["p"] = masked(e_, E_, selm[g][:, kt, :], [SELM[g]])
                    elif kt == qt:
                        st_["p"] = masked(e_, E_, K.causal[:, :], [CONST])
                    else:
                        st_["p"] = (e_, E_)
                else:
                    e_, E_ = qk_exp(lambda ksl: KwT[:, g, ksl], [KWT[g]], g, kt, qt)
                    if kt == qt:
                        st_["p"] = masked(e_, E_, K.causal[:, :], [CONST])
                    elif kt == qt - 4:
                        st_["p"] = masked(e_, E_, K.anti[:, :], [CONST])
                    else:
                        st_["p"] = (e_, E_)

            def back(kt=kt, st_=st_):
                p_, P_ = st_["p"]
                if kt == kts[0]:
                    acc_["po"] = reserve_ps(C)
                poi, po, PO = acc_["po"]
                Vt, VB_ = (Vs, VS) if br == 1 else (Vw, VW)
                for b in range(4):
                    mm2(C, po[:, b * 65:(b + 1) * 65], p_[:, b * 128:(b + 1) * 128], Vt[:, kt, g, :],
                        (kt == kts[0] and b == 0), kt == kts[-1], [P_, VB_], [PO])
                if kt == kts[-1]:
                    evac(po, PO, 65, qt, g, br, False)
                    C.ps_reserved.discard(poi)

            blocks.append((front, back))

    def add_finish(qt):
        s = qt % 2
        qsl = slice(qt * 128, (qt + 1) * 128)

        def front():
            pass

        def back():
            cpy(C, "act", oBb[s][:, :], oB[s][:, :], [OB[s]], [OBB[s]])
            if qt in (0, 1, 3, 7, 9):
                tap(C, f"x_oB{qt}", oB[s][:, :], [128, 512], [OB[s]])
            ptr, PTR = getps(C)
            ptrv = ptr.bitcast(BF16)
            for c in range(4):
                P.op("pe", lambda e, o=ptrv[:, c * 128:(c + 1) * 128], i=oBb[s][:, c * 128:(c + 1) * 128]: e.transpose(o, i, K.ident_b[:, :]),
                     reads=[OBB[s], CONST], writes=[PTR])
            cpy(C, "act", C.oBT[:, :, qsl], ptrv[:, 0:512].rearrange("p (c q) -> p c q", c=4), [PTR], [C.OBT[qt]])

        blocks.append((front, back))

    add_cmp(0, 0)
    add_cmp(0, 1)
    for qt in range(nqt):
        if qt + 1 < nqt:
            add_cmp(qt + 1, 0)
            add_cmp(qt + 1, 1)
        for g in range(2):
            add_branch(qt, g, 1)
            add_branch(qt, g, 2)
        add_finish(qt)
    nb = len(blocks)
    for i in range(nb + DEPTH):
        if i - DEPTH >= 0:
            blocks[i - DEPTH][1]()
        if i < nb:
            blocks[i][0]()
    tap(C, "oBT", C.oBT[:, 0, :], [128, S], C.OBT)


def layer_norm_tile(C, v, V, stats, STATS, mv, MV, gt, bt, out, OUT, mul_eng="dve"):
    P = C.P
    for n in range(2):
        P.op("dve", lambda e, n=n: e.bn_stats(stats[:, n, :], v[:, n * 512:(n + 1) * 512]), reads=[V], writes=[STATS])
    P.op("dve", lambda e: e.bn_aggr(mv[:, 0:2], stats[:].rearrange("p n s -> p (n s)")), reads=[STATS], writes=[MV])
    actf(C, mv[:, 2:3], mv[:, 1:2], AF.Ln, [MV], [MV], bias=LN_EPS)
    actf(C, mv[:, 2:3], mv[:, 2:3], AF.Exp, [MV], [MV], scale=-0.5)
    stt(C, "dve", mv[:, 3:4], mv[:, 0:1], -1.0, mv[:, 2:3], ALU.mult, ALU.mult, [MV], [MV])
    actf(C, v[:, :], v[:, :], AF.Identity, [V, MV], [V], bias=mv[:, 3:4], scale=mv[:, 2:3])
    tt(C, mul_eng, v[:, :], v[:, :], gt[:, :], ALU.mult, [V, C.CONST], [V])
    tt(C, mul_eng, out[:, :], v[:, :], bt[:, :], ALU.add, [V, C.CONST], [OUT])


def phase_mix(C, ph):
    import os
    nc, P, dr, K = C.nc, C.P, C.dr, C.K
    CONST = C.CONST

    def sb(name, shape, dtype):
        return ph.enter_context(nc.sbuf_tensor("m_" + name, list(shape), dtype))

    ln1g = cload(C, sb, "ln1g", "p_ln1g", [128, D])
    ln1b = cload(C, sb, "ln1b", "p_ln1b", [128, D])
    wA = sb("wA", [128, 4, D], BF16)
    wB = sb("wB", [128, 4, D], BF16)
    wo = sb("wo", [128, 8, D], BF16)
    WA, WB, WO = Buf("wA"), Buf("wB"), Buf("wo")
    load_w(C, wA[:], "w_branch_gdn", 0, D, [WA], nrow_chunks=4)
    load_w(C, wB[:], "w_branch_nsa", 0, D, [WB], nrow_chunks=4)
    wg, WG, tlg = C.wst, C.WST, C.wst_tl
    mixT = [sb(f"mixT{i}", [128, 8, 512], BF16) for i in range(2)]
    MIXT = [Buf(f"mixT{i}") for i in range(2)]
    NB = 2
    th = [sb(f"th{i}", [128, 2, 512], F32) for i in range(NB)]
    TH = [Buf(f"th{i}") for i in range(NB)]
    m1 = [sb(f"m1{i}", [128, 512], F32) for i in range(NB)]
    M1 = [Buf(f"m1{i}") for i in range(NB)]
    m2 = [sb(f"m2{i}", [128, 512], F32) for i in range(NB)]
    M2 = [Buf(f"m2{i}") for i in range(NB)]
    xt = [sb(f"xt{i}", [128, D], F32) for i in range(2)]
    XTl = [Buf(f"xt{i}") for i in range(2)]
    tlx = [P.new_dma_tl(f"mxt{i}") for i in range(2)]
    vt = [sb(f"vt{i}", [128, D], F32) for i in range(2)]
    VT = [Buf(f"vt{i}") for i in range(2)]
    ht, HT = vt, VT
    hb = [sb(f"hb{i}", [128, D], BF16) for i in range(2)]
    HB = [Buf(f"hb{i}") for i in range(2)]
    stats = [sb(f"stats{i}", [128, 2, 6], F32) for i in range(2)]
    STATS = [Buf(f"stats{i}") for i in range(2)]
    mv = [sb(f"mv{i}", [128, 4], F32) for i in range(2)]
    MV = [Buf(f"mv{i}") for i in range(2)]
    C.HSCR = [Buf(f"hscr{t}") for t in range(NT)]
    kc = {"k": 0}

    def gates(tb):
        ms = tb % 2
        tsl = slice(tb * 512, (tb + 1) * 512)
        XTB = C.XT[tb * 4:(tb + 1) * 4]
        for j in range(8):
            k = kc["k"]
            ws = C.wk % 3
            C.wk += 1
            bs = k % NB
            kc["k"] += 1
            P.dma("pool", wg[ws][:], dr["pk_mg"][j], writes=[WG[ws]], tl=tlg[ws])
            pga, PGA = getps(C)
            pgb, PGB = getps(C)
            for c in range(8):
                mm(C, pga[:, :], wg[ws][:, c, 0:128], C.xT[:, c, tsl], c == 0, c == 7, [WG[ws]] + XTB, [PGA])
            for c in range(8):
                mm(C, pgb[:, :], wg[ws][:, c, 128:256], C.xT[:, c, tsl], c == 0, c == 7, [WG[ws]] + XTB, [PGB])
            actf(C, th[bs][:, 0, :], pga[:, :], AF.Tanh, [PGA], [TH[bs]], scale=0.5)
            actf(C, th[bs][:, 1, :], pgb[:, :], AF.Tanh, [PGB], [TH[bs]], scale=0.5)
            pa, PA = getps(C)
            pbB, PBB_ = getps(C)
            for c in range(4):
                mm(C, pa[:, :], wA[:, c, j * 128:(j + 1) * 128], C.oAT[:, c, tsl], c == 0, c == 3,
                   [WA] + C.OAT[tb * 4:(tb + 1) * 4], [PA])
            for c in range(4):
                mm(C, pbB[:, :], wB[:, c, j * 128:(j + 1) * 128], C.oBT[:, c, tsl], c == 0, c == 3,
                   [WB] + C.OBT[tb * 4:(tb + 1) * 4], [PBB_])
            stt(C, "dve", m1[bs][:, :], th[bs][:, 0, :], 1.0, pa[:, :], ALU.add, ALU.mult, [TH[bs], PA], [M1[bs]])
            stt(C, "dve", m2[bs][:, :], th[bs][:, 1, :], 1.0, pbB[:, :], ALU.add, ALU.mult, [TH[bs], PBB_], [M2[bs]])
            tt(C, "dve", mixT[ms][:, j, :], m1[bs][:, :], m2[bs][:, :], ALU.add, [M1[bs], M2[bs]], [MIXT[ms]])
            yield

    def epi(tb):
        ms = tb % 2
        for t4 in range(4):
            t = tb * 4 + t4
            s2 = t % 2
            P.dma("sp", xt[s2][:, :], dr["x"][t * 128:(t + 1) * 128, :], writes=[XTl[s2]], tl=tlx[s2])
            for n in range(2):
                py, PY = getps(C)
                for j in range(8):
                    mm(C, py[:, :], mixT[ms][:, j, t4 * 128:(t4 + 1) * 128], wo[:, j, n * 512:(n + 1) * 512], j == 0, j == 7,
                       [MIXT[ms], WO], [PY])
                stt(C, "dve", vt[s2][:, n * 512:(n + 1) * 512], xt[s2][:, n * 512:(n + 1) * 512], DN_ALPHA, py[:, :],
                    ALU.mult, ALU.add, [XTl[s2], PY], [VT[s2]])
            layer_norm_tile(C, vt[s2], VT[s2], stats[s2], STATS[s2], mv[s2], MV[s2], ln1g, ln1b, vt[s2], VT[s2])
            P.dma("sp", C.hscr[t * 128:(t + 1) * 128, :], ht[s2][:, :], reads=[HT[s2]], writes=[C.HSCR[t]])
            cpy(C, "act", hb[s2][:, :], ht[s2][:, :], [HT[s2]], [HB[s2]])
            ptr, PTR = getps(C)
            ptrv = ptr.bitcast(BF16)
            for c in range(8):
                P.op("pe", lambda e, o=ptrv[:, c * 128:(c + 1) * 128], i=hb[s2][:, c * 128:(c + 1) * 128]: e.transpose(o, i, K.ident_b[:, :]),
                     reads=[HB[s2], CONST], writes=[PTR])
            cpy(C, "act", C.xT[:, :, t * 128:(t + 1) * 128], ptrv[:, :].rearrange("p (c q) -> p c q", c=8), [PTR], [C.XT[t]])
            yield

    for j_, _ in enumerate(gates(0)):
        if j_ == 1:
            load_w(C, wo[:], "w_out", 0, D, [WO])
            actf(C, wo[:].rearrange("p c n -> p (c n)"), wo[:].rearrange("p c n -> p (c n)"), AF.Copy, [WO], [WO], scale=0.5)
    for tb in range(4):
        A = gates(tb + 1) if tb + 1 < 4 else iter(())
        B = epi(tb)
        a_done = b_done = False
        while not (a_done and b_done):
            for _ in range(2):
                if not a_done:
                    try:
                        next(A)
                    except StopIteration:
                        a_done = True
            if not b_done:
                try:
                    next(B)
                except StopIteration:
                    b_done = True


def phase_ffn(C, ph):
    import os
    nc, P, dr, K = C.nc, C.P, C.dr, C.K
    CONST = C.CONST
    hT, HTB = C.xT, C.XT

    def sb(name, shape, dtype):
        return ph.enter_context(nc.sbuf_tensor("f_" + name, list(shape), dtype))

    ln2g = cload(C, sb, "ln2g", "p_ln2g", [128, D])
    ln2b = cload(C, sb, "ln2b", "p_ln2b", [128, D])
    fconv = cload(C, sb, "fconv", "p_fconv", [128, 44, 3])
    wd = sb("wd", [128, 22, D], BF16)
    WD = Buf("wd")
    QW = 512
    aT = [sb(f"aT{i}", [128, 22, QW], BF16) for i in range(2)]
    AT = [[Buf(f"aT{q}_{i}") for i in range(22)] for q in range(2)]
    wu, WU, tlu = C.wst, C.WST, C.wst_tl
    raw = [[sb(f"raw{w}{i}", [128, QW + 2], F32) for i in range(2)] for w in range(2)]
    RAW = [[Buf(f"raw{w}{i}") for i in range(2)] for w in range(2)]
    acc = [[sb(f"acc{w}{i}", [128, QW], F32) for i in range(2)] for w in range(2)]
    ACC = [[Buf(f"acc{w}{i}") for i in range(2)] for w in range(2)]
    halo = sb("halo", [128, 44, 2], F32)
    HALO = [Buf(f"halo{c}") for c in range(44)]
    hres = [sb(f"hres{i}", [128, D], F32) for i in range(2)]
    HRES = [Buf(f"hres{i}") for i in range(2)]
    tlh = [P.new_dma_tl(f"fhr{i}") for i in range(2)]
    vt = [sb(f"vt{i}", [128, D], F32) for i in range(2)]
    VT = [Buf(f"vt{i}") for i in range(2)]
    stats = [sb(f"stats{i}", [128, 2, 6], F32) for i in range(2)]
    STATS = [Buf(f"stats{i}") for i in range(2)]
    mv = [sb(f"mv{i}", [128, 4], F32) for i in range(2)]
    MV = [Buf(f"mv{i}") for i in range(2)]
    OUTB = [Buf(f"out{t}") for t in range(NT)]
    kc = {"k": 0}

    def d1(q):
        T0 = q * QW
        qs = q % 2
        for i in range(22):
            k = kc["k"]
            ws = C.wk % 3
            C.wk += 1
            rs = k % 2
            kc["k"] += 1
            P.dma("pool", wu[ws][:], dr["pk_up"][i], writes=[WU[ws]], tl=tlu[ws])
            for w in range(2):
                ck = i + 22 * w
                if q == 0:
                    P.op("dve", lambda e, o=raw[w][rs][:, 0:2]: e.memset(o, 0.0), writes=[RAW[w][rs]])
                else:
                    cpy(C, "dve", raw[w][rs][:, 0:2], halo[:, ck, :], [HALO[ck]], [RAW[w][rs]])
                pu, PU = getps(C)
                for c in range(8):
                    mm(C, pu[:, :], wu[ws][:, c, w * 128:(w + 1) * 128], hT[:, c, T0:T0 + QW],
                       c == 0, c == 7, [WU[ws]] + HTB[T0 // 128:T0 // 128 + 4], [PU])
                cpy(C, "act", raw[w][rs][:, 2:2 + QW], pu[:, :], [PU], [RAW[w][rs]])
                if q < 3:
                    cpy(C, "dve", halo[:, ck, :], raw[w][rs][:, QW:QW + 2], [RAW[w][rs]], [HALO[ck]])
                if FFN_TAP_ACT:
                    actf(C, acc[w][rs][:, :], raw[w][rs][:, 2:QW + 2], AF.Copy, [RAW[w][rs], CONST], [ACC[w][rs]], scale=fconv[:, ck, 2:3])
                else:
                    ts(C, "dve", acc[w][rs][:, :], raw[w][rs][:, 2:QW + 2], fconv[:, ck, 2:3], ALU.mult, [RAW[w][rs], CONST], [ACC[w][rs]])
                for j in (1, 0):
                    stt(C, "dve", acc[w][rs][:, :], raw[w][rs][:, j:QW + j], fconv[:, ck, j:j + 1], acc[w][rs][:, :],
                        ALU.mult, ALU.add, [RAW[w][rs], ACC[w][rs], CONST], [ACC[w][rs]])
            actf(C, acc[0][rs][:, :], acc[0][rs][:, :], AF.Silu, [ACC[0][rs]], [ACC[0][rs]])
            tt(C, "dve", aT[qs][:, i, :], acc[0][rs][:, :], acc[1][rs][:, :], ALU.mult, [ACC[0][rs], ACC[1][rs]], [AT[qs][i]])
            if q == 0 and 3 <= i < 14:
                i2 = (i - 3) * 2
                src = dr["w_down"][i2 * 128:(i2 + 2) * 128, :].rearrange("(c p) n -> p c n", p=128)
                P.dma("pool", wd[:, i2:i2 + 2, :], src, writes=[WD])
            yield

    def d2(q):
        qs = q % 2
        for t4 in range(4):
            t = q * 4 + t4
            s2 = t % 2
            P.dma("sp", hres[s2][:, :], C.hscr[t * 128:(t + 1) * 128, :], reads=[C.HSCR[t]], writes=[HRES[s2]], tl=tlh[s2])
            for n in range(2):
                pf, PF = getps(C)
                for i in range(22):
                    mm(C, pf[:, :], aT[qs][:, i, t4 * 128:(t4 + 1) * 128], wd[:, i, n * 512:(n + 1) * 512], i == 0, i == 21,
                       [AT[qs][i], WD], [PF])
                stt(C, "dve", vt[s2][:, n * 512:(n + 1) * 512], hres[s2][:, n * 512:(n + 1) * 512], DN_ALPHA, pf[:, :],
                    ALU.mult, ALU.add, [HRES[s2], PF], [VT[s2]])
            layer_norm_tile(C, vt[s2], VT[s2], stats[s2], STATS[s2], mv[s2], MV[s2], ln2g, ln2b, vt[s2], VT[s2])
            P.dma("sp", C.out_d[t * 128:(t + 1) * 128, :], vt[s2][:, :], reads=[VT[s2]], writes=[OUTB[t]])
            C.final_bufs.append(OUTB[t])
            yield

    for _ in d1(0):
        pass
    for q in range(4):
        A = d1(q + 1) if q + 1 < 4 else iter(())
        B = d2(q)
        a_done = b_done = False
        while not (a_done and b_done):
            for _ in range(FFN_RATIO):
                if not a_done:
                    try:
                        next(A)
                    except StopIteration:
                        a_done = True
            if not b_done:
                try:
                    next(B)
                except StopIteration:
                    b_done = True


_CACHE = {}


def kernel(**inputs):
    inp = {k: np.asarray(v) for k, v in inputs.items()}
    if "nc" not in _CACHE:
        _CACHE["nc"] = build()[0]
    nc = _CACHE["nc"]
    base = {k: np.ascontiguousarray(inp[k][0], dtype=np.float32) for k in WEIGHT_SHAPES}
    base.update(host_consts())
    base.update(host_params(inp))
    base.update(host_packed(inp))
    n = inp["x"].shape[0]
    in_maps = [dict(base, x=np.ascontiguousarray(inp["x"][b], dtype=np.float32)) for b in range(n)]
    res = run_bass_kernel_spmd(nc, in_maps, core_ids=list(range(n)))
    return np.stack([np.asarray(r["out"], dtype=np.float32) for r in res.results], 0)
```

```python
import math
from contextlib import ExitStack
import numpy as np
import concourse.bass as bass
import concourse.mybir as mybir
from concourse.bass_utils import run_bass_kernel_spmd

F32 = mybir.dt.float32
BF16 = mybir.dt.bfloat16
AF = mybir.ActivationFunctionType
ALU = mybir.AluOpType
AX = mybir.AxisListType

S = 2048
D = 1024
NT = 16
D_IN = 5408
D_FF = 2816
DN_ALPHA = 2.0 ** 0.25
LN_EPS = 1e-5
RMS_EPS = 1e-6
NEG = -30000.0
import os as _os
FFN_RATIO = int(_os.environ.get("FFN_RATIO", "8"))
PE_FIX = float(_os.environ.get("PE_FIX", "45"))
PE_COL = float(_os.environ.get("PE_COL", "0.45"))
CP_PRIO = int(_os.environ.get("CP_PRIO", "1"))
FFN_TAP_ACT = int(_os.environ.get("FFN_TAP_ACT", "1"))

C_GQ, C_GK, C_GV, C_GZ, C_GB, C_GA = 0, 512, 1024, 1536, 2048, 2052
C_NQ, C_NKC, C_NVC, C_NKS, C_NVS, C_NKW, C_NVW, C_NG, C_MG = 2056, 2568, 2696, 2824, 2952, 3080, 3208, 3336, 3360

EPOCH = 6000


class Timeline:
    def __init__(self, prog, name, step):
        self.prog = prog
        self.name = name
        self.step = step
        self.count = 0
        self.sems = []

    def sem_for(self, idx):
        ep = (idx - 1) // EPOCH
        while len(self.sems) <= ep:
            self.sems.append(self.prog.new_sem(f"{self.name}_{len(self.sems)}"))
        return self.sems[ep], ((idx - 1) % EPOCH + 1) * self.step

    def next(self):
        self.count += 1
        return self.count


class Buf:
    __slots__ = ("name", "last_write", "reads", "excl", "also", "persist")

    def __init__(self, name, excl=False, persist=False):
        self.name = name
        self.persist = persist
        self.last_write = None
        self.reads = []
        self.excl = excl
        self.also = None


class Op:
    __slots__ = ("idx", "eng", "fn", "tl", "deps", "epoch", "dur", "busy", "pos", "fin", "sched", "final", "bar", "prio")

    def __init__(self, idx, eng, fn, tl, deps, epoch, dur, busy, bar=True):
        self.idx, self.eng, self.fn, self.tl, self.deps = idx, eng, fn, tl, deps
        self.epoch, self.dur, self.busy = epoch, dur, busy
        self.bar = bar
        self.prio = 0
        self.pos = None
        self.fin = None
        self.sched = False
        self.final = False


def _free(ap):
    n = 1
    for d in list(ap.shape)[1:]:
        n *= int(d)
    return n


class Prog:
    ENGS = ("pe", "act", "dve", "pool", "sp")
    WINDOW = int(_os.environ.get("SCHED_WINDOW", "128"))
    SEM_LAT = float(_os.environ.get("SCHED_SEMLAT", "1500"))

    def __init__(self, nc):
        self.nc = nc
        self.stack = ExitStack()
        self.tl = {e: Timeline(self, "c_" + e, 1) for e in self.ENGS}
        self.ops = []
        self.epoch = 0
        self.dma_pool = {}
        self.dma_rr = {}
        self.dma_last = {}
        self.all_dma_tl = []
        self.same_engine_sync = bool(int(_os.environ.get("SAME_ENG_SYNC", "1")))
        self.finals = []
        self.cur_prio = 0

    def new_sem(self, name):
        return self.stack.enter_context(self.nc.semaphore(name))

    def new_dma_tl(self, name):
        t = Timeline(self, "d_" + name, 16)
        self.all_dma_tl.append(t)
        return t

    def _deps(self, reads, writes):
        deps = set()
        for b in reads:
            if b.last_write is not None:
                deps.add(b.last_write)
            if b.excl:
                deps.update(b.reads)
            if b.also:
                deps.update(b.also)
        for b in writes:
            if b.last_write is not None:
                deps.add(b.last_write)
            deps.update(b.reads)
        return deps

    def _mark(self, op, reads, writes):
        for b in reads:
            if b.excl:
                b.last_write = op
                b.reads = []
            else:
                b.reads.append(op)
        for b in writes:
            b.last_write = op
            b.reads = []

    def op(self, eng, fn, reads=(), writes=(), cost=150.0):
        deps = self._deps(reads, writes)
        bar = not all(b.persist for b in list(reads) + list(writes))
        o = Op(len(self.ops), eng, fn, self.tl[eng], deps, self.epoch, cost, cost, bar)
        o.prio = self.cur_prio
        self.ops.append(o)
        self._mark(o, reads, writes)
        return o

    def dma(self, eng, out, in_, reads=(), writes=(), tl=None, **kw):
        if tl is None:
            if eng not in self.dma_pool:
                self.dma_pool[eng] = [self.new_dma_tl(f"{eng}{i}") for i in range(6)]
            pool = self.dma_pool[eng]
            i = self.dma_rr.get(eng, 0)
            self.dma_rr[eng] = (i + 1) % len(pool)
            tl = pool[i]
        deps = self._deps(reads, writes)
        if tl in self.dma_last:
            deps.add(self.dma_last[tl])
        nbytes = _free(out) * int(out.shape[0]) * 4
        dur = 2200.0 + nbytes / 150.0

        def fn(e, out=out, in_=in_, kw=kw):
            return e.dma_start(out=out, in_=in_, **kw)

        bar = not all(b.persist for b in list(reads) + list(writes))
        o = Op(len(self.ops), eng, fn, tl, deps, self.epoch, dur, 150.0 if eng == "sp" else 400.0, bar)
        self.ops.append(o)
        self.dma_last[tl] = o
        self._mark(o, reads, writes)
        return o

    def barrier(self):
        self.epoch += 1

    def final_wait(self, eng, bufs):
        deps = self._deps(bufs, bufs)
        self.finals.append((eng, deps))

    def schedule(self):
        bl = [0.0] * len(self.ops)
        if CP_PRIO:
            succ = [[] for _ in self.ops]
            for o in self.ops:
                for d in o.deps:
                    succ[d.idx].append(o.idx)
            for o in reversed(self.ops):
                m = 0.0
                for si in succ[o.idx]:
                    v = bl[si] + (0.0 if self.ops[si].eng == o.eng else self.SEM_LAT)
                    if v > m:
                        m = v
                bl[o.idx] = o.dur + m
        self._bl = bl
        pend = {e: [] for e in self.ENGS}
        for o in self.ops:
            pend[o.eng].append(o)
        head = {e: 0 for e in self.ENGS}
        tfree = {e: 0.0 for e in self.ENGS}
        order = {e: [] for e in self.ENGS}
        n_left = len(self.ops)
        ep_left = {}
        for o in self.ops:
            ep_left[o.epoch] = ep_left.get(o.epoch, 0) + 1
        ep_end = {-1: 0.0}
        cur_ep = 0
        ep_fin = 0.0
        while n_left:
            while ep_left.get(cur_ep, 0) == 0:
                ep_end[cur_ep] = ep_fin
                cur_ep += 1
            best = None
            for e in self.ENGS:
                lst = pend[e]
                h = head[e]
                while h < len(lst) and lst[h].sched:
                    h += 1
                head[e] = h
                cnt = 0
                i = h
                while i < len(lst) and cnt < self.WINDOW:
                    o = lst[i]
                    i += 1
                    if o.sched:
                        continue
                    cnt += 1
                    if o.epoch != cur_ep:
                        if o.bar or o.epoch < cur_ep:
                            continue
                        rdy = 0.0
                    else:
                        rdy = ep_end[cur_ep - 1] if o.bar else 0.0
                    ok = True
                    for d in o.deps:
                        if not d.sched:
                            ok = False
                            break
                        f = d.fin + (0.0 if d.eng == e and d.tl is self.tl[e] else self.SEM_LAT)
                        if f > rdy:
                            rdy = f
                    if not ok:
                        continue
                    st = rdy if rdy > tfree[e] else tfree[e]
                    key = (st, -o.prio, -bl[o.idx], o.idx) if CP_PRIO else (st, -o.prio, o.idx)
                    if best is None or key < best[0]:
                        best = (key, e, o, st)
            if best is None:
                raise RuntimeError("scheduler: no candidate")
            _, e, o, st = best
            o.sched = True
            o.fin = st + o.dur
            tfree[e] = st + o.busy
            order[e].append(o)
            n_left -= 1
            ep_left[o.epoch] -= 1
            if o.fin > ep_fin:
                ep_fin = o.fin
        self.est_ns = ep_fin
        return order

    def emit(self):
        nc = self.nc
        order = self.schedule()
        for e in self.ENGS:
            for o in order[e]:
                o.pos = o.tl.next()
        streams = {}
        for e in self.ENGS:
            known = {}
            out = []
            mytl = self.tl[e]
            last_ep = 0
            done_pos = {}
            for o in order[e]:
                waits = []
                if o.bar and o.epoch > last_ep:
                    for tl, p in self._epoch_max(o.epoch).items():
                        if tl is mytl:
                            continue
                        if known.get(tl, 0) < p:
                            known[tl] = p
                            waits.append(tl.sem_for(p))
                    last_ep = o.epoch
                for d in o.deps:
                    if d.tl is mytl and (e in ("pe", "sp") or not self.same_engine_sync):
                        continue
                    if known.get(d.tl, 0) >= d.pos:
                        continue
                    known[d.tl] = d.pos
                    waits.append(d.tl.sem_for(d.pos))
                sem, _ = o.tl.sem_for(o.pos)
                out.append((waits, o.fn, sem, o.tl.step))
            streams[e] = out
        for (e, deps) in self.finals:
            waits = []
            best = {}
            for d in deps:
                if best.get(d.tl, 0) < d.pos:
                    best[d.tl] = d.pos
            for tl, p in best.items():
                waits.append(tl.sem_for(p))
            streams[e].append((waits, None, None, 0))
        self.streams = streams

        def replay(name):
            def body(e):
                for (waits, fn, sem, inc) in streams[name]:
                    for (s_, v) in waits:
                        e.wait_ge(s_, v)
                    if fn is not None:
                        fn(e).then_inc(sem, inc)
            return body

        with nc.Block() as block:
            block.tensor(replay("pe"))
            block.scalar(replay("act"))
            block.vector(replay("dve"))
            block.gpsimd(replay("pool"))
            block.sync(replay("sp"))

    def _epoch_max(self, epoch):
        if not hasattr(self, "_epmax"):
            self._epmax = {}
        if epoch not in self._epmax:
            m = {}
            for o in self.ops:
                if o.epoch < epoch and m.get(o.tl, 0) < o.pos:
                    m[o.tl] = o.pos
            self._epmax[epoch] = m
        return self._epmax[epoch]


def host_consts():
    c = {}
    i = np.arange(128)
    same = (i[:, None] // 64) == (i[None, :] // 64)
    c["c_ident"] = np.eye(128, dtype=np.float32)
    c["c_tri"] = ((i[:, None] <= i[None, :]) & same).astype(np.float32)
    c["c_blk"] = same.astype(np.float32)
    c["c_nega"] = np.where((i[:, None] > i[None, :]) & same, 0.0, NEG).astype(np.float32)
    c["c_negq"] = np.where((i[None, :] >= i[:, None]) & same, 0.0, NEG).astype(np.float32)
    sel = np.zeros((4, 4, 128), np.float32)
    for h in range(4):
        sel[h, h, :] = 1.0
    c["c_sel4"] = sel
    sel12 = np.zeros((12, 4, 128), np.float32)
    for h in range(4):
        sel12[3 * h:3 * h + 3, h, :] = 1.0
    c["c_sel12"] = sel12
    sr = np.zeros((128, 2, 128), np.float32)
    sr[0, 0, :] = 1.0
    sr[64, 1, :] = 1.0
    c["c_selrow"] = sr
    n = np.arange(127)
    t = np.arange(S)
    c["c_cmpmask"] = np.concatenate([((n[:, None] * 16 + 31) <= t[None, :]).astype(np.float32),
                                     np.zeros((1, S), np.float32)], 0)
    starts = n * 16
    jb = np.arange(32) * 64
    ov = ((starts[:, None] < jb[None] + 64) & (starts[:, None] + 32 > jb[None])).astype(np.float32)
    c["c_overlap"] = np.concatenate([ov, np.zeros((1, 32), np.float32)], 0)
    c["c_causal"] = (i[:, None] <= i[None, :]).astype(np.float32)
    c["c_anti"] = (i[:, None] > i[None, :]).astype(np.float32)
    E = np.zeros((32, 16, 128), np.float32)
    for kt in range(16):
        for k in range(128):
            E[2 * kt + k // 64, kt, k] = 1.0
    c["c_expand"] = E
    fb = np.zeros((128, 8, 32), np.float32)
    for qi in range(8):
        qt = 8 + qi
        pos = qt * 128 + i
        cur = pos // 64
        jj = np.arange(32)
        causal = (jj[None] * 64) <= pos[:, None]
        b_ = np.where(causal, 0.0, -1e30)
        b_ = np.where(jj[None] == 0, 1e9, b_)
        b_ = np.where(jj[None] == cur[:, None], 2e9, b_)
        b_ = np.where(jj[None] == cur[:, None] - 1, 3e9, b_)
        fb[:, qi, :] = b_
    c["c_forced"] = fb
    return c


CONST_SHAPES = {k: v.shape for k, v in host_consts().items()}


def host_params(inp):
    p = {}
    f = np.float32
    p["p_gconv"] = np.ascontiguousarray(inp["gdn_conv_w"][0].reshape(4, 12, 128).transpose(2, 1, 0)).astype(f)
    p["p_alog"] = np.ascontiguousarray(np.broadcast_to(inp["gdn_a_log"][0][None, None, :], (128, NT, 4))).astype(f)
    p["p_dtb"] = np.ascontiguousarray(np.broadcast_to(inp["gdn_dt_bias"][0][None, None, :], (128, NT, 4))).astype(f)
    p["p_gnw"] = np.ascontiguousarray(np.broadcast_to(inp["gdn_norm_w"][0][None, None, :], (128, 4, 128))).astype(f)
    p["p_poskT"] = np.ascontiguousarray(np.concatenate([inp["cmp_pos_k"][0].T] * 2, 0)).astype(f)
    p["p_posvT"] = np.ascontiguousarray(np.concatenate([inp["cmp_pos_v"][0].T] * 2, 0)).astype(f)
    p["p_ln1g"] = np.ascontiguousarray(np.broadcast_to(inp["ln1_g"][0][None, :], (128, D))).astype(f)
    p["p_ln1b"] = np.ascontiguousarray(np.broadcast_to(inp["ln1_b"][0][None, :], (128, D))).astype(f)
    p["p_ln2g"] = np.ascontiguousarray(np.broadcast_to(inp["ln2_g"][0][None, :], (128, D))).astype(f)
    p["p_ln2b"] = np.ascontiguousarray(np.broadcast_to(inp["ln2_b"][0][None, :], (128, D))).astype(f)
    p["p_fconv"] = np.ascontiguousarray(inp["ffn_conv_w"][0].reshape(3, 44, 128).transpose(2, 1, 0)).astype(f)
    return p


def _pk(w, cols):
    sub = w[:, cols]
    return np.ascontiguousarray(sub.reshape(8, 128, sub.shape[1]).transpose(1, 0, 2))


def host_packed(inp):
    f = np.float32
    w_in = np.asarray(inp["w_in"][0], dtype=f)
    w_up = np.asarray(inp["w_up"][0], dtype=f)
    p = {}
    ar = np.arange
    p["pk_up"] = np.stack([_pk(w_up, np.concatenate([ar(i * 128, (i + 1) * 128), ar(D_FF + i * 128, D_FF + (i + 1) * 128)]))
                           for i in range(22)], 0)
    p["pk_mg"] = np.stack([_pk(w_in, np.concatenate([ar(C_MG + j * 128, C_MG + (j + 1) * 128),
                                                     ar(C_MG + 1024 + j * 128, C_MG + 1024 + (j + 1) * 128)]))
                           for j in range(8)], 0)
    p["pk_g"] = np.stack([_pk(w_in, ar(ck * 128, (ck + 1) * 128)) for ck in range(12)], 0)
    jobs = [ar(C_NQ + i * 128, C_NQ + (i + 1) * 128) for i in range(4)]
    jobs += [ar(C_NKS, C_NKS + 128), ar(C_NKW, C_NKW + 128), ar(C_NKC, C_NKC + 128), ar(C_NVC, C_NVC + 128)]
    p["pk_n128"] = np.stack([_pk(w_in, c) for c in jobs], 0)
    return p


PACKED_SHAPES = {"pk_up": (22, 128, 8, 256), "pk_mg": (8, 128, 8, 256), "pk_g": (12, 128, 8, 128),
                 "pk_n128": (8, 128, 8, 128)}

PARAM_SHAPES = {"p_gconv": (128, 12, 4), "p_alog": (128, NT, 4), "p_dtb": (128, NT, 4), "p_gnw": (128, 4, 128),
                "p_poskT": (128, 32), "p_posvT": (128, 32), "p_ln1g": (128, D), "p_ln1b": (128, D),
                "p_ln2g": (128, D), "p_ln2b": (128, D), "p_fconv": (128, 44, 3)}

WEIGHT_SHAPES = {"w_in": (D, D_IN), "cmp_w1_k": (2048, 256), "cmp_w2_k": (256, 64), "cmp_w1_v": (2048, 256),
                 "cmp_w2_v": (256, 64), "w_branch_gdn": (512, D), "w_branch_nsa": (512, D), "w_out": (D, D),
                 "w_up": (D, 2 * D_FF), "w_down": (D_FF, D)}


class Ctx:
    pass


def build(taps=(), phases=("gdn", "nsa", "mix", "ffn")):
    nc = bass.Bass("TRN2", target_bir_lowering=False)
    dr = {}
    dr["x"] = nc.dram_tensor("x", [S, D], F32, kind="ExternalInput").ap()
    for k, shp in list(WEIGHT_SHAPES.items()) + list(CONST_SHAPES.items()) + list(PARAM_SHAPES.items()) + list(PACKED_SHAPES.items()):
        dr[k] = nc.dram_tensor(k, list(shp), F32, kind="ExternalInput").ap()
    out_d = nc.dram_tensor("out", [S, D], F32, kind="ExternalOutput").ap()
    hscr = nc.dram_tensor("hscr", [S, D], F32).ap()
    P = Prog(nc)
    C = Ctx()
    C.nc, C.P, C.dr, C.out_d, C.hscr, C.taps = nc, P, dr, out_d, hscr, set(taps)
    C.tap_out = {}
    with P.stack:
        C.ps = [nc.alloc_psum_tensor(f"ps{i}", [128, 512], F32) for i in range(8)]
        C.PB = [Buf(f"ps{i}", excl=True, persist=True) for i in range(8)]
        C.ps_rr = 0
        C.ps_reserved = set()
        C.xT = nc.alloc_sbuf_tensor("xT", [128, 8, S], BF16)
        C.XT = [Buf(f"xT{t}", persist=True) for t in range(NT)]
        C.OAT = [Buf(f"oAT{t}", persist=True) for t in range(NT)]
        C.OBT = [Buf(f"oBT{t}", persist=True) for t in range(NT)]
        C.wst = [nc.alloc_sbuf_tensor(f"wst{i}", [128, 8, 256], BF16) for i in range(3)]
        C.WST = [Buf(f"wst{i}", persist=True) for i in range(3)]
        C.wst_tl = [P.new_dma_tl(f"wst{i}") for i in range(3)]
        C.wk = 0
        C.CONST = Buf("const")
        C.tl_const = {"sp": P.new_dma_tl("const_sp"), "pool": P.new_dma_tl("const_pool")}
        load_consts(C)
        with ExitStack() as ab:
            C.oAT = ab.enter_context(nc.sbuf_tensor("oAT", [128, 4, S], BF16))
            with ExitStack() as ph:
                phase_x(C, ph)
                if "gdn" in phases:
                    phase_gdn(C, ph)
            P.barrier()
            C.oBT = ab.enter_context(nc.sbuf_tensor("oBT", [128, 4, S], BF16))
            if "nsa" in phases:
                with ExitStack() as ph:
                    phase_nsa(C, ph)
                P.barrier()
            if "mix" in phases:
                with ExitStack() as ph:
                    phase_mix(C, ph)
                P.barrier()
        if "ffn" in phases:
            with ExitStack() as ph:
                phase_ffn(C, ph)
        P.final_wait("sp", C.final_bufs)
        P.emit()
    return nc, C


def getps(C):
    while True:
        i = C.ps_rr
        C.ps_rr = (i + 1) % 8
        if i not in C.ps_reserved:
            return C.ps[i], C.PB[i]


def reserve_ps(C):
    while True:
        i = C.ps_rr
        C.ps_rr = (i + 1) % 8
        if i not in C.ps_reserved:
            C.ps_reserved.add(i)
            return i, C.ps[i], C.PB[i]


def const_dma(C, eng, out, in_):
    P = C.P
    tl = C.tl_const[eng]
    nbytes = _free(out) * int(out.shape[0]) * 4

    def fn(e, out=out, in_=in_):
        return e.dma_start(out=out, in_=in_)

    o = Op(len(P.ops), eng, fn, tl, set(), P.epoch, 2200.0 + nbytes / 150.0, 150.0 if eng == "sp" else 400.0)
    P.ops.append(o)
    if C.CONST.also is None:
        C.CONST.also = []
    C.CONST.also.append(o)


def cload(C, nc_alloc, name, key, shape, dtype=F32):
    t = nc_alloc(name, list(shape), dtype)
    const_dma(C, "pool" if dtype != F32 else "sp", t[:], C.dr[key])
    return t


def load_consts(C):
    nc = C.nc
    K = Ctx()
    C.K = K

    def al(name, shape, dtype):
        return nc.alloc_sbuf_tensor(name, shape, dtype)

    K.ident_f = cload(C, al, "k_identf", "c_ident", [128, 128])
    K.ident_b = cload(C, al, "k_identb", "c_ident", [128, 128], BF16)
    K.ones_b = nc.alloc_sbuf_tensor("k_onesb", [128, 128], BF16)
    C.ONES = Buf("ones")
    C.P.op("dve", lambda e: e.memset(K.ones_b[:], 1.0), writes=[C.ONES])


def tap(C, name, ap, shape, rd):
    if name not in C.taps:
        return
    t = C.nc.dram_tensor("tap_" + name, list(shape), ap.dtype, kind="ExternalOutput").ap()
    b = Buf("tap_" + name)
    C.P.dma("sp", t, ap, reads=rd, writes=[b])
    C.final_bufs.append(b)
    C.tap_out[name] = "tap_" + name


def phase_x(C, phx):
    nc, P, dr, K = C.nc, C.P, C.dr, C.K
    C.final_bufs = []
    xb = [phx.enter_context(nc.sbuf_tensor(f"xb{i}", [128, D], BF16)) for i in range(2)]
    XB = [Buf(f"xb{i}") for i in range(2)]
    tls = [P.new_dma_tl(f"xb{i}") for i in range(2)]
    for tt in range(NT):
        s = tt % 2
        P.dma("pool", xb[s][:, :], dr["x"][tt * 128:(tt + 1) * 128, :], writes=[XB[s]], tl=tls[s])
        pb, PBb = getps(C)
        pbv = pb.bitcast(BF16)
        for c in range(8):
            P.op("pe", lambda e, o=pbv[:, c * 128:(c + 1) * 128], i=xb[s][:, c * 128:(c + 1) * 128]:
                 e.transpose(o, i, K.ident_b[:, :]), reads=[XB[s], C.CONST], writes=[PBb])
        o = C.xT[:, :, tt * 128:(tt + 1) * 128]
        i = pbv[:, :].rearrange("p (c t) -> p c t", c=8)
        if tt % 2 == 0:
            P.op("act", lambda e, o=o, i=i: e.copy(o, i), reads=[PBb], writes=[C.XT[tt]])
        else:
            P.op("dve", lambda e, o=o, i=i: e.tensor_copy(o, i), reads=[PBb], writes=[C.XT[tt]])


def mm(C, out, lhsT, rhs, start, stop, rd, wr):
    C.P.op("pe", lambda e: e.matmul(out, lhsT, rhs, start=start, stop=stop), reads=rd, writes=wr,
           cost=PE_FIX + PE_COL * _free(rhs))


def actf(C, out, in_, func, rd, wr, bias=None, scale=None):
    kw = {}
    if bias is not None:
        kw["bias"] = bias
    if scale is not None:
        kw["scale"] = scale
    C.P.op("act", lambda e: e.activation(out, in_, func, **kw), reads=rd, writes=wr, cost=220.0 + 0.75 * _free(out))


def cpy(C, eng, out, in_, rd, wr):
    if eng == "act":
        C.P.op("act", lambda e: e.copy(out, in_), reads=rd, writes=wr, cost=220.0 + 0.75 * _free(out))
    else:
        C.P.op(eng, lambda e: e.tensor_copy(out, in_), reads=rd, writes=wr, cost=(120.0 + 1.05 * _free(out)) * (6.0 if eng == "pool" else 1.0))


def tt(C, eng, out, in0, in1, op, rd, wr):
    C.P.op(eng, lambda e: e.tensor_tensor(out, in0, in1, op), reads=rd, writes=wr, cost=(120.0 + 1.1 * _free(out)) * (6.0 if eng == "pool" else 1.0))


def ts(C, eng, out, in0, s1, op0, rd, wr, s2=None, op1=None):
    if op1 is None:
        C.P.op(eng, lambda e: e.tensor_scalar(out, in0, s1, None, op0), reads=rd, writes=wr, cost=(120.0 + 1.0 * _free(out)) * (6.0 if eng == "pool" else 1.0))
    else:
        C.P.op(eng, lambda e: e.tensor_scalar(out, in0, s1, s2, op0, op1), reads=rd, writes=wr, cost=(120.0 + 1.0 * _free(out)) * (6.0 if eng == "pool" else 1.0))


def stt(C, eng, out, in0, scalar, in1, op0, op1, rd, wr):
    C.P.op(eng, lambda e: e.scalar_tensor_tensor(out, in0, scalar, in1, op0, op1), reads=rd, writes=wr, cost=120.0 + 1.4 * _free(out))


def load_w(C, dst, key, col0, ncols, wr, tl=None, row0=0, nrow_chunks=8):
    src = C.dr[key][row0:row0 + 128 * nrow_chunks, col0:col0 + ncols].rearrange("(c p) n -> p c n", p=128)
    C.P.dma("pool", dst, src, writes=wr, tl=tl)


def bc_last(ap, n):
    shp = list(ap.shape)
    return ap.unsqueeze(len(shp)).to_broadcast(shp + [n])


def bc_mid(ap, n):
    shp = list(ap.shape)
    return ap.unsqueeze(1).to_broadcast([shp[0], n] + shp[1:])


def phase_gdn(C, ph):
    nc, P, dr, K = C.nc, C.P, C.dr, C.K
    CONST = C.CONST

    def sb(name, shape, dtype):
        return ph.enter_context(nc.sbuf_tensor("g_" + name, list(shape), dtype))

    K.tri = cload(C, sb, "k_tri", "c_tri", [128, 128], BF16)
    K.blk = cload(C, sb, "k_blk", "c_blk", [128, 128], BF16)
    K.nega = cload(C, sb, "k_nega", "c_nega", [128, 128], BF16)
    K.negq = cload(C, sb, "k_negq", "c_negq", [128, 128], BF16)
    K.sel12 = cload(C, sb, "k_sel12", "c_sel12", [12, 4, 128], BF16)
    K.selrow = cload(C, sb, "k_selrow", "c_selrow", [128, 2, 128], BF16)
    K.gconv = cload(C, sb, "k_gconv", "p_gconv", [128, 12, 4])
    K.alog = cload(C, sb, "k_alog", "p_alog", [128, NT, 4])
    K.dtb = cload(C, sb, "k_dtb", "p_dtb", [128, NT, 4])
    K.gnw = cload(C, sb, "k_gnw", "p_gnw", [128, 4, 128])

    qT = sb("qT", [128, 4, S], BF16)
    kT = sb("kT", [128, 4, S], BF16)
    vT = sb("vT", [128, 4, S], BF16)
    QKV = {"q": [Buf(f"qT{h}") for h in range(4)], "k": [Buf(f"kT{h}") for h in range(4)],
           "v": [Buf(f"vT{h}") for h in range(4)]}
    qkvT = {"q": qT, "k": kT, "v": vT}

    wba = sb("wba", [128, 8, 8], BF16)
    WBA = Buf("wba")
    load_w(C, wba[:], "w_in", C_GB, 8, [WBA])
    ba = sb("ba", [128, NT, 8], F32)
    BA = Buf("ba")
    pb, PBb = getps(C)
    for t in range(NT):
        for c in range(8):
            mm(C, pb[:, t * 8:(t + 1) * 8], C.xT[:, c, t * 128:(t + 1) * 128], wba[:, c, :], c == 0, c == 7,
               [C.XT[t], WBA], [PBb])
    cpy(C, "dve", ba[:].rearrange("p t c -> p (t c)"), pb[:, 0:NT * 8], [PBb], [BA])

    import os
    stop = os.environ.get("GDN_STOP", "")
    if stop == "a0":
        return
    ss = sb("ss", [128, NT, 8], F32)
    SSb = Buf("ss")
    ph1 = ExitStack()
    sb_outer = sb

    def sb(name, shape, dtype):
        return ph1.enter_context(nc.sbuf_tensor("g_" + name, list(shape), dtype))

    wq = [t[:, :, 0:128] for t in C.wst]
    WQ = C.WST
    tlw = C.wst_tl
    raw = [sb(f"raw{i}", [128, S + 3], F32) for i in range(2)]
    RAW = [Buf(f"raw{i}") for i in range(2)]
    acc = [sb(f"acc{i}", [128, S], F32) for i in range(2)]
    ACC = [Buf(f"acc{i}") for i in range(2)]
    sq = [sb(f"sq{i}", [128, S], BF16) for i in range(2)]
    SQ = [Buf(f"sq{i}") for i in range(2)]
    for i in range(2):
        P.op("dve", lambda e, o=raw[i][:, 0:3]: e.memset(o, 0.0), writes=[RAW[i]])
    ss_i, ss_pb, SS_PB = reserve_ps(C)
    n_ss = 0
    for ck in range(12):
        which = "qkv"[ck // 4]
        h = ck % 4
        ws = C.wk % 3
        C.wk += 1
        P.dma("pool", wq[ws], dr["pk_g"][ck], writes=[WQ[ws]], tl=tlw[ws])
        rs = ck % 2
        for tb in range(4):
            pb, PBb = getps(C)
            for c in range(8):
                mm(C, pb[:, :], wq[ws][:, c, :], C.xT[:, c, tb * 512:(tb + 1) * 512], c == 0, c == 7,
                   [WQ[ws]] + C.XT[tb * 4:(tb + 1) * 4], [PBb])
            cpy(C, "act", raw[rs][:, 3 + tb * 512:3 + (tb + 1) * 512], pb[:, :], [PBb], [RAW[rs]])
        ceng = "dve"
        ts(C, ceng, acc[rs][:, :], raw[rs][:, 3:S + 3], K.gconv[:, ck, 3:4], ALU.mult, [RAW[rs], CONST], [ACC[rs]])
        for j in (2, 1, 0):
            stt(C, ceng, acc[rs][:, :], raw[rs][:, j:S + j], K.gconv[:, ck, j:j + 1], acc[rs][:, :], ALU.mult, ALU.add,
                [RAW[rs], ACC[rs], CONST], [ACC[rs]])
        dst = qkvT[which][:, h, :]
        actf(C, dst, acc[rs][:, :], AF.Silu, [ACC[rs]], [QKV[which][h]])
        if which in "qk":
            col = (0 if which == "q" else 4) + h
            actf(C, sq[rs][:, :], dst, AF.Square, [QKV[which][h]], [SQ[rs]])
            for t in range(NT):
                mm(C, ss_pb[:, t * 8 + col:t * 8 + col + 1], sq[rs][:, t * 128:(t + 1) * 128], K.ones_b[:, 0:1],
                   True, True, [SQ[rs], C.ONES], [SS_PB])
    cpy(C, "dve", ss[:].rearrange("p t c -> p (t c)"), ss_pb[:, 0:NT * 8], [SS_PB], [SSb])
    C.ps_reserved.discard(ss_i)
    P.barrier()
    ph1.close()
    sb = sb_outer
    tap(C, "qT", qT[:, 0, :], [128, S], QKV["q"])
    tap(C, "vT", vT[:, 1, :], [128, S], QKV["v"])

    if stop == "a1":
        return
    names = ["t1", "t2", "t3", "spa", "spb", "g", "lb", "gc", "gl", "lrk", "lrq", "biasA", "biasQ", "rowQ",
             "s_kbg", "s_kdec", "beta", "s_o", "ea", "ya", "yb"]
    st = {n: sb("st_" + n, [128, NT, 4], F32) for n in names}
    SB_ = {n: Buf("st_" + n) for n in names}

    def softplus(dst, y):
        actf(C, st["t1"][:], st[y][:], AF.Abs, [SB_[y]], [SB_["t1"]])
        actf(C, st["t2"][:], st["t1"][:], AF.Exp, [SB_["t1"]], [SB_["t2"]], scale=-1.0)
        actf(C, st["t3"][:], st["t2"][:], AF.Ln, [SB_["t2"]], [SB_["t3"]], bias=1.0)
        stt(C, "dve", st[dst][:], st[y][:], 0.0, st["t3"][:], ALU.max, ALU.add, [SB_[y], SB_["t3"]], [SB_[dst]])

    tt(C, "dve", st["ya"][:], ba[:, :, 4:8], K.dtb[:], ALU.add, [BA, CONST], [SB_["ya"]])
    softplus("spa", "ya")
    actf(C, st["ea"][:], K.alog[:], AF.Exp, [CONST], [SB_["ea"]])
    stt(C, "dve", st["g"][:], st["spa"][:], -1.0, st["ea"][:], ALU.mult, ALU.mult, [SB_["spa"], SB_["ea"]], [SB_["g"]])
    ts(C, "dve", st["yb"][:], ba[:, :, 0:4], -1.0, ALU.mult, [BA], [SB_["yb"]])
    softplus("spb", "yb")
    ts(C, "dve", st["lb"][:], st["spb"][:], -1.0, ALU.mult, [SB_["spb"]], [SB_["lb"]])
    lrt = sb("lrt", [128, NT, 8], F32)
    LRT = Buf("lrt")
    actf(C, lrt[:], ss[:], AF.Ln, [SSb], [LRT], bias=RMS_EPS)
    ts(C, "dve", st["lrq"][:], lrt[:, :, 0:4], -0.5, ALU.mult, [LRT], [SB_["lrq"]], s2=-0.5 * math.log(128.0), op1=ALU.add)
    ts(C, "dve", st["lrk"][:], lrt[:, :, 4:8], -0.5, ALU.mult, [LRT], [SB_["lrk"]])
    spl_r = sb("spl_r", [128, NT, 4], F32)
    SPLR = Buf("spl_r")

    def split3(name, src):
        x3 = sb("x3_" + name, [128, 3, NT, 4], BF16)
        X3 = Buf("x3_" + name)
        cur, CUR = st[src], SB_[src]
        for k in range(3):
            cpy(C, "dve", x3[:, k, :, :], cur[:], [CUR], [X3])
            if k < 2:
                tt(C, "dve", spl_r[:], cur[:], x3[:, k, :, :], ALU.subtract, [CUR, X3], [SPLR])
                cur, CUR = spl_r, SPLR
        return x3, X3

    g3, G3 = split3("g", "g")
    pb, PBb = getps(C)
    for t in range(NT):
        for k in range(3):
            mm(C, pb[:, t * 4:(t + 1) * 4], K.tri[:, :], g3[:, k, t, :], k == 0, k == 2, [CONST, G3], [PBb])
        for k in range(3):
            mm(C, pb[:, 64 + t * 4:64 + (t + 1) * 4], K.blk[:, :], g3[:, k, t, :], k == 0, k == 2, [CONST, G3], [PBb])
    cpy(C, "dve", st["gc"][:].rearrange("p t c -> p (t c)"), pb[:, 0:64], [PBb], [SB_["gc"]])
    cpy(C, "dve", st["gl"][:].rearrange("p t c -> p (t c)"), pb[:, 64:128], [PBb], [SB_["gl"]])
    tt(C, "dve", st["biasQ"][:], st["lrk"][:], st["gc"][:], ALU.subtract, [SB_["lrk"], SB_["gc"]], [SB_["biasQ"]])
    tt(C, "dve", st["biasA"][:], st["gc"][:], st["lb"][:], ALU.add, [SB_["gc"], SB_["lb"]], [SB_["biasA"]])
    tt(C, "dve", st["biasA"][:], st["biasA"][:], st["lrk"][:], ALU.add, [SB_["biasA"], SB_["lrk"]], [SB_["biasA"]])
    tt(C, "dve", st["rowQ"][:], st["gc"][:], st["lrq"][:], ALU.add, [SB_["gc"], SB_["lrq"]], [SB_["rowQ"]])
    actf(C, st["s_kbg"][:], st["biasA"][:], AF.Exp, [SB_["biasA"]], [SB_["s_kbg"]])
    tt(C, "dve", st["t1"][:], st["biasQ"][:], st["gl"][:], ALU.add, [SB_["biasQ"], SB_["gl"]], [SB_["t1"]])
    actf(C, st["s_kdec"][:], st["t1"][:], AF.Exp, [SB_["t1"]], [SB_["s_kdec"]])
    actf(C, st["beta"][:], st["lb"][:], AF.Exp, [SB_["lb"]], [SB_["beta"]])
    actf(C, st["s_o"][:], st["rowQ"][:], AF.Exp, [SB_["rowQ"]], [SB_["s_o"]])
    tap(C, "g", st["g"][:], [128, NT, 4], [SB_["g"]])
    tap(C, "gc", st["gc"][:], [128, NT, 4], [SB_["gc"]])
    tap(C, "beta", st["beta"][:], [128, NT, 4], [SB_["beta"]])
    tap(C, "lrk", st["lrk"][:], [128, NT, 4], [SB_["lrk"]])

    if stop == "stats":
        return
    eglb = sb("eglb", [128, 2, NT * 4], F32)
    EGLB = Buf("eglb")
    gl3, GL3 = split3("gl", "gl")
    pb, PBb = getps(C)
    for c in range(2):
        for k in range(3):
            mm(C, pb[:, c * 64:(c + 1) * 64], K.selrow[:, c, :], gl3[:, k, :, :].rearrange("p t h -> p (t h)"),
               k == 0, k == 2, [CONST, GL3], [PBb])
    actf(C, eglb[:].rearrange("p c n -> p (c n)"), pb[:, 0:128], AF.Exp, [PBb], [EGLB])
    bq3, BQ3 = split3("bq", "biasQ")
    rq3, RQ3 = split3("rq", "rowQ")

    if stop == "eglb":
        return
    def dbl(name, shape, dtype, n=2):
        return [sb(f"{name}{i}", shape, dtype) for i in range(n)], [Buf(f"{name}{i}") for i in range(n)]

    H4 = [128, 4, 128]
    kbg, KBG = dbl("kbg", H4, BF16)
    kdec, KDEC = dbl("kdec", H4, BF16, 3)
    vb, VB = dbl("vb", H4, BF16)
    expA, EXPA = dbl("expA", H4, F32)
    expQ, EXPQ = dbl("expQ", H4, F32)
    Xs, XS = dbl("X", H4, BF16, 6)
    Ys, YS = dbl("Y", H4, BF16, 6)
    Ws, WS = dbl("W", H4, BF16, 6)
    qkT, QKT = dbl("qkT", H4, BF16, 3)
    u0, U0 = dbl("u0", H4, F32, 3)
    kcdT, KCDT = dbl("kcdT", H4, BF16, 3)
    ut, UT = dbl("u", H4, BF16)
    ot, OT = dbl("o", H4, F32)
    tmpo, TMPO = dbl("tmpo", H4, F32)
    Sst = sb("S", H4, F32)
    Sb = sb("Sb", H4, BF16)
    SST, SBB = Buf("S"), Buf("Sb")
    P.op("dve", lambda e: e.memset(Sst[:], 0.0), writes=[SST])
    P.op("pool", lambda e: e.memset(Sb[:], 0.0), writes=[SBB])
    wz = sb("wz", [128, 8, 512], BF16)
    WZ = Buf("wz")
    load_w(C, wz[:], "w_in", C_GZ, 512, [WZ])
    sz, SZ = dbl("sz", [128, 512], F32)
    osq, OSQ = dbl("osq", H4, F32)
    rr_, RR = dbl("rr", [128, 4], F32)
    oab, OAB = dbl("oab", H4, BF16)
    ident4 = sb("ident4", H4, F32)
    ID4 = Buf("ident4")
    for h in range(4):
        cpy(C, "dve", ident4[:, h, :], K.ident_b[:, :], [CONST], [ID4])
    rowA, ROWA = dbl("rowA", [12, 128], BF16)
    rowQ, ROWQ = dbl("rowQ", [12, 128], BF16)
    r12, R12 = dbl("r12", [128, 2, 4, 3], BF16)

    def prep(p):
        s = p % 2
        s3 = p % 3
        tsl = slice(p * 128, (p + 1) * 128)
        pbk, PBK = getps(C)
        pbkv = pbk.bitcast(BF16)
        for h in range(4):
            P.op("pe", lambda e, o=pbkv[:, h * 128:(h + 1) * 128], i=kT[:, h, tsl]: e.transpose(o, i, K.ident_b[:, :]),
                 reads=[QKV["k"][h], CONST], writes=[PBK])
        kin = pbkv[:, 0:512].rearrange("p (h d) -> p h d", h=4)
        tt(C, "dve", kbg[s][:], kin, bc_last(st["s_kbg"][:, p, :], 128), ALU.mult, [PBK, SB_["s_kbg"]], [KBG[s]])
        tt(C, "dve", kdec[s3][:], kin, bc_last(st["s_kdec"][:, p, :], 128), ALU.mult, [PBK, SB_["s_kdec"]], [KDEC[s3]])
        pbv_, PBV = getps(C)
        pbvv = pbv_.bitcast(BF16)
        for h in range(4):
            P.op("pe", lambda e, o=pbvv[:, h * 128:(h + 1) * 128], i=vT[:, h, tsl]: e.transpose(o, i, K.ident_b[:, :]),
                 reads=[QKV["v"][h], CONST], writes=[PBV])
        vin = pbvv[:, 0:512].rearrange("p (h d) -> p h d", h=4)
        tt(C, "dve", vb[s][:], vin, bc_last(st["beta"][:, p, :], 128), ALU.mult, [PBV, SB_["beta"]], [VB[s]])
        yield
        cpy(C, "dve", r12[s][:, 0, :, :], bq3[:, :, p, :].rearrange("p k h -> p h k"), [BQ3], [R12[s]])
        cpy(C, "dve", r12[s][:, 1, :, :], rq3[:, :, p, :].rearrange("p k h -> p h k"), [RQ3], [R12[s]])
        prw, PRW = getps(C)
        prwv = prw.bitcast(BF16)
        P.op("pe", lambda e: e.transpose(prwv[0:12, 0:128], r12[s][:, 0, :, :].rearrange("p h k -> p (h k)"), K.ident_b[:, :]),
             reads=[R12[s], CONST], writes=[PRW])
        P.op("pe", lambda e: e.transpose(prwv[0:12, 128:256], r12[s][:, 1, :, :].rearrange("p h k -> p (h k)"), K.ident_b[:, :]),
             reads=[R12[s], CONST], writes=[PRW])
        cpy(C, "dve", rowA[s][0:12, :], prwv[0:12, 0:128], [PRW], [ROWA[s]])
        cpy(C, "dve", rowQ[s][0:12, :], prwv[0:12, 128:256], [PRW], [ROWQ[s]])
        pea, PEA = getps(C)
        peq, PEQ = getps(C)
        for h in range(4):
            hs = slice(h * 128, (h + 1) * 128)
            mm(C, pea[:, hs], K.sel12[0:12, h, :], rowA[s][0:12, :], True, False, [CONST, ROWA[s]], [PEA])
            mm(C, pea[:, hs], K.ident_b[:, :], K.nega[:, :], False, True, [CONST], [PEA])
            mm(C, peq[:, hs], K.sel12[0:12, h, :], rowQ[s][0:12, :], True, False, [CONST, ROWQ[s]], [PEQ])
            mm(C, peq[:, hs], K.ident_b[:, :], K.negq[:, :], False, True, [CONST], [PEQ])
        pkk, PKK = getps(C)
        pkq, PKQ = getps(C)
        for h in range(4):
            hs = slice(h * 128, (h + 1) * 128)
            mm(C, pkk[:, hs], kT[:, h, tsl], kT[:, h, tsl], True, True, [QKV["k"][h]], [PKK])
            mm(C, pkq[:, hs], kT[:, h, tsl], qT[:, h, tsl], True, True, [QKV["k"][h], QKV["q"][h]], [PKQ])
        for h in range(4):
            hs = slice(h * 128, (h + 1) * 128)
            actf(C, expA[s][:, h, :], pea[:, hs], AF.Exp, [PEA, SB_["biasA"]], [EXPA[s]], bias=st["biasA"][:, p, h:h + 1])
            actf(C, expQ[s][:, h, :], peq[:, hs], AF.Exp, [PEQ, SB_["biasQ"]], [EXPQ[s]], bias=st["biasQ"][:, p, h:h + 1])
        x0 = s * 3
        tt(C, "dve", Xs[x0][:].rearrange("p h d -> p (h d)"), pkk[:, :], expA[s][:].rearrange("p h d -> p (h d)"),
           ALU.mult, [PKK, EXPA[s]], [XS[x0]])
        tt(C, "dve", qkT[s3][:].rearrange("p h d -> p (h d)"), pkq[:, :], expQ[s][:].rearrange("p h d -> p (h d)"),
           ALU.mult, [PKQ, EXPQ[s]], [QKT[s3]])
        yield
        pbt, PBT = getps(C)
        pbtv = pbt.bitcast(BF16)
        for h in range(4):
            P.op("pe", lambda e, o=pbtv[:, h * 128:(h + 1) * 128], i=Xs[x0][:, h, :]: e.transpose(o, i, K.ident_b[:, :]),
                 reads=[XS[x0], CONST], writes=[PBT])
        bin_ = pbtv[:, 0:512].rearrange("p (h d) -> p h d", h=4)
        cpy(C, "act", Ys[x0][:], bin_, [PBT], [YS[x0]])
        tt(C, "dve", Ws[x0][:], ident4[:], bin_, ALU.subtract, [PBT, ID4], [WS[x0]])
        yield
        for lvl in range(1, 6):
            xi, xo = s * 3 + (lvl - 1) % 3, s * 3 + lvl % 3
            pa, PA = getps(C)
            for h in range(4):
                hs = slice(h * 128, (h + 1) * 128)
                mm(C, pa[:, hs], Ys[xi][:, h, :], Xs[xi][:, h, :], True, True, [YS[xi], XS[xi]], [PA])
            cpy(C, "act", Xs[xo][:].rearrange("p h d -> p (h d)"), pa[:, :], [PA], [XS[xo]])
            if lvl < 5:
                pbb, PBB_ = getps(C)
                for h in range(4):
                    hs = slice(h * 128, (h + 1) * 128)
                    mm(C, pbb[:, hs], Xs[xi][:, h, :], Ys[xi][:, h, :], True, True, [YS[xi], XS[xi]], [PBB_])
                cpy(C, "act", Ys[xo][:].rearrange("p h d -> p (h d)"), pbb[:, :], [PBB_], [YS[xo]])
            yield
            pw, PW = getps(C)
            for h in range(4):
                hs = slice(h * 128, (h + 1) * 128)
                mm(C, pw[:, hs], Xs[xo][:, h, :], Ws[xi][:, h, :], True, True, [XS[xo], WS[xi]], [PW])
            tt(C, "dve", Ws[xo][:].rearrange("p h d -> p (h d)"), pw[:, :], Ws[xi][:].rearrange("p h d -> p (h d)"),
               ALU.add, [PW, WS[xi]], [WS[xo]])
            yield
        wf = s * 3 + 5 % 3
        pu, PU = getps(C)
        pk, PK = getps(C)
        for h in range(4):
            hs = slice(h * 128, (h + 1) * 128)
            mm(C, pu[:, hs], Ws[wf][:, h, :], vb[s][:, h, :], True, True, [WS[wf], VB[s]], [PU])
            mm(C, pk[:, hs], kbg[s][:, h, :], Ws[wf][:, h, :], True, True, [WS[wf], KBG[s]], [PK])
        cpy(C, "act", u0[s3][:].rearrange("p h d -> p (h d)"), pu[:, :], [PU], [U0[s3]])
        cpy(C, "dve", kcdT[s3][:].rearrange("p h d -> p (h d)"), pk[:, :], [PK], [KCDT[s3]])
        yield

    def scan(p):
        s = p % 2
        s3 = p % 3
        tsl = slice(p * 128, (p + 1) * 128)
        for c in range(2):
            r = slice(64 * c, 64 * c + 64)
            n = 2 * p + c
            pm1, PM1 = getps(C)
            for h in range(4):
                hs = slice(h * 128, (h + 1) * 128)
                mm(C, pm1[:, hs], kcdT[s3][:, h, :], Sb[:, h, :], True, True, [KCDT[s3], SBB], [PM1])
            tt(C, "dve", ut[s][r, :, :].rearrange("p h d -> p (h d)"), u0[s3][r, :, :].rearrange("p h d -> p (h d)"),
               pm1[r, :], ALU.subtract, [U0[s3], PM1], [UT[s]])
            pm2i, pm2, PM2 = reserve_ps(C)
            for h in range(4):
                hs = slice(h * 128, (h + 1) * 128)
                mm(C, pm2[:, hs], qT[:, h, tsl], Sb[:, h, :], True, True, [QKV["q"][h], SBB], [PM2])
            yield
            pm3, PM3 = getps(C)
            pm4, PM4 = getps(C)
            for h in range(4):
                hs = slice(h * 128, (h + 1) * 128)
                mm(C, pm3[:, hs], qkT[s3][r, h, :], ut[s][r, h, :], True, True, [QKT[s3], UT[s]], [PM3])
                mm(C, pm4[:, hs], kdec[s3][r, h, :], ut[s][r, h, :], True, True, [KDEC[s3], UT[s]], [PM4])
            for h in range(4):
                hs = slice(h * 128, (h + 1) * 128)
                actf(C, tmpo[s][r, h, :], pm2[r, hs], AF.Identity, [PM2, SB_["s_o"]], [TMPO[s]], scale=st["s_o"][r, p, h:h + 1])
            C.ps_reserved.discard(pm2i)
            tt(C, "dve", ot[s][r, :, :].rearrange("p h d -> p (h d)"), tmpo[s][r, :, :].rearrange("p h d -> p (h d)"),
               pm3[r, :], ALU.add, [TMPO[s], PM3], [OT[s]])
            for h in range(4):
                hs = slice(h * 128, (h + 1) * 128)
                stt(C, "dve", Sst[:, h, :], Sst[:, h, :], eglb[:, c, p * 4 + h:p * 4 + h + 1], pm4[:, hs], ALU.mult, ALU.add,
                    [SST, EGLB, PM4], [SST])
            cpy(C, "act", Sb[:].rearrange("p h d -> p (h d)"), Sst[:].rearrange("p h d -> p (h d)"), [SST], [SBB])
            yield

    def outp(p):
        s = p % 2
        tsl = slice(p * 128, (p + 1) * 128)
        pz, PZ = getps(C)
        for c in range(8):
            mm(C, pz[:, :], C.xT[:, c, tsl], wz[:, c, :], c == 0, c == 7, [C.XT[p], WZ], [PZ])
        actf(C, sz[s][:, :], pz[:, :], AF.Silu, [PZ], [SZ[s]])
        o2 = ot[s][:].rearrange("p h d -> p (h d)")
        tt(C, "dve", osq[s][:].rearrange("p h d -> p (h d)"), o2, o2, ALU.mult, [OT[s]], [OSQ[s]])
        P.op("dve", lambda e: e.tensor_reduce(rr_[s][:, :], osq[s][:], AX.X, ALU.add), reads=[OSQ[s]], writes=[RR[s]])
        actf(C, rr_[s][:, :], rr_[s][:, :], AF.Ln, [RR[s]], [RR[s]], scale=1.0 / 128.0, bias=RMS_EPS)
        actf(C, rr_[s][:, :], rr_[s][:, :], AF.Exp, [RR[s]], [RR[s]], scale=-0.5)
        tt(C, "dve", osq[s][:], ot[s][:], bc_last(rr_[s][:, :], 128), ALU.mult, [OT[s], RR[s]], [OSQ[s]])
        tt(C, "dve", osq[s][:], osq[s][:], K.gnw[:], ALU.mult, [OSQ[s], CONST], [OSQ[s]])
        tt(C, "dve", oab[s][:].rearrange("p h d -> p (h d)"), osq[s][:].rearrange("p h d -> p (h d)"), sz[s][:, :],
           ALU.mult, [OSQ[s], SZ[s]], [OAB[s]])
        if p == 0:
            tap(C, "o_raw0", ot[s][:], [128, 4, 128], [OT[s]])
            tap(C, "oab0", oab[s][:], [128, 4, 128], [OAB[s]])
        if p == 9:
            tap(C, "o_raw9", ot[s][:], [128, 4, 128], [OT[s]])
        pt, PT = getps(C)
        ptv = pt.bitcast(BF16)
        for h in range(4):
            P.op("pe", lambda e, o=ptv[:, h * 128:(h + 1) * 128], i=oab[s][:, h, :]: e.transpose(o, i, K.ident_b[:, :]),
                 reads=[OAB[s], CONST], writes=[PT])
        cpy(C, "act", C.oAT[:, :, tsl], ptv[:, 0:512].rearrange("p (h d) -> p h d", h=4), [PT], [C.OAT[p]])
        yield

    SCAN_PRIO = int(os.environ.get("GDN_SCAN_PRIO", "1"))

    def scan_out(p):
        g_ = scan(p)
        while True:
            P.cur_prio = SCAN_PRIO
            try:
                next(g_)
            except StopIteration:
                P.cur_prio = 0
                break
            P.cur_prio = 0
            yield
        yield from outp(p)

    if stop != "":
        for p in range(1):
            nst = int(stop[4:]) if stop.startswith("prep") and len(stop) > 4 else 999
            for i_, _ in enumerate(prep(p)):
                if i_ + 1 >= nst:
                    break
            if stop.startswith("prep"):
                continue
            for _ in scan(p):
                pass
            if stop == "scan":
                continue
            for _ in outp(p):
                pass
    else:
        npar = int(os.environ.get("GDN_NPAR", "2"))
        active = []
        next_prep = 0
        next_scan = 0
        prep_done = set()
        scan_gen = None
        while next_scan < NT:
            while len(active) < npar and next_prep < NT and next_prep <= next_scan + 2:
                active.append([next_prep, prep(next_prep)])
                next_prep += 1
            if scan_gen is None and next_scan in prep_done:
                scan_gen = scan_out(next_scan)
            for ent in list(active):
                try:
                    next(ent[1])
                except StopIteration:
                    prep_done.add(ent[0])
                    active.remove(ent)
            if scan_gen is not None:
                try:
                    next(scan_gen)
                except StopIteration:
                    scan_gen = None
                    next_scan += 1
    tap(C, "oAT", C.oAT[:, 0, :], [128, S], C.OAT)


def mm2(C, out, lhsT, rhs, start, stop, rd, wr):
    C.P.op("pe", lambda e: e.matmul(out, lhsT, rhs, start=start, stop=stop, skip_group_check=True), reads=rd, writes=wr,
           cost=PE_FIX + PE_COL * _free(rhs))


def phase_nsa(C, ph):
    import os
    nc, P, dr, K = C.nc, C.P, C.dr, C.K
    CONST = C.CONST
    stop = os.environ.get("NSA_STOP", "")

    def sb(name, shape, dtype):
        return ph.enter_context(nc.sbuf_tensor("n_" + name, list(shape), dtype))

    phs1 = ExitStack()

    def sb1(name, shape, dtype):
        return phs1.enter_context(nc.sbuf_tensor("n_" + name, list(shape), dtype))

    K.cmpmask = cload(C, sb, "k_cmpmask", "c_cmpmask", [128, S], BF16)
    K.overlap = cload(C, sb, "k_overlap", "c_overlap", [128, 32], BF16)
    K.causal = cload(C, sb, "k_causal", "c_causal", [128, 128], BF16)
    K.anti = cload(C, sb, "k_anti", "c_anti", [128, 128], BF16)
    K.expand = cload(C, sb, "k_expand", "c_expand", [32, 16, 128], BF16)
    K.forced = cload(C, sb, "k_forced", "c_forced", [128, 8, 32])
    K.poskT = cload(C, sb, "k_poskT", "p_poskT", [128, 32], BF16)
    K.posvT = cload(C, sb, "k_posvT", "p_posvT", [128, 32], BF16)

    QT = sb("QT", [64, 8, S], BF16)
    QTB = [Buf(f"QT{i}") for i in range(8)]
    KsT = sb("KsT", [64, 2, S], BF16)
    KwT = sb("KwT", [64, 2, S], BF16)
    KST = [Buf(f"KsT{g}") for g in range(2)]
    KWT = [Buf(f"KwT{g}") for g in range(2)]
    KcTc = sb("KcTc", [64, 2, 128], BF16)
    Vca = sb("Vca", [128, 2, 97], BF16)
    Vs = sb("Vs", [128, NT, 2, 65], BF16)
    Vw = sb("Vw", [128, NT, 2, 65], BF16)
    VS, VW = Buf("Vs"), Buf("Vw")
    gts = sb("gates", [128, NT, 24], F32)
    KcT = sb1("KcT", [128, S], BF16)
    VcT = sb1("VcT", [128, S], BF16)
    KCT, VCT = Buf("KcT"), Buf("VcT")
    GTS = Buf("gates")
    P.op("pool", lambda e: e.memset(Vs[:], 1.0), writes=[VS])
    P.op("pool", lambda e: e.memset(Vw[:], 1.0), writes=[VW])

    wt = [t[:, :, 0:128] for t in C.wst]
    WT = C.WST
    tlw = C.wst_tl
    hi = [sb1(f"hi{i}", [128, 512], BF16) for i in range(2)]
    HI = [Buf(f"hi{i}") for i in range(2)]
    nhi = 0
    jobs = [("q", 0), ("q", 1), ("q", 2), ("q", 3), ("ks", 0), ("kw", 0), ("kc", 0), ("vc", 0)]
    for ji, (kind, idx) in enumerate(jobs):
        ws = C.wk % 3
        C.wk += 1
        P.dma("pool", C.wst[ws][:, :, 0:128], dr["pk_n128"][ji], writes=[WT[ws]], tl=tlw[ws])
        for tb in range(4):
            pb, PBb = getps(C)
            for c in range(8):
                mm(C, pb[:, :], wt[ws][:, c, :], C.xT[:, c, tb * 512:(tb + 1) * 512], c == 0, c == 7,
                   [WT[ws]] + C.XT[tb * 4:(tb + 1) * 4], [PBb])
            tsl = slice(tb * 512, (tb + 1) * 512)
            if kind == "kc":
                cpy(C, "act", KcT[:, tsl], pb[:, :], [PBb], [KCT])
            elif kind == "vc":
                cpy(C, "dve", VcT[:, tsl], pb[:, :], [PBb], [VCT])
            else:
                hs_ = nhi % 2
                nhi += 1
                if kind == "q":
                    actf(C, QT[:, 2 * idx, tsl], pb[0:64, :], AF.Copy, [PBb], [QTB[2 * idx]], scale=0.125)
                    actf(C, hi[hs_][64:128, :], pb[64:128, :], AF.Copy, [PBb], [HI[hs_]], scale=0.125)
                    P.dma("sp", QT[:, 2 * idx + 1, tsl], hi[hs_][64:128, :], reads=[HI[hs_]], writes=[QTB[2 * idx + 1]])
                else:
                    dst, DST = (KsT, KST) if kind == "ks" else (KwT, KWT)
                    cpy(C, "dve", dst[:, 0, tsl], pb[0:64, :], [PBb], [DST[0]])
                    cpy(C, "dve", hi[hs_][64:128, :], pb[64:128, :], [PBb], [HI[hs_]])
                    P.dma("sp", dst[:, 1, tsl], hi[hs_][64:128, :], reads=[HI[hs_]], writes=[DST[1]])
    wv = sb1("wv", [128, 8, 280], BF16)
    WV = Buf("wv")
    tlv = P.new_dma_tl("nwv")
    for (c0, ncol, dcol) in ((C_NVS, 128, 0), (C_NVW, 128, 128), (C_NG, 24, 256)):
        src = dr["w_in"][:, c0:c0 + ncol].rearrange("(c p) n -> p c n", p=128)
        P.dma("pool", wv[:, :, dcol:dcol + ncol], src, writes=[WV], tl=tlv)
    gtmp = sb1("gtmp", [128, NT, 24], F32)
    GTMP = Buf("gtmp")
    for t in range(NT):
        pb, PBb = getps(C)
        for c in range(8):
            mm(C, pb[:, 0:280], C.xT[:, c, t * 128:(t + 1) * 128], wv[:, c, :], c == 0, c == 7, [C.XT[t], WV], [PBb])
        cpy(C, "act", Vs[:, t, :, 0:64], pb[:, 0:128].rearrange("p (g d) -> p g d", g=2), [PBb], [VS])
        cpy(C, "dve", Vw[:, t, :, 0:64], pb[:, 128:256].rearrange("p (g d) -> p g d", g=2), [PBb], [VW])
        actf(C, gtmp[:, t, :], pb[:, 256:280], AF.Tanh, [PBb], [GTMP], scale=0.5)
    ts(C, "dve", gts[:], gtmp[:], 0.5, ALU.mult, [GTMP], [GTS], s2=0.5, op1=ALU.add)
    if stop == "proj":
        tap(C, "x_QT", QT[:, 3, :], [64, S], QTB)
        tap(C, "x_KsT", KsT[:, 1, :], [64, S], KST)
        tap(C, "x_Vw", Vw[:], [128, NT, 2, 65], [VW])
        tap(C, "x_gates", gts[:], [128, NT, 24], [GTS])
        return

    KCTC, VCA = Buf("KcTc"), Buf("Vca")
    P.op("pool", lambda e: e.memset(KcTc[:], 0.0), writes=[KCTC])
    P.op("pool", lambda e: e.memset(Vca[:], 0.0), writes=[VCA])
    w1 = sb1("w1", [128, 32, 256], BF16)
    W1B = Buf("w1")
    tl1 = P.new_dma_tl("nw1")
    w2k = sb1("w2k", [128, 2, 64], BF16)
    w2v = sb1("w2v", [128, 2, 64], BF16)
    W2K, W2V = Buf("w2k"), Buf("w2v")
    P.dma("pool", w2k[:, :, :], dr["cmp_w2_k"].rearrange("(j c) d -> c j d", c=128), writes=[W2K])
    P.dma("pool", w2v[:, :, :], dr["cmp_w2_v"].rearrange("(j c) d -> c j d", c=128), writes=[W2V])
    hx = sb1("hx", [128, 128], F32)
    hx2 = sb1("hx2", [128, 128], F32)
    hth = sb1("hth", [128, 128], F32)
    h1 = sb1("h1", [128, 2, 128], BF16)
    b1 = sb1("b1", [128, 2], F32)
    HX, HX2, HTH, H1, B1 = Buf("hx"), Buf("hx2"), Buf("hth"), Buf("h1"), Buf("b1")
    for kv in ("k", "v"):
        key = "cmp_w1_" + kv
        src = dr[key].rearrange("(l d) c -> d l c", d=64)
        P.dma("pool", w1[0:64, :, :], src, writes=[W1B], tl=tl1)
        P.dma("pool", w1[64:128, :, :], src, writes=[W1B], tl=tl1)
        XcT, XCT = (KcT, KCT) if kv == "k" else (VcT, VCT)
        posT = K.poskT if kv == "k" else K.posvT
        for g in range(2):
            hr = slice(64 * g, 64 * g + 64)
            pbb, PBB_ = getps(C)
            for j in range(2):
                for l in range(32):
                    mm(C, pbb[:, j:j + 1], w1[hr, l, j * 128:(j + 1) * 128], posT[hr, l:l + 1], l == 0, l == 31,
                       [W1B, CONST], [PBB_])
            cpy(C, "dve", b1[:, :], pbb[:, 0:2], [PBB_], [B1])
            P.op("dve", lambda e: e.memset(h1[:], 0.0), writes=[H1])
            for j in range(2):
                ph1, PH1 = getps(C)
                xv = XcT[hr, :].rearrange("p (n r) -> p n r", r=16)
                for l in range(32):
                    rhs = xv[:, (l // 16):(l // 16) + 127, l % 16]
                    mm(C, ph1[:, 0:127], w1[hr, l, j * 128:(j + 1) * 128], rhs, l == 0, l == 31, [W1B, XCT], [PH1])
                ts(C, "dve", hx[:, 0:127], ph1[:, 0:127], b1[:, j:j + 1], ALU.add, [PH1, B1], [HX])
                tt(C, "dve", hx2[:, 0:127], hx[:, 0:127], hx[:, 0:127], ALU.mult, [HX], [HX2])
                ts(C, "dve", hx2[:, 0:127], hx2[:, 0:127], 0.044715, ALU.mult, [HX2], [HX2], s2=1.0, op1=ALU.add)
                tt(C, "dve", hx2[:, 0:127], hx2[:, 0:127], hx[:, 0:127], ALU.mult, [HX2, HX], [HX2])
                actf(C, hth[:, 0:127], hx2[:, 0:127], AF.Tanh, [HX2], [HTH], scale=0.7978845608028654)
                stt(C, "dve", hth[:, 0:127], hth[:, 0:127], 1.0, hx[:, 0:127], ALU.add, ALU.mult, [HTH, HX], [HTH])
                ts(C, "dve", h1[:, j, 0:127], hth[:, 0:127], 0.5, ALU.mult, [HTH], [H1])
            po, PO = getps(C)
            if kv == "k":
                for j in range(2):
                    mm(C, po[0:64, 0:128], w2k[:, j, :], h1[:, j, :], j == 0, j == 1, [W2K, H1], [PO])
                cpy(C, "dve", KcTc[:, g, 0:127], po[0:64, 0:127], [PO], [KCTC])
            else:
                for j in range(2):
                    mm(C, po[:, 0:64], h1[:, j, :], w2v[:, j, :], j == 0, j == 1, [W2V, H1], [PO])
                cpy(C, "dve", Vca[0:127, g, 0:64], po[0:127, 0:64], [PO], [VCA])
    for g in range(2):
        P.op("dve", lambda e, g=g: e.memset(Vca[0:127, g, 64:65], 1.0), reads=[], writes=[VCA])
        cpy(C, "dve", Vca[:, g, 65:97], K.overlap[:, :], [CONST], [VCA])
    if stop == "cmp":
        tap(C, "x_KcTc", KcTc[:], [64, 2, 128], [KCTC])
        tap(C, "x_Vca", Vca[:], [128, 2, 97], [VCA])
        return

    P.barrier()
    phs1.close()
    NE = 4
    et = [sb(f"e{i}", [128, 512], BF16) for i in range(NE)]
    ET = [Buf(f"e{i}") for i in range(NE)]
    pt = [sb(f"p{i}", [128, 512], BF16) for i in range(NE)]
    PT_ = [Buf(f"p{i}") for i in range(NE)]
    selm4 = [sb(f"selm{i}", [128, 16, 128], BF16) for i in range(4)]
    SELM4 = [Buf(f"selm{i}") for i in range(4)]
    oB = [sb(f"oB{i}", [128, 512], F32) for i in range(2)]
    OB = [Buf(f"oB{i}") for i in range(2)]
    oBb = [sb(f"oBb{i}", [128, 512], BF16) for i in range(2)]
    OBB = [Buf(f"oBb{i}") for i in range(2)]
    rden = [sb(f"rden{i}", [128, 4], F32) for i in range(2)]
    RDEN = [Buf(f"rden{i}") for i in range(2)]
    fac = [sb(f"fac{i}", [128, 4], F32) for i in range(2)]
    FAC = [Buf(f"fac{i}") for i in range(2)]
    obr = [sb(f"obr{i}", [128, 4, 64], F32) for i in range(2)]
    OBR = [Buf(f"obr{i}") for i in range(2)]
    impt = sb("impt", [128, 4, 32], F32)
    imp = sb("imp", [128, 32], F32)
    imp2 = sb("imp2", [128, 32], F32)
    mx8 = sb("mx8", [128, 8], F32)
    thr = sb("thr", [128, 1], F32)
    bmf = sb("bmf", [128, 32], BF16)
    bmT = sb("bmT", [32, 128], BF16)
    IMPT, IMP, IMP2, MX8, THR, BMF, BMT = (Buf(n) for n in ("impt", "imp", "imp2", "mx8", "thr", "bmf", "bmT"))
    cnt = {"e": 0, "ev": 0, "m": 0}

    def gate_view(t, g, br):
        v = gts[:, t, g * 12:(g + 1) * 12].rearrange("p (b br) -> p b br", b=4)
        return v[:, :, br]

    def qk_exp(kT_, KB, g, kt, qt):
        ps_, PS_ = getps(C)
        ksl = slice(kt * 128, (kt + 1) * 128)
        qsl = slice(qt * 128, (qt + 1) * 128)
        mm(C, ps_[:, :], kT_(ksl), QT[:, 4 * g:4 * g + 4, qsl], True, True, KB + QTB[4 * g:4 * g + 4], [PS_])
        i = cnt["e"] % NE
        cnt["e"] += 1
        actf(C, et[i][:, :], ps_[:, :], AF.Exp, [PS_], [ET[i]])
        return et[i], ET[i]

    def masked(e_, E_, mask_ap, MB):
        i = cnt["m"] % NE
        cnt["m"] += 1
        eng = "dve"
        tt(C, eng, pt[i][:].rearrange("p (b q) -> p b q", b=4), e_[:].rearrange("p (b q) -> p b q", b=4),
           bc_mid(mask_ap, 4), ALU.mult, [E_] + MB, [PT_[i]])
        return pt[i], PT_[i]

    def evac(po, PO, width, t, g, br, first):
        s = t % 2
        i = cnt["ev"] % 2
        cnt["ev"] += 1
        pov = po[:, 0:4 * width].rearrange("p (b w) -> p b w", b=4)
        ts(C, "dve", rden[i][:, :], pov[:, :, 64], 1e-30, ALU.add, [PO], [RDEN[i]])
        P.op("dve", lambda e: e.reciprocal(rden[i][:, :], rden[i][:, :]), reads=[RDEN[i]], writes=[RDEN[i]])
        tt(C, "dve", fac[i][:, :], rden[i][:, :], gate_view(t, g, br), ALU.mult, [RDEN[i], GTS], [FAC[i]])
        ov = oB[s][:, g * 256:(g + 1) * 256].rearrange("p (b d) -> p b d", b=4)
        if first:
            tt(C, "dve", ov, pov[:, :, 0:64], bc_last(fac[i][:, :], 64), ALU.mult, [PO, FAC[i]], [OB[s]])
        else:
            tt(C, "dve", obr[i][:], pov[:, :, 0:64], bc_last(fac[i][:, :], 64), ALU.mult, [PO, FAC[i]], [OBR[i]])
            tt(C, "dve", ov, ov, obr[i][:], ALU.add, [OB[s], OBR[i]], [OB[s]])
        return i

    nqt = NT if stop == "" else int(os.environ.get("NSA_NQT", "16"))
    DEPTH = int(os.environ.get("NSA_DEPTH", "4"))
    blocks = []

    def add_cmp(qt, g):
        st_ = {}
        qsl = slice(qt * 128, (qt + 1) * 128)
        selm = selm4[(qt % 2) * 2:(qt % 2) * 2 + 2]
        SELM = SELM4[(qt % 2) * 2:(qt % 2) * 2 + 2]

        def front():
            e_, E_ = qk_exp(lambda ksl: KcTc[:, g, :], [KCTC], g, 0, qt)
            st_["p"] = masked(e_, E_, K.cmpmask[:, qsl], [CONST])

        def back():
            p_, P_ = st_["p"]
            po, PO = getps(C)
            for b in range(4):
                mm(C, po[:, b * 97:(b + 1) * 97], p_[:, b * 128:(b + 1) * 128], Vca[:, g, :], True, True, [P_, VCA], [PO])
            ri = evac(po, PO, 97, qt, g, 0, True)
            if qt < 8:
                return
            pov = po[:, 0:388].rearrange("p (b w) -> p b w", b=4)
            tt(C, "dve", impt[:], pov[:, :, 65:97], bc_last(rden[ri][:, :], 32), ALU.mult, [PO, RDEN[ri]], [IMPT])
            P.op("dve", lambda e: e.tensor_reduce(imp[:, :], impt[:].rearrange("p b j -> p j b"), AX.X, ALU.add),
                 reads=[IMPT], writes=[IMP])
            tt(C, "dve", imp[:, :], imp[:, :], K.forced[:, qt - 8, :], ALU.add, [IMP, CONST], [IMP])
            P.op("dve", lambda e: e.max(mx8[:, :], imp[:, :]), reads=[IMP], writes=[MX8])
            P.op("dve", lambda e: e.match_replace(imp2[:, :], mx8[:, :], imp[:, :], -3.0e38), reads=[IMP, MX8], writes=[IMP2])
            P.op("dve", lambda e: e.max(mx8[:, :], imp2[:, :]), reads=[IMP2], writes=[MX8])
            P.op("dve", lambda e: e.tensor_reduce(thr[:, :], mx8[:, :], AX.X, ALU.min), reads=[MX8], writes=[THR])
            ts(C, "dve", bmf[:, :], imp[:, :], thr[:, 0:1], ALU.is_ge, [IMP, THR], [BMF])
            pbt, PBT = getps(C)
            pbtv = pbt.bitcast(BF16)
            P.op("pe", lambda e, o=pbtv[0:32, 0:128]: e.transpose(o, bmf[:, :], K.ident_b[:, :]), reads=[BMF, CONST], writes=[PBT])
            cpy(C, "dve", bmT[:, :], pbtv[0:32, 0:128], [PBT], [BMT])
            for k4 in range(0, qt + 1, 4):
                pe_, PE_ = getps(C)
                nk = min(4, qt + 1 - k4)
                for j in range(nk):
                    mm(C, pe_[:, j * 128:(j + 1) * 128], K.expand[0:32, k4 + j, :], bmT[0:32, :], True, True, [CONST, BMT], [PE_])
                if k4 + nk - 1 == qt:
                    if nk > 1:
                        cpy(C, "act", selm[g][:, k4:k4 + nk - 1, :], pe_[:, 0:(nk - 1) * 128].rearrange("p (k q) -> p k q", q=128),
                            [PE_], [SELM[g]])
                    tt(C, "dve", selm[g][:, qt, :], pe_[:, (nk - 1) * 128:nk * 128], K.causal[:, :], ALU.mult, [PE_, CONST], [SELM[g]])
                else:
                    cpy(C, "act", selm[g][:, k4:k4 + nk, :], pe_[:, 0:nk * 128].rearrange("p (k q) -> p k q", q=128), [PE_], [SELM[g]])

        blocks.append((front, back))

    def add_branch(qt, g, br):
        acc_ = {}
        selm = selm4[(qt % 2) * 2:(qt % 2) * 2 + 2]
        SELM = SELM4[(qt % 2) * 2:(qt % 2) * 2 + 2]
        if br == 1:
            kts = list(range(qt + 1))
        else:
            kts = list(range(max(0, qt - 4), qt + 1))
        for kt in kts:
            st_ = {}

            def front(kt=kt, st_=st_):
                if br == 1:
                    e_, E_ = qk_exp(lambda ksl: KsT[:, g, ksl], [KST[g]], g, kt, qt)
                    if qt >= 8:
                        st_["p"] = masked(e_, E_, selm[g][:, kt, :], [SELM[g]])
                    elif kt == qt:
                        st_["p"] = masked(e_, E_, K.causal[:, :], [CONST])
                    else:
                        st_["p"] = (e_, E_)
                else:
                    e_, E_ = qk_exp(lambda ksl: KwT[:, g, ksl], [KWT[g]], g, kt, qt)
                    if kt == qt:
                        st_["p"] = masked(e_, E_, K.causal[:, :], [CONST])
                    elif kt == qt - 4:
                        st_["p"] = masked(e_, E_, K.anti[:, :], [CONST])
                    else:
                        st_["p"] = (e_, E_)

            def back(kt=kt, st_=st_):
                p_, P_ = st_["p"]
                if kt == kts[0]:
                    acc_["po"] = reserve_ps(C)
                poi, po, PO = acc_["po"]
                Vt, VB_ = (Vs, VS) if br == 1 else (Vw, VW)
                for b in range(4):
                    mm2(C, po[:, b * 65:(b + 1) * 65], p_[:, b * 128:(b + 1) * 128], Vt[:, kt, g, :],
                        (kt == kts[0] and b == 0), kt == kts[-1], [P_, VB_], [PO])
                if kt == kts[-1]:
                    evac(po, PO, 65, qt, g, br, False)
                    C.ps_reserved.discard(poi)

            blocks.append((front, back))

    def add_finish(qt):
        s = qt % 2
        qsl = slice(qt * 128, (qt + 1) * 128)

        def front():
            pass

        def back():
            cpy(C, "act", oBb[s][:, :], oB[s][:, :], [OB[s]], [OBB[s]])
            if qt in (0, 1, 3, 7, 9):
                tap(C, f"x_oB{qt}", oB[s][:, :], [128, 512], [OB[s]])
            ptr, PTR = getps(C)
            ptrv = ptr.bitcast(BF16)
            for c in range(4):
                P.op("pe", lambda e, o=ptrv[:, c * 128:(c + 1) * 128], i=oBb[s][:, c * 128:(c + 1) * 128]: e.transpose(o, i, K.ident_b[:, :]),
                     reads=[OBB[s], CONST], writes=[PTR])
            cpy(C, "act", C.oBT[:, :, qsl], ptrv[:, 0:512].rearrange("p (c q) -> p c q", c=4), [PTR], [C.OBT[qt]])

        blocks.append((front, back))

    add_cmp(0, 0)
    add_cmp(0, 1)
    for qt in range(nqt):
        if qt + 1 < nqt:
            add_cmp(qt + 1, 0)
            add_cmp(qt + 1, 1)
        for g in range(2):
            add_branch(qt, g, 1)
            add_branch(qt, g, 2)
        add_finish(qt)
    nb = len(blocks)
    for i in range(nb + DEPTH):
        if i - DEPTH >= 0:
            blocks[i - DEPTH][1]()
        if i < nb:
            blocks[i][0]()
    tap(C, "oBT", C.oBT[:, 0, :], [128, S], C.OBT)


def layer_norm_tile(C, v, V, stats, STATS, mv, MV, gt, bt, out, OUT, mul_eng="dve"):
    P = C.P
    for n in range(2):
        P.op("dve", lambda e, n=n: e.bn_stats(stats[:, n, :], v[:, n * 512:(n + 1) * 512]), reads=[V], writes=[STATS])
    P.op("dve", lambda e: e.bn_aggr(mv[:, 0:2], stats[:].rearrange("p n s -> p (n s)")), reads=[STATS], writes=[MV])
    actf(C, mv[:, 2:3], mv[:, 1:2], AF.Ln, [MV], [MV], bias=LN_EPS)
    actf(C, mv[:, 2:3], mv[:, 2:3], AF.Exp, [MV], [MV], scale=-0.5)
    stt(C, "dve", mv[:, 3:4], mv[:, 0:1], -1.0, mv[:, 2:3], ALU.mult, ALU.mult, [MV], [MV])
    actf(C, v[:, :], v[:, :], AF.Identity, [V, MV], [V], bias=mv[:, 3:4], scale=mv[:, 2:3])
    tt(C, mul_eng, v[:, :], v[:, :], gt[:, :], ALU.mult, [V, C.CONST], [V])
    tt(C, mul_eng, out[:, :], v[:, :], bt[:, :], ALU.add, [V, C.CONST], [OUT])


def phase_mix(C, ph):
    import os
    nc, P, dr, K = C.nc, C.P, C.dr, C.K
    CONST = C.CONST

    def sb(name, shape, dtype):
        return ph.enter_context(nc.sbuf_tensor("m_" + name, list(shape), dtype))

    ln1g = cload(C, sb, "ln1g", "p_ln1g", [128, D])
    ln1b = cload(C, sb, "ln1b", "p_ln1b", [128, D])
    wA = sb("wA", [128, 4, D], BF16)
    wB = sb("wB", [128, 4, D], BF16)
    wo = sb("wo", [128, 8, D], BF16)
    WA, WB, WO = Buf("wA"), Buf("wB"), Buf("wo")
    load_w(C, wA[:], "w_branch_gdn", 0, D, [WA], nrow_chunks=4)
    load_w(C, wB[:], "w_branch_nsa", 0, D, [WB], nrow_chunks=4)
    wg, WG, tlg = C.wst, C.WST, C.wst_tl
    mixT = [sb(f"mixT{i}", [128, 8, 512], BF16) for i in range(2)]
    MIXT = [Buf(f"mixT{i}") for i in range(2)]
    NB = 2
    th = [sb(f"th{i}", [128, 2, 512], F32) for i in range(NB)]
    TH = [Buf(f"th{i}") for i in range(NB)]
    m1 = [sb(f"m1{i}", [128, 512], F32) for i in range(NB)]
    M1 = [Buf(f"m1{i}") for i in range(NB)]
    m2 = [sb(f"m2{i}", [128, 512], F32) for i in range(NB)]
    M2 = [Buf(f"m2{i}") for i in range(NB)]
    xt = [sb(f"xt{i}", [128, D], F32) for i in range(2)]
    XTl = [Buf(f"xt{i}") for i in range(2)]
    tlx = [P.new_dma_tl(f"mxt{i}") for i in range(2)]
    vt = [sb(f"vt{i}", [128, D], F32) for i in range(2)]
    VT = [Buf(f"vt{i}") for i in range(2)]
    ht, HT = vt, VT
    hb = [sb(f"hb{i}", [128, D], BF16) for i in range(2)]
    HB = [Buf(f"hb{i}") for i in range(2)]
    stats = [sb(f"stats{i}", [128, 2, 6], F32) for i in range(2)]
    STATS = [Buf(f"stats{i}") for i in range(2)]
    mv = [sb(f"mv{i}", [128, 4], F32) for i in range(2)]
    MV = [Buf(f"mv{i}") for i in range(2)]
    C.HSCR = [Buf(f"hscr{t}") for t in range(NT)]
    kc = {"k": 0}

    def gates(tb):
        ms = tb % 2
        tsl = slice(tb * 512, (tb + 1) * 512)
        XTB = C.XT[tb * 4:(tb + 1) * 4]
        for j in range(8):
            k = kc["k"]
            ws = C.wk % 3
            C.wk += 1
            bs = k % NB
            kc["k"] += 1
            P.dma("pool", wg[ws][:], dr["pk_mg"][j], writes=[WG[ws]], tl=tlg[ws])
            pga, PGA = getps(C)
            pgb, PGB = getps(C)
            for c in range(8):
                mm(C, pga[:, :], wg[ws][:, c, 0:128], C.xT[:, c, tsl], c == 0, c == 7, [WG[ws]] + XTB, [PGA])
            for c in range(8):
                mm(C, pgb[:, :], wg[ws][:, c, 128:256], C.xT[:, c, tsl], c == 0, c == 7, [WG[ws]] + XTB, [PGB])
            actf(C, th[bs][:, 0, :], pga[:, :], AF.Tanh, [PGA], [TH[bs]], scale=0.5)
            actf(C, th[bs][:, 1, :], pgb[:, :], AF.Tanh, [PGB], [TH[bs]], scale=0.5)
            pa, PA = getps(C)
            pbB, PBB_ = getps(C)
            for c in range(4):
                mm(C, pa[:, :], wA[:, c, j * 128:(j + 1) * 128], C.oAT[:, c, tsl], c == 0, c == 3,
                   [WA] + C.OAT[tb * 4:(tb + 1) * 4], [PA])
            for c in range(4):
                mm(C, pbB[:, :], wB[:, c, j * 128:(j + 1) * 128], C.oBT[:, c, tsl], c == 0, c == 3,
                   [WB] + C.OBT[tb * 4:(tb + 1) * 4], [PBB_])
            stt(C, "dve", m1[bs][:, :], th[bs][:, 0, :], 1.0, pa[:, :], ALU.add, ALU.mult, [TH[bs], PA], [M1[bs]])
            stt(C, "dve", m2[bs][:, :], th[bs][:, 1, :], 1.0, pbB[:, :], ALU.add, ALU.mult, [TH[bs], PBB_], [M2[bs]])
            tt(C, "dve", mixT[ms][:, j, :], m1[bs][:, :], m2[bs][:, :], ALU.add, [M1[bs], M2[bs]], [MIXT[ms]])
            yield

    def epi(tb):
        ms = tb % 2
        for t4 in range(4):
            t = tb * 4 + t4
            s2 = t % 2
            P.dma("sp", xt[s2][:, :], dr["x"][t * 128:(t + 1) * 128, :], writes=[XTl[s2]], tl=tlx[s2])
            for n in range(2):
                py, PY = getps(C)
                for j in range(8):
                    mm(C, py[:, :], mixT[ms][:, j, t4 * 128:(t4 + 1) * 128], wo[:, j, n * 512:(n + 1) * 512], j == 0, j == 7,
                       [MIXT[ms], WO], [PY])
                stt(C, "dve", vt[s2][:, n * 512:(n + 1) * 512], xt[s2][:, n * 512:(n + 1) * 512], DN_ALPHA, py[:, :],
                    ALU.mult, ALU.add, [XTl[s2], PY], [VT[s2]])
            layer_norm_tile(C, vt[s2], VT[s2], stats[s2], STATS[s2], mv[s2], MV[s2], ln1g, ln1b, vt[s2], VT[s2])
            P.dma("sp", C.hscr[t * 128:(t + 1) * 128, :], ht[s2][:, :], reads=[HT[s2]], writes=[C.HSCR[t]])
            cpy(C, "act", hb[s2][:, :], ht[s2][:, :], [HT[s2]], [HB[s2]])
            ptr, PTR = getps(C)
            ptrv = ptr.bitcast(BF16)
            for c in range(8):
                P.op("pe", lambda e, o=ptrv[:, c * 128:(c + 1) * 128], i=hb[s2][:, c * 128:(c + 1) * 128]: e.transpose(o, i, K.ident_b[:, :]),
                     reads=[HB[s2], CONST], writes=[PTR])
            cpy(C, "act", C.xT[:, :, t * 128:(t + 1) * 128], ptrv[:, :].rearrange("p (c q) -> p c q", c=8), [PTR], [C.XT[t]])
            yield

    for j_, _ in enumerate(gates(0)):
        if j_ == 1:
            load_w(C, wo[:], "w_out", 0, D, [WO])
            actf(C, wo[:].rearrange("p c n -> p (c n)"), wo[:].rearrange("p c n -> p (c n)"), AF.Copy, [WO], [WO], scale=0.5)
    for tb in range(4):
        A = gates(tb + 1) if tb + 1 < 4 else iter(())
        B = epi(tb)
        a_done = b_done = False
        while not (a_done and b_done):
            for _ in range(2):
                if not a_done:
                    try:
                        next(A)
                    except StopIteration:
                        a_done = True
            if not b_done:
                try:
                    next(B)
                except StopIteration:
                    b_done = True


def phase_ffn(C, ph):
    import os
    nc, P, dr, K = C.nc, C.P, C.dr, C.K
    CONST = C.CONST
    hT, HTB = C.xT, C.XT

    def sb(name, shape, dtype):
        return ph.enter_context(nc.sbuf_tensor("f_" + name, list(shape), dtype))

    ln2g = cload(C, sb, "ln2g", "p_ln2g", [128, D])
    ln2b = cload(C, sb, "ln2b", "p_ln2b", [128, D])
    fconv = cload(C, sb, "fconv", "p_fconv", [128, 44, 3])
    wd = sb("wd", [128, 22, D], BF16)
    WD = Buf("wd")
    QW = 512
    aT = [sb(f"aT{i}", [128, 22, QW], BF16) for i in range(2)]
    AT = [[Buf(f"aT{q}_{i}") for i in range(22)] for q in range(2)]
    wu, WU, tlu = C.wst, C.WST, C.wst_tl
    raw = [[sb(f"raw{w}{i}", [128, QW + 2], F32) for i in range(2)] for w in range(2)]
    RAW = [[Buf(f"raw{w}{i}") for i in range(2)] for w in range(2)]
    acc = [[sb(f"acc{w}{i}", [128, QW], F32) for i in range(2)] for w in range(2)]
    ACC = [[Buf(f"acc{w}{i}") for i in range(2)] for w in range(2)]
    halo = sb("halo", [128, 44, 2], F32)
    HALO = [Buf(f"halo{c}") for c in range(44)]
    hres = [sb(f"hres{i}", [128, D], F32) for i in range(2)]
    HRES = [Buf(f"hres{i}") for i in range(2)]
    tlh = [P.new_dma_tl(f"fhr{i}") for i in range(2)]
    vt = [sb(f"vt{i}", [128, D], F32) for i in range(2)]
    VT = [Buf(f"vt{i}") for i in range(2)]
    stats = [sb(f"stats{i}", [128, 2, 6], F32) for i in range(2)]
    STATS = [Buf(f"stats{i}") for i in range(2)]
    mv = [sb(f"mv{i}", [128, 4], F32) for i in range(2)]
    MV = [Buf(f"mv{i}") for i in range(2)]
    OUTB = [Buf(f"out{t}") for t in range(NT)]
    kc = {"k": 0}

    def d1(q):
        T0 = q * QW
        qs = q % 2
        for i in range(22):
            k = kc["k"]
            ws = C.wk % 3
            C.wk += 1
            rs = k % 2
            kc["k"] += 1
            P.dma("pool", wu[ws][:], dr["pk_up"][i], writes=[WU[ws]], tl=tlu[ws])
            for w in range(2):
                ck = i + 22 * w
                if q == 0:
                    P.op("dve", lambda e, o=raw[w][rs][:, 0:2]: e.memset(o, 0.0), writes=[RAW[w][rs]])
                else:
                    cpy(C, "dve", raw[w][rs][:, 0:2], halo[:, ck, :], [HALO[ck]], [RAW[w][rs]])
                pu, PU = getps(C)
                for c in range(8):
                    mm(C, pu[:, :], wu[ws][:, c, w * 128:(w + 1) * 128], hT[:, c, T0:T0 + QW],
                       c == 0, c == 7, [WU[ws]] + HTB[T0 // 128:T0 // 128 + 4], [PU])
                cpy(C, "act", raw[w][rs][:, 2:2 + QW], pu[:, :], [PU], [RAW[w][rs]])
                if q < 3:
                    cpy(C, "dve", halo[:, ck, :], raw[w][rs][:, QW:QW + 2], [RAW[w][rs]], [HALO[ck]])
                if FFN_TAP_ACT:
                    actf(C, acc[w][rs][:, :], raw[w][rs][:, 2:QW + 2], AF.Copy, [RAW[w][rs], CONST], [ACC[w][rs]], scale=fconv[:, ck, 2:3])
                else:
                    ts(C, "dve", acc[w][rs][:, :], raw[w][rs][:, 2:QW + 2], fconv[:, ck, 2:3], ALU.mult, [RAW[w][rs], CONST], [ACC[w][rs]])
                for j in (1, 0):
                    stt(C, "dve", acc[w][rs][:, :], raw[w][rs][:, j:QW + j], fconv[:, ck, j:j + 1], acc[w][rs][:, :],
                        ALU.mult, ALU.add, [RAW[w][rs], ACC[w][rs], CONST], [ACC[w][rs]])
            actf(C, acc[0][rs][:, :], acc[0][rs][:, :], AF.Silu, [ACC[0][rs]], [ACC[0][rs]])
            tt(C, "dve", aT[qs][:, i, :], acc[0][rs][:, :], acc[1][rs][:, :], ALU.mult, [ACC[0][rs], ACC[1][rs]], [AT[qs][i]])
            if q == 0 and 3 <= i < 14:
                i2 = (i - 3) * 2
                src = dr["w_down"][i2 * 128:(i2 + 2) * 128, :].rearrange("(c p) n -> p c n", p=128)
                P.dma("pool", wd[:, i2:i2 + 2, :], src, writes=[WD])
            yield

    def d2(q):
        qs = q % 2
        for t4 in range(4):
            t = q * 4 + t4
            s2 = t % 2
            P.dma("sp", hres[s2][:, :], C.hscr[t * 128:(t + 1) * 128, :], reads=[C.HSCR[t]], writes=[HRES[s2]], tl=tlh[s2])
            for n in range(2):
                pf, PF = getps(C)
                for i in range(22):
                    mm(C, pf[:, :], aT[qs][:, i, t4 * 128:(t4 + 1) * 128], wd[:, i, n * 512:(n + 1) * 512], i == 0, i == 21,
                       [AT[qs][i], WD], [PF])
                stt(C, "dve", vt[s2][:, n * 512:(n + 1) * 512], hres[s2][:, n * 512:(n + 1) * 512], DN_ALPHA, pf[:, :],
                    ALU.mult, ALU.add, [HRES[s2], PF], [VT[s2]])
            layer_norm_tile(C, vt[s2], VT[s2], stats[s2], STATS[s2], mv[s2], MV[s2], ln2g, ln2b, vt[s2], VT[s2])
            P.dma("sp", C.out_d[t * 128:(t + 1) * 128, :], vt[s2][:, :], reads=[VT[s2]], writes=[OUTB[t]])
            C.final_bufs.append(OUTB[t])
            yield

    for _ in d1(0):
        pass
    for q in range(4):
        A = d1(q + 1) if q + 1 < 4 else iter(())
        B = d2(q)
        a_done = b_done = False
        while not (a_done and b_done):
            for _ in range(FFN_RATIO):
                if not a_done:
                    try:
                        next(A)
                    except StopIteration:
                        a_done = True
            if not b_done:
                try:
                    next(B)
                except StopIteration:
                    b_done = True


_CACHE = {}


def kernel(**inputs):
    inp = {k: np.asarray(v) for k, v in inputs.items()}
    if "nc" not in _CACHE:
        _CACHE["nc"] = build()[0]
    nc = _CACHE["nc"]
    base = {k: np.ascontiguousarray(inp[k][0], dtype=np.float32) for k in WEIGHT_SHAPES}
    base.update(host_consts())
    base.update(host_params(inp))
    base.update(host_packed(inp))
    n = inp["x"].shape[0]
    in_maps = [dict(base, x=np.ascontiguousarray(inp["x"][b], dtype=np.float32)) for b in range(n)]
    res = run_bass_kernel_spmd(nc, in_maps, core_ids=list(range(n)))
    return np.stack([np.asarray(r["out"], dtype=np.float32) for r in res.results], 0)
```
